# Optimizing a Trainium2 kernel written in Bass

```python
import math
import jax, jax.numpy as jnp
from jax import lax
import numpy as np

D_MODEL = 1024
BATCH = 2
SEQ = 8192
DEPTH = 1
DEC_BATCH = 128
DEC_SEQ = 1
PAST_LEN = 2048
PAGE_SIZE = 128

HEAD_DIM_A = 64
N_HEADS_A = 6
D_A = N_HEADS_A * HEAD_DIM_A
D_B = D_MODEL - D_A
N_HEADS_B = 4
HEAD_DIM_B = D_B // N_HEADS_B
DILATED_BRANCHES = ((128, 1), (512, 4), (2048, 16))
WINDOW_MAX = 2048
N_BUCKETS = 32
REL_MAX_DIST = 2048
CONV_W = 4
MLSTM_CHUNK = 128
D_FF = 4 * D_MODEL
N_GATES = 2 * N_HEADS_B
D_IN = 3 * D_A + 2 * D_B + N_GATES
SPLITS = [D_A, 2 * D_A, 3 * D_A, 3 * D_A + D_B, 3 * D_A + 2 * D_B]
EPS = 1e-6
NEG = -1e30

kernel_name = "hybrid_dilated_attn_mlstm_decode_step"


def rmsnorm(x, g):
    xf = x.astype(jnp.float32)
    y = xf * lax.rsqrt(jnp.mean(xf * xf, axis=-1, keepdims=True) + EPS)
    return (y * g.astype(jnp.float32)).astype(x.dtype)


def t5_bucket(dist):
    max_exact = N_BUCKETS // 2
    df = jnp.maximum(dist, 1).astype(jnp.float32)
    large = max_exact + (jnp.log(df / max_exact) / math.log(REL_MAX_DIST / max_exact)
                         * (N_BUCKETS - max_exact)).astype(jnp.int32)
    large = jnp.minimum(large, N_BUCKETS - 1)
    return jnp.where(dist < max_exact, dist, large)


def dilated_prompt(q, k, v, rel_bias, window, dil):
    B, S, H, E = q.shape
    nk = window // dil
    blk = nk
    unit = blk * dil
    s_pad = -(-S // unit) * unit
    nb = s_pad // unit
    padw = ((0, 0), (0, s_pad - S), (0, 0), (0, 0))

    def to_blocks(t):
        return jnp.pad(t, padw).reshape(B, nb, blk, dil, H, E)

    def with_prev(t):
        prev = jnp.pad(t, ((0, 0), (1, 0), (0, 0), (0, 0), (0, 0), (0, 0)))[:, :-1]
        return jnp.concatenate([prev, t], axis=2)

    qb = to_blocks(q)
    kc = with_prev(to_blocks(k))
    vc = with_prev(to_blocks(v))
    qi = jnp.arange(blk)[:, None]
    ki = jnp.arange(2 * blk)[None, :]
    j = qi + blk - ki
    band = (j >= 0) & (j <= nk)
    blk_idx = jnp.arange(nb)[:, None, None]
    mask = band[None] & (blk_idx * blk + ki[None] - blk >= 0)
    bias = rel_bias[t5_bucket(jnp.clip(j, 0, None) * dil)].astype(jnp.float32)
    bias = bias.transpose(2, 0, 1)
    scale = 1.0 / math.sqrt(E)
    s = jnp.einsum('bnqrhe,bnkrhe->bnrhqk', qb, kc).astype(jnp.float32) * scale + bias
    s = jnp.where(mask[None, :, None, None], s, NEG)
    lse = jax.nn.logsumexp(s, axis=-1)
    p = jnp.exp(s - lse[..., None]).astype(v.dtype)
    o = jnp.einsum('bnrhqk,bnkrhe->bnqrhe', p, vc).reshape(B, s_pad, H, E)[:, :S]
    lse = lse.transpose(0, 1, 4, 2, 3).reshape(B, s_pad, H)[:, :S]
    return o, lse


def dilated_step(q, k_all, v_all, rel_bias, window, dil):
    B, T, H, E = q.shape
    P = k_all.shape[1] - T
    nk = window // dil
    j = jnp.arange(nk + 1)
    idx = P + jnp.arange(T)[:, None] - j[None, :] * dil
    valid = idx >= 0
    idx = jnp.maximum(idx, 0)
    kg = k_all[:, idx]
    vg = v_all[:, idx]
    bias = rel_bias[t5_bucket(j * dil)].astype(jnp.float32).T
    scale = 1.0 / math.sqrt(E)
    s = jnp.einsum('bthe,btjhe->bthj', q, kg).astype(jnp.float32) * scale + bias
    s = jnp.where(valid[None, :, None, :], s, NEG)
    lse = jax.nn.logsumexp(s, axis=-1)
    p = jnp.exp(s - lse[..., None]).astype(v_all.dtype)
    o = jnp.einsum('bthj,btjhe->bthe', p, vg)
    return o, lse


def combine_branches(branches):
    o = jnp.stack([b[0] for b in branches])
    lse = jnp.stack([b[1] for b in branches])
    w = jax.nn.softmax(lse, axis=0).astype(o.dtype)
    return jnp.einsum('gbsh,gbshe->bshe', w, o)


def causal_conv(x, buf, w, b):
    S = x.shape[1]
    xp = jnp.concatenate([buf.astype(x.dtype), x], axis=1)
    y = b + sum(xp[:, i:i + S] * w[i] for i in range(CONV_W))
    return y, xp[:, -(CONV_W - 1):]


def mlstm_cell(q, k, v, i_pre, f_pre, C0, n0, m0):
    B, S, H, E = q.shape
    L = min(MLSTM_CHUNK, S)
    s_pad = -(-S // L) * L
    nc = s_pad // L
    pad = s_pad - S
    f32 = jnp.float32

    def chunks4(t):
        t = jnp.pad(t.astype(f32), ((0, 0), (0, pad), (0, 0), (0, 0)))
        return t.reshape(B, nc, L, H, E).transpose(1, 0, 3, 2, 4)

    def chunks3(t, fill):
        t = jnp.pad(t.astype(f32), ((0, 0), (0, pad), (0, 0)), constant_values=fill)
        return t.reshape(B, nc, L, H).transpose(1, 0, 3, 2)

    logf = jax.nn.log_sigmoid(f_pre.astype(f32))
    xs = (chunks4(q), chunks4(k), chunks4(v), chunks3(i_pre, NEG), chunks3(logf, 0.0))
    causal = jnp.tril(jnp.ones((L, L), dtype=bool))

    def step(carry, inp):
        C, n, m = carry
        qc, kc, vc, ic, fc = inp
        a = jnp.cumsum(fc, axis=-1)
        g = a + m[..., None]
        D = jnp.where(causal, a[..., :, None] - a[..., None, :] + ic[..., None, :], NEG)
        m_t = jnp.maximum(g, jnp.max(D, axis=-1))
        w_state = jnp.exp(g - m_t)
        A = jnp.einsum('bhte,bhse->bhts', qc, kc) * jnp.exp(D - m_t[..., None])
        num = (w_state[..., None] * jnp.einsum('bhvk,bhtk->bhtv', C, qc)
               + jnp.einsum('bhts,bhsv->bhtv', A, vc))
        den = w_state * jnp.einsum('bhk,bhtk->bht', n, qc) + jnp.sum(A, axis=-1)
        h = num / jnp.maximum(jnp.abs(den), jnp.exp(-m_t))[..., None]
        b_tot = a[..., -1]
        wl = b_tot[..., None] - a + ic
        m_new = jnp.maximum(b_tot + m, jnp.max(wl, axis=-1))
        wk = jnp.exp(wl - m_new[..., None])
        decay = jnp.exp(b_tot + m - m_new)
        C_new = decay[..., None, None] * C + jnp.einsum('bhs,bhsv,bhsk->bhvk', wk, vc, kc)
        n_new = decay[..., None] * n + jnp.einsum('bhs,bhsk->bhk', wk, kc)
        return (C_new, n_new, m_new), h

    (C, n, m), hs = lax.scan(step, (C0.astype(f32), n0.astype(f32), m0.astype(f32)), xs)
    h = hs.transpose(1, 0, 3, 2, 4).reshape(B, s_pad, H, E)[:, :S]
    return h, C, n, m


def hybrid_layer(x, win_k, win_v, conv_buf, C0, n0, m0, rel_bias,
                 norm1_g, w_in, gate_bias, conv_w, conv_b, wq_head, wk_head,
                 attn_out_g, mh_norm_g, skip, w_out, norm2_g, w_ff1, w_ff2):
    B, S, _ = x.shape
    h = rmsnorm(x, norm1_g)
    z = h @ w_in
    qa, ka, va, xb, ob, gates = jnp.split(z, SPLITS, axis=-1)
    qa = qa.reshape(B, S, N_HEADS_A, HEAD_DIM_A)
    ka = ka.reshape(B, S, N_HEADS_A, HEAD_DIM_A)
    va = va.reshape(B, S, N_HEADS_A, HEAD_DIM_A)

    if win_k is None:
        branches = [dilated_prompt(qa, ka, va, rel_bias, w, d) for (w, d) in DILATED_BRANCHES]
        P = min(WINDOW_MAX, S)
        new_k = ka[:, S - P:]
        new_v = va[:, S - P:]
    else:
        P = win_k.shape[1]
        k_all = jnp.concatenate([win_k.astype(ka.dtype), ka], axis=1)
        v_all = jnp.concatenate([win_v.astype(va.dtype), va], axis=1)
        branches = [dilated_step(qa, k_all, v_all, rel_bias, w, d) for (w, d) in DILATED_BRANCHES]
        new_k = k_all[:, -P:]
        new_v = v_all[:, -P:]
    out_a = rmsnorm(combine_branches(branches).reshape(B, S, D_A), attn_out_g)

    c, new_conv = causal_conv(xb, conv_buf, conv_w, conv_b)
    c_act = jax.nn.silu(c)
    c_h = c_act.reshape(B, S, N_HEADS_B, HEAD_DIM_B)
    qb = jnp.einsum('bshd,hde->bshe', c_h, wq_head)
    kb = jnp.einsum('bshd,hde->bshe', c_h, wk_head) * (1.0 / math.sqrt(HEAD_DIM_B))
    vb = xb.reshape(B, S, N_HEADS_B, HEAD_DIM_B)
    gates = gates + gate_bias
    i_pre, f_pre = gates[..., :N_HEADS_B], gates[..., N_HEADS_B:]
    hb, C, n, m = mlstm_cell(qb, kb, vb, i_pre, f_pre, C0, n0, m0)
    hb = hb * lax.rsqrt(jnp.mean(hb * hb, axis=-1, keepdims=True) + EPS)
    hb = hb.reshape(B, S, D_B) * mh_norm_g.astype(jnp.float32)
    out_b = jax.nn.sigmoid(ob) * (hb.astype(x.dtype) + skip * c_act)

    x = x + jnp.concatenate([out_a, out_b], axis=-1) @ w_out
    h2 = rmsnorm(x, norm2_g)
    x = x + jnp.square(jax.nn.relu(h2 @ w_ff1)) @ w_ff2
    return x, (new_k, new_v, new_conv, C, n, m)


def setup_inputs(seed: int = 0) -> dict:
    key = jax.random.key(seed)
    ks = jax.random.split(key, 32)
    nrm = jax.random.normal
    f32 = jnp.float32
    win_buf = min(WINDOW_MAX, PAST_LEN)
    gate_bias = jnp.concatenate([
        0.1 * nrm(ks[20], (DEPTH, N_HEADS_B), f32),
        jax.random.uniform(ks[21], (DEPTH, N_HEADS_B), f32, 3.0, 6.0)], axis=-1)
    return {
        "x_prompt": nrm(ks[0], (BATCH, SEQ, D_MODEL), f32),
        "x_sample": nrm(ks[1], (DEC_BATCH, DEC_SEQ, D_MODEL), f32),
        "cache_win_k": nrm(ks[2], (DEPTH, DEC_BATCH, win_buf, N_HEADS_A, HEAD_DIM_A), f32),
        "cache_win_v": nrm(ks[3], (DEPTH, DEC_BATCH, win_buf, N_HEADS_A, HEAD_DIM_A), f32),
        "state_conv": nrm(ks[4], (DEPTH, DEC_BATCH, CONV_W - 1, D_B), f32),
        "state_C": 0.1 * nrm(ks[5], (DEPTH, DEC_BATCH, N_HEADS_B, HEAD_DIM_B, HEAD_DIM_B), f32),
        "state_n": 0.5 * nrm(ks[6], (DEPTH, DEC_BATCH, N_HEADS_B, HEAD_DIM_B), f32),
        "state_m": nrm(ks[7], (DEPTH, DEC_BATCH, N_HEADS_B), f32),
        "rel_bias": 0.5 * nrm(ks[8], (N_BUCKETS, N_HEADS_A), f32),
        "norm1_g": 1.0 + 0.02 * nrm(ks[9], (DEPTH, D_MODEL), f32),
        "w_in": nrm(ks[10], (DEPTH, D_MODEL, D_IN), f32) * D_MODEL ** -0.5,
        "gate_bias": gate_bias,
        "conv_w": 0.5 * nrm(ks[11], (DEPTH, CONV_W, D_B), f32),
        "conv_b": 0.02 * nrm(ks[12], (DEPTH, D_B), f32),
        "wq_head": nrm(ks[13], (DEPTH, N_HEADS_B, HEAD_DIM_B, HEAD_DIM_B), f32) * HEAD_DIM_B ** -0.5,
        "wk_head": nrm(ks[14], (DEPTH, N_HEADS_B, HEAD_DIM_B, HEAD_DIM_B), f32) * HEAD_DIM_B ** -0.5,
        "attn_out_g": 1.0 + 0.02 * nrm(ks[15], (DEPTH, D_A), f32),
        "mh_norm_g": 1.0 + 0.02 * nrm(ks[16], (DEPTH, D_B), f32),
        "skip": 1.0 + 0.02 * nrm(ks[17], (DEPTH, D_B), f32),
        "w_out": nrm(ks[18], (DEPTH, D_MODEL, D_MODEL), f32) * D_MODEL ** -0.5,
        "norm2_g": 1.0 + 0.02 * nrm(ks[19], (DEPTH, D_MODEL), f32),
        "w_ff1": nrm(ks[22], (DEPTH, D_MODEL, D_FF), f32) * D_MODEL ** -0.5,
        "w_ff2": nrm(ks[23], (DEPTH, D_FF, D_MODEL), f32) * D_FF ** -0.5,
        "final_g": 1.0 + 0.02 * nrm(ks[24], (D_MODEL,), f32),
    }


def reference(x_prompt, x_sample, cache_win_k, cache_win_v, state_conv, state_C, state_n, state_m,
              rel_bias, norm1_g, w_in, gate_bias, conv_w, conv_b, wq_head, wk_head,
              attn_out_g, mh_norm_g, skip, w_out, norm2_g, w_ff1, w_ff2, final_g):
    xp, xs = x_prompt, x_sample
    Bp = x_prompt.shape[0]
    sts_p, sts_s = [], []
    for l in range(DEPTH):
        lp = (norm1_g[l], w_in[l], gate_bias[l], conv_w[l], conv_b[l], wq_head[l], wk_head[l],
              attn_out_g[l], mh_norm_g[l], skip[l], w_out[l], norm2_g[l], w_ff1[l], w_ff2[l])
        xp, st_p = hybrid_layer(
            xp, None, None,
            jnp.zeros((Bp, CONV_W - 1, D_B), xp.dtype),
            jnp.zeros((Bp, N_HEADS_B, HEAD_DIM_B, HEAD_DIM_B), jnp.float32),
            jnp.zeros((Bp, N_HEADS_B, HEAD_DIM_B), jnp.float32),
            jnp.zeros((Bp, N_HEADS_B), jnp.float32),
            rel_bias, *lp)
        xs, st_s = hybrid_layer(
            xs, cache_win_k[l], cache_win_v[l], state_conv[l], state_C[l], state_n[l], state_m[l],
            rel_bias, *lp)
        sts_p.append(st_p)
        sts_s.append(st_s)
    y_prompt = rmsnorm(xp, final_g)
    y_sample = rmsnorm(xs, final_g)
    p_k = jnp.stack([s[0] for s in sts_p])
    p_v = jnp.stack([s[1] for s in sts_p])
    p_conv = jnp.stack([s[2] for s in sts_p])
    p_C = jnp.stack([s[3] for s in sts_p])
    p_n = jnp.stack([s[4] for s in sts_p])
    p_m = jnp.stack([s[5] for s in sts_p])
    s_k = jnp.stack([s[0] for s in sts_s])
    s_v = jnp.stack([s[1] for s in sts_s])
    s_conv = jnp.stack([s[2] for s in sts_s])
    s_C = jnp.stack([s[3] for s in sts_s])
    s_n = jnp.stack([s[4] for s in sts_s])
    s_m = jnp.stack([s[5] for s in sts_s])
    return (y_prompt, y_sample, p_k, p_v, p_conv, p_C, p_n, p_m, s_k, s_v, s_conv, s_C, s_n, s_m)
```

```python
import numpy as np
import os
from contextlib import ExitStack
VAR = ''
import concourse.bass as bass
import concourse.mybir as mybir
from concourse.bass_utils import run_bass_kernel_spmd

F32 = mybir.dt.float32
BF16 = mybir.dt.bfloat16
AF = mybir.ActivationFunctionType
ALU = mybir.AluOpType
AX = mybir.AxisListType

D = 1024
DIN = 2440
DA = 384
DB = 640
NHB = 4
EB = 160
DFF = 4096
NCORES = 8
SEG = 2048
WIN = 8192
NT = WIN // 128
MAIN0 = 48
HALO0 = 32
NS = 16
EPS = 1e-6
NEG = -1e30
MASKV = -30000.0


class Eng:
    def __init__(self, fw, name, handle, sem):
        self.fw, self.name, self.h, self.sem = fw, name, handle, sem
        self.count = 0
        self.prog = []
        self.waited = {}
        self.dsems = []
        self.dtot = []
        self.dnext = 0

    def wait(self, sem, val):
        if val <= 0:
            return
        k = id(sem)
        if self.waited.get(k, 0) >= val:
            return
        self.waited[k] = val
        self.prog.append(lambda h, sem=sem, val=val: h.wait_ge(sem, val))


class Buf:
    def __init__(self, name):
        self.name = name
        self.w = {}
        self.r = {}
        self.excl = False


class V:
    def __init__(self, ap, buf):
        self.ap = ap
        self.bufs = list(buf) if isinstance(buf, (list, tuple)) else [buf]

    @property
    def buf(self):
        return self.bufs if len(self.bufs) > 1 else self.bufs[0]


class T:
    def __init__(self, handle, name):
        self.h = handle
        self._buf = Buf(name)
        self.subs = None

    @property
    def buf(self):
        return self.subs if self.subs else self._buf

    def split(self, n):
        self.subs = [Buf("%s.%d" % (self._buf.name, k)) for k in range(n)]
        return self

    def sub(self, k):
        t = T.__new__(T)
        t.h = self.h
        t._buf = self.subs[k]
        t.subs = None
        return t

    def __getitem__(self, key):
        return V(self.h[key], self.buf)

    def ap(self, offset, pat):
        return V(bass.AP(self.h, offset, pat), self.buf)

    def re(self, pat, **kw):
        return V(self.h.ap().rearrange(pat, **kw) if hasattr(self.h, "ap") else self.h[:].rearrange(pat, **kw), self.buf)

    def bitcast(self, dt):
        t = T.__new__(T)
        t.h = self.h.bitcast(dt)
        t._buf = self._buf
        t.subs = self.subs
        return t


class FW:
    def __init__(self, nc):
        self.nc = nc
        mk = lambda n, h: Eng(self, n, h, nc.alloc_semaphore("sem_" + n))
        self.pe = mk("pe", nc.tensor)
        self.act = mk("act", nc.scalar)
        self.dve = mk("dve", nc.vector)
        self.pool = mk("pool", nc.gpsimd)
        self.sp = mk("sp", nc.sync)
        self.engs = [self.pe, self.act, self.dve, self.pool, self.sp]
        for e in (self.sp, self.pool, self.act):
            n = 12
            e.dsems = [nc.alloc_semaphore("dsem_%s_%d" % (e.name, i)) for i in range(n)]
            e.dtot = [0] * n
        self.same_engine_sync = True

    def _deps(self, eng, reads, writes):
        for v in reads:
            for b in v.bufs:
                for sem, val in b.w.values():
                    self._w(eng, sem, val, True)
                if b.excl:
                    for sem, val in b.r.values():
                        self._w(eng, sem, val, False)
        for v in writes:
            for b in v.bufs:
                for sem, val in list(b.w.values()) + list(b.r.values()):
                    self._w(eng, sem, val, False)

    def _w(self, eng, sem, val, raw):
        if sem is eng.sem:
            if eng is self.pe or not self.same_engine_sync or not raw:
                return
        eng.wait(sem, val)

    def _mark(self, sem, val, reads, writes):
        for v in reads:
            for b in v.bufs:
                b.r[id(sem)] = (sem, val)
        for v in writes:
            for b in v.bufs:
                b.w[id(sem)] = (sem, val)

    def op(self, eng, fn, reads, writes, inc=True):
        reads = [v for v in reads if isinstance(v, V)]
        self._deps(eng, reads, writes)
        if inc:
            eng.count += 1
            c = eng.count
            eng.prog.append(lambda h, fn=fn, s=eng.sem: fn(h).then_inc(s, 1))
            self._mark(eng.sem, c, reads, writes)
        else:
            eng.prog.append(lambda h, fn=fn: fn(h))

    def dma(self, q, out, in_, **kw):
        self._deps(q, [in_], [out])
        j = q.dnext
        q.dnext = (j + 1) % len(q.dsems)
        sem = q.dsems[j]
        q.wait(sem, q.dtot[j])
        q.dtot[j] += 16
        tot = q.dtot[j]
        q.prog.append(lambda h, o=out.ap, i=in_.ap, s=sem, kw=kw: h.dma_start(out=o, in_=i, **kw).then_inc(s, 16))
        self._mark(sem, tot, [in_], [out])

    def finish(self, eng, skip=()):
        for e in self.engs:
            if e is not eng:
                eng.wait(e.sem, e.count)
            if e in skip:
                continue
            for s, t in zip(e.dsems, e.dtot):
                eng.wait(s, t)

    def barrier(self):
        for e in self.engs:
            self.finish(e, skip=(self.pool,))

    def emit(self):
        nc = self.nc
        with nc.Block() as block:
            @block.tensor
            def _(h):
                for f in self.pe.prog:
                    f(h)

            @block.scalar
            def _(h):
                for f in self.act.prog:
                    f(h)

            @block.vector
            def _(h):
                for f in self.dve.prog:
                    f(h)

            @block.gpsimd
            def _(h):
                for f in self.pool.prog:
                    f(h)

            @block.sync
            def _(h):
                for f in self.sp.prog:
                    f(h)

    def mm(self, out, lhsT, rhs, start=True, stop=True, inc=None):
        if inc is None:
            inc = stop
        self.op(self.pe, lambda h, o=out.ap, l=lhsT.ap, r=rhs.ap: h.matmul(o, l, r, start=start, stop=stop, skip_group_check=True),
                [lhsT, rhs], [out], inc=inc)

    def transpose(self, out, in_, ident, inc=True):
        self.op(self.pe, lambda h, o=out.ap, i=in_.ap, d=ident.ap: h.transpose(o, i, d), [in_, ident], [out], inc=inc)

    def activation(self, out, in_, func, bias=0.0, scale=1.0, accum_out=None, eng=None):
        rd = [in_, bias, scale]
        wr = [out] + ([accum_out] if accum_out is not None else [])
        b = bias.ap if isinstance(bias, V) else bias
        s = scale.ap if isinstance(scale, V) else scale
        kw = {}
        if accum_out is not None:
            kw["accum_out"] = accum_out.ap
        self.op(self.act, lambda h, o=out.ap, i=in_.ap: h.activation(o, i, func, bias=b, scale=s, **kw), rd, wr)

    def tt(self, eng, out, in0, in1, op):
        self.op(eng, lambda h, o=out.ap, a=in0.ap, b=in1.ap: h.tensor_tensor(o, a, b, op), [in0, in1], [out])

    def ts(self, eng, out, in0, s1, s2, op0, op1=None, accum_out=None):
        a1 = s1.ap if isinstance(s1, V) else s1
        a2 = s2.ap if isinstance(s2, V) else s2
        kw = {}
        if op1 is not None:
            kw["op1"] = op1
        wr = [out]
        if accum_out is not None:
            kw["accum_out"] = accum_out.ap
            wr.append(accum_out)
        self.op(eng, lambda h, o=out.ap, a=in0.ap: h.tensor_scalar(o, a, a1, a2, op0, **kw), [in0, s1, s2], wr)

    def stt(self, out, in0, scalar, in1, op0, op1, accum_out=None):
        sc = scalar.ap if isinstance(scalar, V) else scalar
        kw = {}
        wr = [out]
        if accum_out is not None:
            kw["accum_out"] = accum_out.ap
            wr.append(accum_out)
        self.op(self.dve, lambda h, o=out.ap, a=in0.ap, b=in1.ap: h.scalar_tensor_tensor(o, a, sc, b, op0, op1, **kw),
                [in0, scalar, in1], wr)

    def scale_copy(self, eng, out, in_, sc):
        if eng is self.act:
            self.activation(out, in_, AF.Copy, scale=sc)
        else:
            self.ts(eng, out, in_, sc, None, ALU.mult)

    def copy(self, eng, out, in_):
        if eng is self.act:
            self.op(eng, lambda h, o=out.ap, i=in_.ap: h.copy(o, i), [in_], [out])
        else:
            self.op(eng, lambda h, o=out.ap, i=in_.ap: h.tensor_copy(o, i), [in_], [out])

    def memset(self, eng, out, val):
        self.op(eng, lambda h, o=out.ap: h.memset(o, val), [], [out])

    def reduce(self, out, in_, op, axis=AX.X):
        self.op(self.dve, lambda h, o=out.ap, i=in_.ap: h.tensor_reduce(o, i, axis, op), [in_], [out])

    def recip(self, out, in_):
        self.op(self.dve, lambda h, o=out.ap, i=in_.ap: h.reciprocal(o, i), [in_], [out])

    def scan(self, out, d0, d1, initial, op0, op1):
        ini = initial.ap if isinstance(initial, V) else initial
        self.op(self.dve, lambda h, o=out.ap, a=d0.ap, b=d1.ap: h.tensor_tensor_scan(o, a, b, ini, op0, op1),
                [d0, d1, initial], [out])


_NC_CACHE = {}


def build_nc(stage=99, dbg=False, do_sample=True, do_attn=True, do_ffn=True, tile_list=None, do_cout=True):
    nc = bass.Bass("TRN2", target_bir_lowering=False)
    fw = FW(nc)
    pe, act, dve, pool, sp = fw.pe, fw.act, fw.dve, fw.pool, fw.sp

    in_names = _NC_CACHE.setdefault("in_names", set())

    def din(name, shape, dt=F32):
        in_names.add(name)
        return T(nc.dram_tensor(name, list(shape), dt, kind="ExternalInput"), name)

    def dout(name, shape, dt=F32):
        return T(nc.dram_tensor(name, list(shape), dt, kind="ExternalOutput"), name)

    def dscr(name, shape, dt=F32):
        return T(nc.dram_tensor(name, list(shape), dt, kind="Internal"), name)

    stacks = []

    def push():
        stacks.append(ExitStack())

    def pop():
        fw.barrier()
        stacks.pop().close()

    def sb(name, shape, dt=F32):
        return T(stacks[-1].enter_context(nc.sbuf_tensor(name, list(shape), dt)), name)

    push()

    xw = din("xw", [WIN, D])
    tmA = din("tmA", [4, NT])
    tmV = din("tmV", [4, NT])
    tmO = din("tmO", [128, NT])
    negmask_d = din("negmask", [128, 128])
    sel_d = din("sel", [4, 4 * 128])
    w_in = din("w_in", [D, DIN])
    norm1_g = din("norm1_g", [D, 1])
    gate_bias = din("gate_bias", [8, 1])
    conv_wT = din("conv_wT", [DB, 4])
    conv_b = din("conv_b", [DB, 1])
    wq_d = din("wq", [NHB, EB, EB])
    wk_d = din("wk", [NHB, EB, EB])
    mhg_d = din("mhg", [DB, 1])
    skip_d = din("skip", [DB, 1])

    rel_bias_d = din("rel_bias", [32, 6])
    ohb_d = din("ohb", [3, 32, 512])
    mvec_d = din("mvec", [3, 1, 512])
    attn_g_d = din("attn_g", [DA, 1])
    w_out_d = din("w_out", [D, D])
    norm2_g_d = din("norm2_g", [D, 1])
    w_ff1_d = din("w_ff1", [D, DFF])
    w_ff2_d = din("w_ff2", [DFF, D])
    final_g_d = din("final_g", [1, D])

    xs_d = din("xs", [NS, D])
    cwk_d = din("cwk", [NS, 2048, DA])
    cwv_d = din("cwv", [NS, 2048, DA])
    s_conv_d = din("s_conv", [NS, 3, DB])
    s_C_d = din("s_C", [NS, NHB, EB, EB])
    s_n_d = din("s_n", [NS, NHB, EB])
    s_m_d = din("s_m", [NS, NHB])
    conv_w_d = din("conv_w", [4, DB])
    ohs_d = din("ohs", [3, 32, 128])
    e16_d = din("e16", [NS, NS * 128])
    ecol_d = din("ecol", [128, NS * NS])

    o_y = dout("o_y", [SEG, D])
    o_ys = dout("o_ys", [NS, D])
    o_swk = dout("o_swk", [NS, 2048, DA])
    o_swv = dout("o_swv", [NS, 2048, DA])
    o_sconv = dout("o_sconv", [NS, 3, DB])
    o_sC = dout("o_sC", [NS, NHB, EB, EB])
    o_sn = dout("o_sn", [NS, NHB * EB])
    o_sm = dout("o_sm", [NS, NHB])
    o_wk = dout("o_wk", [SEG, DA])
    o_wv = dout("o_wv", [SEG, DA])
    o_conv = dout("o_conv", [3, DB])
    o_C = dout("o_C", [NHB, EB, EB])
    o_n = dout("o_n", [NHB, EB])
    o_m = dout("o_m", [NHB, 1])
    if dbg:
        o_dbg = dout("o_dbg", [SEG, DB])

    ps = [T(nc.alloc_psum_tensor("ps%d" % i, [128, 512], F32), "ps%d" % i) for i in range(8)]
    for b_ in ps:
        b_._buf.excl = True
    psn = [0]

    reserved = []

    def pb():
        while True:
            b = ps[psn[0] % 8]
            psn[0] += 1
            if b not in reserved:
                return b

    ident_f = sb("ident_f", [128, 128], F32)
    ident_b = sb("ident_b", [128, 128], BF16)
    iot = sb("iot", [128, 128], F32)
    fw.op(pool, lambda h, o=iot[:].ap: h.iota(o, [[1, 128]], base=0, channel_multiplier=-1,
                                              allow_small_or_imprecise_dtypes=True), [], [iot[:]])
    fw.ts(dve, ident_f[:], iot[:], 0.0, None, ALU.is_equal)
    fw.copy(dve, ident_b[:], ident_f[:])
    negmask = sb("negmask_s", [128, 128], F32)
    fw.dma(sp, negmask[:], negmask_d[:])
    sel = sb("sel_s", [4, 4 * 128], F32)
    fw.dma(sp, sel[:], sel_d[:])
    tmA_s = sb("tmA_s", [4, NT], F32)
    tmV_s = sb("tmV_s", [4, NT], F32)
    tmO_s = sb("tmO_s", [128, NT], F32)
    fw.dma(sp, tmA_s[:], tmA[:])
    fw.dma(sp, tmV_s[:], tmV[:])
    fw.dma(sp, tmO_s[:], tmO[:])
    ones4 = sb("ones4", [4, 128], F32)
    fw.memset(dve, ones4[:], 1.0)
    gb_i = sb("gb_i", [4, 1], F32)
    gb_fn = sb("gb_fn", [4, 1], F32)
    fw.dma(sp, gb_i[:], gate_bias[0:4, :])
    fw.dma(sp, gb_fn[:], gate_bias[4:8, :])
    fw.ts(dve, gb_fn[:], gb_fn[:], -1.0, None, ALU.mult)
    cw = sb("cw", [128, 5, 4], F32)
    cb = sb("cb", [128, 5], F32)
    mhg = sb("mhg_s", [128, 5], F32)
    skp = sb("skp_s", [128, 5], F32)
    for c in range(5):
        fw.dma(sp, cw[:, c, :], conv_wT[128 * c:128 * c + 128, :])
        fw.dma(sp, cb[:, c:c + 1], conv_b[128 * c:128 * c + 128, :])
        fw.dma(sp, mhg[:, c:c + 1], mhg_d[128 * c:128 * c + 128, :])
        fw.dma(sp, skp[:, c:c + 1], skip_d[128 * c:128 * c + 128, :])

    if do_sample:
        for (src, dst) in ((cwk_d, o_swk), (cwv_d, o_swv)):
            for g in range(NS):
                for hh in range(4):
                    fw.dma(pool, dst[g, 512 * hh:min(512 * hh + 512, 2047), :],
                           src[g, 512 * hh + 1:min(512 * hh + 513, 2048), :])
    push()
    outaT = sb("outaT", [128, 3, SEG + 128], BF16)
    outbT = sb("outbT", [128, 5, SEG + 128], BF16)
    push()
    KT = sb("KT", [128, 3, 2 * SEG], BF16)
    QT = sb("QT", [128, 3, SEG], BF16)
    push()
    w_in_b = sb("w_in_b", [128, 8, DIN], BF16)
    g1 = sb("g1", [128, 8], F32)
    wq_b = sb("wq_b", [128, 5, DB], BF16)
    wk_b = sb("wk_b", [128, 5, DB], BF16)
    obT = sb("obT", [128, 5, 128], BF16)
    v_scr = dscr("v_scr", [2 * SEG, 6 * 65], BF16)

    CTa = sb("CTa", [128, NHB, EB + 1], F32)
    CTb = sb("CTb", [32, NHB, EB + 1], F32)
    CTa_b = sb("CTa_b", [128, NHB, EB + 1], BF16)
    CTb_b = sb("CTb_b", [32, NHB, EB + 1], BF16)
    m_st = sb("m_st", [4, 1], F32)
    fw.memset(dve, CTa[:], 0.0)
    fw.memset(dve, CTb[:], 0.0)
    CTa.split(NHB)
    CTb.split(NHB)
    fw.memset(dve, CTa_b[:], 0.0)
    fw.memset(dve, CTb_b[:], 0.0)
    fw.memset(dve, m_st[:], 0.0)

    xt = [sb("xt%d" % i, [128, D], F32) for i in range(2)]
    xn_2 = [sb("xn_%d" % i_, [128, D], BF16) for i_ in range(2)]
    xn = xn_2[0]
    hT_2 = [sb("hT_%d" % i_, [128, 8, 128], BF16) for i_ in range(2)]
    hT = hT_2[0]
    ss_2 = [sb("ss_%d" % i_, [128, 1], F32) for i_ in range(2)]
    ss = ss_2[0]
    rstd_2 = [sb("rstd_%d" % i_, [128, 1], F32) for i_ in range(2)]
    rstd = rstd_2[0]
    junk = sb("junk", [128, D], BF16)
    caT_2 = [sb("caT_%d" % i_, [128, 5, 128], BF16) for i_ in range(2)]
    caT = caT_2[0]
    push()
    for c in range(8):
        fw.dma(sp, g1[:, c:c + 1], norm1_g[128 * c:128 * c + 128, :])
    wstage = [sb("wstage%d" % i, [128, 1600], F32) for i in range(2)]
    HW_ = DIN // 2
    for c in range(8):
        for hf in range(2):
            st = wstage[hf]
            fw.dma(sp, st[:, 0:HW_], w_in[128 * c:128 * c + 128, HW_ * hf:HW_ * hf + HW_])
            if hf == 0:
                fw.ts(dve, st[:, 0:DA], st[:, 0:DA], 0.125, None, ALU.mult)
            fw.scale_copy(act if hf else dve, w_in_b[:, c, HW_ * hf:HW_ * hf + HW_], st[:, 0:HW_], g1[:, c:c + 1])
    for (src, dst, scl) in ((wq_d, wq_b, 1.0), (wk_d, wk_b, float(EB) ** -0.5)):
        stv = wstage[0] if src is wq_d else wstage[1]
        fw.memset(dve, stv[:, 0:5 * 320], 0.0)
        for c in range(5):
            lo, hi = 128 * c, 128 * c + 128
            for hh in range(NHB):
                a0, a1 = max(lo, EB * hh), min(hi, EB * hh + EB)
                if a0 >= a1:
                    continue
                slot = hh - (lo // EB)
                fw.dma(sp, stv[a0 - lo:a1 - lo, 320 * c + 160 * slot:320 * c + 160 * slot + 160],
                       src[hh, a0 - EB * hh:a1 - EB * hh, :])
        fw.memset(dve, dst[:], 0.0)
        for c in range(5):
            h0 = (128 * c) // EB
            nh = 2 if (128 * c + 127) // EB > h0 else 1
            fw.ts(dve, dst[:, c, EB * h0:EB * h0 + EB * nh], stv[:, 320 * c:320 * c + EB * nh], scl, None, ALU.mult)

    pop()
    push()
    kvst = [sb("kvst%d" % i, [128, 2 * DA], F32) for i in range(2)]
    xbst = [sb("xbst%d" % i, [128, DB], F32) for i in range(1)]
    vext_2 = [sb("vext_%d" % i_, [128, NHB, EB + 1], BF16) for i_ in range(2)]
    vext = vext_2[0]
    vatt = [sb("vatt%d" % i, [128, 6, 65], BF16) for i in range(2)]
    xbT = [sb("xbT%d" % i, [128, 5, 131], F32) for i in range(2)]
    fw.memset(dve, xbT[0][:], 0.0)
    fw.memset(dve, xbT[1][:], 0.0)
    fw.memset(dve, vext_2[0][:], 1.0)
    fw.memset(dve, vext_2[1][:], 1.0)
    cT_2 = [sb("cT_%d" % i_, [128, 5, 128], F32) for i_ in range(2)]
    cT = cT_2[0]
    ktok_2 = [sb("ktok_%d" % i_, [128, DB], F32) for i_ in range(2)]
    ktok = ktok_2[0]
    kw_2 = [sb("kw_%d" % i_, [128, NHB, EB], BF16) for i_ in range(2)]
    kw = kw_2[0]
    qTa = sb("qTa", [128, NHB, 128], BF16)
    qTb = sb("qTb", [32, NHB, 128], BF16)
    kTa = sb("kTa", [128, NHB, 128], BF16)
    kTb = sb("kTb", [32, NHB, 128], BF16)
    gi_2 = [sb("gi_%d" % i_, [4, 128], F32) for i_ in range(2)]
    gi = gi_2[0]
    gl_2 = [sb("gl_%d" % i_, [4, 128], F32) for i_ in range(2)]
    gl = gl_2[0]
    ga_2 = [sb("ga_%d" % i_, [4, 128], F32) for i_ in range(2)]
    ga = ga_2[0]
    gu_2 = [sb("gu_%d" % i_, [4, 128], F32) for i_ in range(2)]
    gu = gu_2[0]
    gM_2 = [sb("gM_%d" % i_, [4, 128], F32) for i_ in range(2)]
    gM = gM_2[0]
    gnM_2 = [sb("gnM_%d" % i_, [4, 128], F32) for i_ in range(2)]
    gnM = gnM_2[0]
    G1a_2 = [sb("G1a_%d" % i_, [4, 128], F32) for i_ in range(2)]
    G1a = G1a_2[0]
    G1b_2 = [sb("G1b_%d" % i_, [4, 128], F32) for i_ in range(2)]
    G1b = G1b_2[0]
    G1c_2 = [sb("G1c_%d" % i_, [4, 128], F32) for i_ in range(2)]
    G1c = G1c_2[0]
    gdl_2 = [sb("gdl_%d" % i_, [4, 1], F32) for i_ in range(2)]
    gdl = gdl_2[0]
    TM_2 = [sb("TM_%d" % i_, [128, 16], F32) for i_ in range(2)]
    TM = TM_2[0]
    TMe_2 = [sb("TMe_%d" % i_, [128, 12], F32) for i_ in range(2)]
    TMe = TMe_2[0]
    dec_bc_2 = [sb("dec_bc_%d" % i_, [128, 4], F32) for i_ in range(2)]
    dec_bc = dec_bc_2[0]
    i4 = sb("i4", [4, 4], F32)
    fw.copy(dve, i4[:], ident_f[0:4, 0:4])
    expD = sb("expD", [128, 128], F32)
    AT = sb("AT", [128, 128], BF16)
    inter_s = sb("inter_s", [128, EB + 1], F32)
    numd = sb("numd", [128, EB + 1], F32)
    sc1 = sb("sc1", [128, 8], F32)
    hbn = sb("hbn", [128, DB], BF16)
    gt1 = sb("gt1", [128, 5, 128], F32)

    def load_x(i):
        fw.dma(sp, xt[i % 2][:], xw[128 * i:128 * i + 128, :])

    tile_list = list(range(NT)) if tile_list is None else tile_list
    if tile_list:
        load_x(tile_list[0])
    obT_2 = [obT, sb("obT_b", [128, 5, 128], BF16)]

    def stage1a(i):
            p = i % 2
            xn = xn_2[p]
            hT = hT_2[p]
            ss = ss_2[p]
            rstd = rstd_2[p]
            caT = caT_2[p]
            vext = vext_2[p]
            cT = cT_2[p]
            ktok = ktok_2[p]
            kw = kw_2[p]
            gi = gi_2[p]
            gl = gl_2[p]
            ga = ga_2[p]
            gu = gu_2[p]
            gM = gM_2[p]
            gnM = gnM_2[p]
            G1a = G1a_2[p]
            G1b = G1b_2[p]
            G1c = G1c_2[p]
            gdl = gdl_2[p]
            TM = TM_2[p]
            TMe = TMe_2[p]
            dec_bc = dec_bc_2[p]
            is_main = i >= MAIN0
            obT = obT_2[p]
            if cT.subs is None:
                cT.split(5)
            is_halo = i >= HALO0
            mt = i - MAIN0
            if i + 1 < NT and (i + 1) in tile_list:
                load_x(i + 1)
            fw.stt(junk[:], xt[p][:], 1.0, xt[p][:], ALU.mult, ALU.mult, accum_out=ss[:])
            fw.ts(dve, rstd[:], ss[:], 1.0 / D, EPS, ALU.mult, ALU.add)
            fw.activation(rstd[:], rstd[:], AF.Ln)
            fw.activation(rstd[:], rstd[:], AF.Exp, scale=-0.5)
            fw.ts(dve, xn[:], xt[p][:], rstd[:], None, ALU.mult)
            bT = pb()
            pT = bT.bitcast(BF16)
            for c in range(8):
                fw.transpose(pT[:, 128 * c:128 * c + 128], xn[:, 128 * c:128 * c + 128], ident_b[:], inc=(c == 7))
            fw.copy(act, hT.re("p c t -> p (c t)"), pT[:, 0:1024])


    def stage1b(i):
            p = i % 2
            xn = xn_2[p]
            hT = hT_2[p]
            ss = ss_2[p]
            rstd = rstd_2[p]
            caT = caT_2[p]
            vext = vext_2[p]
            cT = cT_2[p]
            ktok = ktok_2[p]
            kw = kw_2[p]
            gi = gi_2[p]
            gl = gl_2[p]
            ga = ga_2[p]
            gu = gu_2[p]
            gM = gM_2[p]
            gnM = gnM_2[p]
            G1a = G1a_2[p]
            G1b = G1b_2[p]
            G1c = G1c_2[p]
            gdl = gdl_2[p]
            TM = TM_2[p]
            TMe = TMe_2[p]
            dec_bc = dec_bc_2[p]
            is_main = i >= MAIN0
            obT = obT_2[p]
            if cT.subs is None:
                cT.split(5)
            is_halo = i >= HALO0
            mt = i - MAIN0
            def proj_tok(c0, n):
                b = pb()
                for c in range(8):
                    fw.mm(b[:, 0:n], hT[:, c, :], w_in_b[:, c, c0:c0 + n], start=(c == 0), stop=(c == 7))
                return b

            def proj_feat(c0, nchunks, M=128):
                b = pb()
                for j in range(nchunks):
                    for c in range(8):
                        fw.mm(b[0:M, 128 * j:128 * j + 128], w_in_b[:, c, c0 + M * j:c0 + M * j + M], hT[:, c, :],
                              start=(c == 0), stop=(c == 7))
                return b

            if is_halo:
                at = i - HALO0
                bk = proj_feat(DA, 3)
                fw.copy(act, KT[:, :, 128 * at:128 * at + 128], V(bk.h[:, 0:384].rearrange("p (c t) -> p c t", c=3), bk.buf))
                bkv = proj_tok(2 * DA, DA)
                va = vatt[p]
                if is_main:
                    bkk = proj_tok(DA, DA)
                    st = kvst[p]
                    fw.copy(dve, st[:, 0:DA], bkk[:, 0:DA])
                    fw.copy(dve, st[:, DA:2 * DA], bkv[:, 0:DA])
                    r0 = 128 * mt
                    fw.dma(sp, o_wk[r0:r0 + 128, :], st[:, 0:DA])
                    fw.dma(sp, o_wv[r0:r0 + 128, :], st[:, DA:2 * DA])
                fw.copy(act, va[:, :, 0:64], V(bkv.h[:, 0:DA].rearrange("p (h e) -> p h e", h=6), bkv.buf))
                fw.copy(dve, va[:, :, 64:65], V(tmO_s.h[:, i:i + 1].unsqueeze(1).to_broadcast([128, 6, 1]), tmO_s.buf))
                fw.dma(sp, v_scr[128 * at:128 * at + 128, :], va.re("p h e -> p (h e)"))
            if is_main:
                bq = proj_feat(0, 3)
                fw.copy(act, QT[:, :, 128 * mt:128 * mt + 128], V(bq.h[:, 0:384].rearrange("p (c t) -> p c t", c=3), bq.buf))
                b1 = proj_feat(3 * DA + DB, 4)
                fw.activation(obT[:, 0:4, :], V(b1.h[:, 0:512].rearrange("p (c t) -> p c t", c=4), b1.buf), AF.Sigmoid)
                b2 = proj_feat(3 * DA + DB + 512, 1)
                fw.activation(obT[:, 4, :], b2[:, 0:128], AF.Sigmoid)

            if stage < 1:
                return
            xT = xbT[p]
            xTp = xbT[1 - p]
            fw.copy(dve, xT[:, :, 0:3], xTp[:, :, 128:131])
            if stage < 1.2:
                return
            b1 = proj_feat(3 * DA, 4)
            if stage < 1.4:
                return
            fw.copy(act, xT[:, 0:4, 3:131], V(b1.h[:, 0:512].rearrange("p (c t) -> p c t", c=4), b1.buf))
            if stage < 1.6:
                return
            b2 = proj_feat(3 * DA + 512, 1)
            if stage < 1.7:
                return
            if stage < 1.8:
                fw.copy(act, junk[:, 0:128], b2[:, 0:128])
                return
            if VAR == 'dve':
                fw.copy(dve, xT[:, 4, 3:131], b2[:, 0:128])
            elif VAR == 'col0':
                fw.copy(act, xT[:, 4, 0:128], b2[:, 0:128])
            elif VAR == 'chunk3':
                fw.copy(act, xT[:, 3, 3:131], b2[:, 0:128])
            else:
                fw.copy(act, xT[:, 4, 3:131], b2[:, 0:128])
            if stage < 2:
                return
            bg = pb()
            for (j, c0) in ((0, 3 * DA + 2 * DB), (1, 3 * DA + 2 * DB + 4)):
                for c in range(8):
                    fw.mm(bg[0:4, 128 * j:128 * j + 128], w_in_b[:, c, c0:c0 + 4], hT[:, c, :], start=(c == 0), stop=(c == 7))
            if stage < 2.1:
                fw.copy(dve, gi[:], bg[0:4, 0:128])
                return
            fw.ts(dve, gi[:], bg[0:4, 0:128], gb_i[:], tmA_s[:, i:i + 1], ALU.add, ALU.add)
            if stage < 2.2:
                return
            fw.activation(gl[:], bg[0:4, 128:256], AF.Exp, bias=gb_fn[:], scale=-1.0)
            if stage < 2.3:
                return
            fw.activation(gl[:], gl[:], AF.Ln, bias=1.0)
            if stage < 2.4:
                return
            fw.ts(dve, gl[:], gl[:], tmV_s[:, i:i + 1], None, ALU.mult)
            if stage < 3:
                return
            bx = proj_tok(3 * DA, 512)
            bx2 = proj_tok(3 * DA + 512, 128)
            for hh in range(NHB):
                c0 = EB * hh
                if c0 + EB <= 512:
                    fw.copy(act if hh % 2 else dve, vext[:, hh, 0:EB], bx[:, c0:c0 + EB])
                else:
                    fw.copy(dve, vext[:, hh, 0:512 - c0], bx[:, c0:512])
                    fw.copy(act, vext[:, hh, 512 - c0:EB], bx2[:, 0:c0 + EB - 512])
            if i == NT - 1:
                xs_ = xbst[0]
                fw.copy(dve, xs_[:, 0:512], bx[:, 0:512])
                fw.copy(dve, xs_[:, 512:640], bx2[:, 0:128])
                fw.dma(sp, o_conv[:, :], xs_[125:128, :])
            if stage < 4:
                return
            for c in range(5):
                fw.activation(cT.sub(c)[:, c, :], xT[:, c, 0:128], AF.Identity, bias=cb[:, c:c + 1], scale=cw[:, c, 0:1])
            for k in range(1, 4):
                for c in range(5):
                    fw.stt(cT.sub(c)[:, c, :], xT[:, c, k:k + 128], cw[:, c, k:k + 1], cT.sub(c)[:, c, :], ALU.mult, ALU.add)
            fw.activation(caT[:], cT[:], AF.Silu)
            if stage < 5:
                return
            bk1 = pb()
            bk2 = pb()
            for c in range(5):
                fw.mm(bk1[:, 0:512], caT[:, c, :], wk_b[:, c, 0:512], start=(c == 0), stop=(c == 4))
            for c in range(5):
                fw.mm(bk2[:, 0:128], caT[:, c, :], wk_b[:, c, 512:640], start=(c == 0), stop=(c == 4))
            fw.copy(act, ktok[:, 0:512], bk1[:, 0:512])
            fw.copy(act, ktok[:, 512:640], bk2[:, 0:128])


    def stage2a(i):
            p = i % 2
            xn = xn_2[p]
            hT = hT_2[p]
            ss = ss_2[p]
            rstd = rstd_2[p]
            caT = caT_2[p]
            vext = vext_2[p]
            cT = cT_2[p]
            ktok = ktok_2[p]
            kw = kw_2[p]
            gi = gi_2[p]
            gl = gl_2[p]
            ga = ga_2[p]
            gu = gu_2[p]
            gM = gM_2[p]
            gnM = gnM_2[p]
            G1a = G1a_2[p]
            G1b = G1b_2[p]
            G1c = G1c_2[p]
            gdl = gdl_2[p]
            TM = TM_2[p]
            TMe = TMe_2[p]
            dec_bc = dec_bc_2[p]
            is_main = i >= MAIN0
            obT = obT_2[p]
            if cT.subs is None:
                cT.split(5)
            is_halo = i >= HALO0
            mt = i - MAIN0
            if stage < 6:
                return
            fw.scan(ga[:], ones4[:], gl[:], 0.0, ALU.mult, ALU.add)
            fw.tt(dve, gu[:], gi[:], ga[:], ALU.subtract)
            fw.scan(gM[:], gu[:], gu[:], m_st[:], ALU.max, ALU.max)
            fw.ts(dve, gnM[:], gM[:], -1.0, None, ALU.mult)
            fw.ts(dve, G1a[:], gu[:], gM[:, 127:128], None, ALU.subtract)
            fw.ts(dve, G1b[:], gM[:], -1.0, m_st[:], ALU.mult, ALU.add)
            fw.stt(G1c[:], ga[:], -1.0, gM[:], ALU.mult, ALU.subtract)
            fw.ts(dve, gdl[:], gM[:, 127:128], -1.0, m_st[:], ALU.mult, ALU.add)
            if stage < 7:
                return
            bt = pb()
            for (j_, g_) in enumerate((G1a, G1b, G1c, gu)):
                fw.transpose(bt[:, 4 * j_:4 * j_ + 4], g_[:], ident_f[0:4, 0:4])
            fw.copy(dve, TM[:], bt[:, 0:16])
            fw.activation(TMe[:], TM[:, 0:12], AF.Exp)
            if stage < 8:
                return
            bd = pb()
            fw.mm(bd[:, 0:4], V(gdl.h[:, 0:1].to_broadcast([4, 128]), gdl.buf), i4[:], start=True, stop=True)
            fw.activation(dec_bc[:], bd[:, 0:4], AF.Exp)

            if stage < 9:
                return
            if is_main:
                for (wsrc, da_, db_) in ((wq_b, qTa, qTb), (wk_b, kTa, kTb)):
                    ba = pb()
                    bb = pb()
                    for hh in range(NHB):
                        cs = sorted(set([(EB * hh) // 128, (EB * hh + EB - 1) // 128]))
                        for n_, c in enumerate(cs):
                            fw.mm(ba[:, 128 * hh:128 * hh + 128], wsrc[:, c, EB * hh:EB * hh + 128], caT[:, c, :],
                                  start=(n_ == 0), stop=(n_ == len(cs) - 1))
                        for n_, c in enumerate(cs):
                            fw.mm(bb[0:32, 128 * hh:128 * hh + 128], wsrc[:, c, EB * hh + 128:EB * hh + EB], caT[:, c, :],
                                  start=(n_ == 0), stop=(n_ == len(cs) - 1))
                    fw.copy(act, da_.re("p h t -> p (h t)"), ba[:, 0:512])
                    fw.copy(dve, db_.re("p h t -> p (h t)"), bb[0:32, 0:512])
                for hh in range(NHB):
                    bs = pb()
                    fw.mm(bs[:, 0:128], kTa[:, hh, :], qTa[:, hh, :], start=True, stop=False)
                    fw.mm(bs[:, 0:128], kTb[:, hh, :], qTb[:, hh, :], start=False, stop=True)
                    fw.mm(bs[:, 128:256], sel[:, 128 * hh:128 * hh + 128], gnM[:], start=True, stop=False)
                    fw.mm(bs[:, 128:256], ident_f[:], negmask[:], start=False, stop=True)
                    fw.activation(expD[:], bs[:, 128:256], AF.Exp, bias=TM[:, 12 + hh:13 + hh])
                    fw.tt(dve, AT[:], bs[:, 0:128], expD[:], ALU.mult)
                    bn = pb()
                    fw.mm(bn[:, 0:EB + 1], AT[:], vext[:, hh, :], start=True, stop=True)
                    fw.mm(bn[:, 256:256 + EB + 1], qTa[:, hh, :], CTa_b[:, hh, :], start=True, stop=False)
                    fw.mm(bn[:, 256:256 + EB + 1], qTb[:, hh, :], CTb_b[:, hh, :], start=False, stop=True)
                    fw.activation(inter_s[:], bn[:, 256:256 + EB + 1], AF.Copy, scale=TMe[:, 4 + hh:5 + hh])
                    fw.tt(dve, numd[:], bn[:, 0:EB + 1], inter_s[:], ALU.add)
                    fw.stt(sc1[:, 6:7], numd[:, EB:EB + 1], -1.0, numd[:, EB:EB + 1], ALU.mult, ALU.max)
                    fw.ts(dve, sc1[:, 0:1], sc1[:, 6:7], TMe[:, 8 + hh:9 + hh], None, ALU.max)
                    fw.recip(sc1[:, 1:2], sc1[:, 0:1])
                    fw.stt(junk[:, 0:EB], numd[:, 0:EB], 1.0, numd[:, 0:EB], ALU.mult, ALU.mult, accum_out=sc1[:, 2:3])
                    fw.tt(dve, sc1[:, 3:4], sc1[:, 1:2], sc1[:, 1:2], ALU.mult)
                    fw.tt(dve, sc1[:, 3:4], sc1[:, 3:4], sc1[:, 2:3], ALU.mult)
                    fw.ts(dve, sc1[:, 3:4], sc1[:, 3:4], 1.0 / EB, EPS, ALU.mult, ALU.add)
                    fw.activation(sc1[:, 4:5], sc1[:, 3:4], AF.Ln)
                    fw.activation(sc1[:, 4:5], sc1[:, 4:5], AF.Exp, scale=-0.5)
                    fw.tt(dve, sc1[:, 5:6], sc1[:, 4:5], sc1[:, 1:2], ALU.mult)
                    fw.ts(dve, hbn[:, EB * hh:EB * hh + EB], numd[:, 0:EB], sc1[:, 5:6], None, ALU.mult)
                bh = pb()
                bh2 = pb()
                pTh = bh.bitcast(BF16)
                pTh2 = bh2.bitcast(BF16)
                for c in range(5):
                    dstp = pTh[:, 128 * c:128 * c + 128] if c < 4 else pTh2[:, 0:128]
                    fw.transpose(dstp, hbn[:, 128 * c:128 * c + 128], ident_b[:])
                for c in range(5):
                    srcp = pTh[:, 128 * c:128 * c + 128] if c < 4 else pTh2[:, 0:128]
                    fw.ts(dve, gt1[:, c, :], srcp, mhg[:, c:c + 1], None, ALU.mult)
                    fw.stt(gt1[:, c, :], caT[:, c, :], skp[:, c:c + 1], gt1[:, c, :], ALU.mult, ALU.add)
                fw.tt(dve, outbT[:, :, 128 * mt:128 * mt + 128], gt1[:], obT[:], ALU.mult)


    def stage2b(i):
            p = i % 2
            xn = xn_2[p]
            hT = hT_2[p]
            ss = ss_2[p]
            rstd = rstd_2[p]
            caT = caT_2[p]
            vext = vext_2[p]
            cT = cT_2[p]
            ktok = ktok_2[p]
            kw = kw_2[p]
            gi = gi_2[p]
            gl = gl_2[p]
            ga = ga_2[p]
            gu = gu_2[p]
            gM = gM_2[p]
            gnM = gnM_2[p]
            G1a = G1a_2[p]
            G1b = G1b_2[p]
            G1c = G1c_2[p]
            gdl = gdl_2[p]
            TM = TM_2[p]
            TMe = TMe_2[p]
            dec_bc = dec_bc_2[p]
            is_main = i >= MAIN0
            obT = obT_2[p]
            if cT.subs is None:
                cT.split(5)
            is_halo = i >= HALO0
            mt = i - MAIN0
            if stage < 10:
                return
            fw.tt(dve, kw[:], V(ktok.h[:].rearrange("p (h e) -> p h e", h=NHB), ktok.buf),
                  V(TMe.h[:, 0:4].unsqueeze(2).to_broadcast([128, NHB, EB]), TMe.buf), ALU.mult)
            for hh in range(NHB):
                bu = pb()
                fw.mm(bu[:, 0:EB + 1], kw[:, hh, 0:128], vext[:, hh, :], start=True, stop=True)
                fw.mm(bu[0:32, 256:256 + EB + 1], kw[:, hh, 128:EB], vext[:, hh, :], start=True, stop=True)
                fw.stt(CTa.sub(hh)[:, hh, :], CTa.sub(hh)[:, hh, :], dec_bc[:, hh:hh + 1], bu[:, 0:EB + 1], ALU.mult, ALU.add)
                fw.stt(CTb.sub(hh)[:, hh, :], CTb.sub(hh)[:, hh, :], dec_bc[0:32, hh:hh + 1], bu[0:32, 256:256 + EB + 1], ALU.mult, ALU.add)
            fw.copy(act, CTa_b[:], CTa[:])
            fw.copy(act, CTb_b[:], CTb[:])
            fw.tt(dve, m_st[:], ga[:, 127:128], gM[:, 127:128], ALU.add)


    for idx_, i in enumerate(tile_list):
        stage1a(i)
        if idx_ > 0:
            stage2a(tile_list[idx_ - 1])
        stage1b(i)
        if idx_ > 0:
            stage2b(tile_list[idx_ - 1])
    if tile_list:
        stage2a(tile_list[-1])
        stage2b(tile_list[-1])
    xn, hT, ss, rstd, caT = xn_2[0], hT_2[0], ss_2[0], rstd_2[0], caT_2[0]
    pop()
    push()
    if do_sample:
        zs = sb("zs", [NS, DIN], F32)
        s_q = dscr("s_q", [NS * NHB, EB], F32)
        s_k = dscr("s_k", [NS * NHB, EB], F32)
        s_v = dscr("s_v", [NS * NHB, EB], F32)
        s_sc = dscr("s_sc", [NS * NHB, 4], F32)
        s_misc = dscr("s_misc", [NS, 2 * DB + 3 * DA], F32)
        p = 0
        fw.dma(sp, xt[p][0:NS, :], xs_d[:, :])
        fw.stt(junk[0:NS, :], xt[p][0:NS, :], 1.0, xt[p][0:NS, :], ALU.mult, ALU.mult, accum_out=ss[0:NS, :])
        fw.ts(dve, rstd[0:NS, :], ss[0:NS, :], 1.0 / D, EPS, ALU.mult, ALU.add)
        fw.activation(rstd[0:NS, :], rstd[0:NS, :], AF.Ln)
        fw.activation(rstd[0:NS, :], rstd[0:NS, :], AF.Exp, scale=-0.5)
        fw.ts(dve, xn[0:NS, :], xt[p][0:NS, :], rstd[0:NS, :], None, ALU.mult)
        bT = pb()
        pT = bT.bitcast(BF16)
        for c in range(8):
            fw.transpose(pT[:, 128 * c:128 * c + NS], xn[0:NS, 128 * c:128 * c + 128], ident_b[0:NS, 0:NS], inc=(c == 7))
        for c in range(8):
            fw.copy(act, hT[:, c, 0:NS], pT[:, 128 * c:128 * c + NS])
        for c0 in range(0, DIN, 512):
            n = min(512, DIN - c0)
            bk_ = pb()
            for c in range(8):
                fw.mm(bk_[0:NS, 0:n], hT[:, c, 0:NS], w_in_b[:, c, c0:c0 + n], start=(c == 0), stop=(c == 7))
            fw.copy(act if (c0 // 512) % 2 else dve, zs[:, c0:c0 + n], bk_[0:NS, 0:n])
        fw.dma(sp, o_swk[:, 2047, :], zs[:, DA:2 * DA])
        fw.dma(sp, o_swv[:, 2047, :], zs[:, 2 * DA:3 * DA])
        fw.dma(sp, o_sconv[:, 2, :], zs[:, 3 * DA:3 * DA + DB])
        fw.dma(sp, o_sconv[:, 0:2, :], s_conv_d[:, 1:3, :])
        cwr = sb("cwr", [NS, 4, DB], F32)
        cbr = sb("cbr", [NS, DB], F32)
        scv = sb("scv", [NS, 3, DB], F32)
        fw.dma(sp, cwr.re("p i d -> p (i d)"), V(conv_w_d.h.ap().rearrange("i d -> (i d)").partition_broadcast(NS), conv_w_d.buf))
        fw.dma(sp, cbr[:], V(conv_b.h.ap().rearrange("d o -> (d o)").partition_broadcast(NS), conv_b.buf))
        fw.dma(sp, scv[:], s_conv_d[:, :, :])
        cs = sb("cs", [NS, DB], F32)
        cs2 = sb("cs2", [NS, DB], F32)
        fw.tt(dve, cs[:], zs[:, 3 * DA:3 * DA + DB], cwr[:, 3, :], ALU.mult)
        fw.tt(dve, cs[:], cs[:], cbr[:], ALU.add)
        for k in range(3):
            fw.tt(dve, cs2[:], scv[:, k, :], cwr[:, k, :], ALU.mult)
            fw.tt(dve, cs[:], cs[:], cs2[:], ALU.add)
        cas = sb("cas", [NS, DB], F32)
        casb = sb("casb", [NS, DB], BF16)
        fw.activation(cas[:], cs[:], AF.Silu)
        fw.copy(dve, casb[:], cas[:])
        fw.dma(sp, s_misc[:, DB:2 * DB], cas[:])
        sob = sb("sob", [NS, DB], F32)
        fw.activation(sob[:], zs[:, 3 * DA + DB:3 * DA + 2 * DB], AF.Sigmoid)
        fw.dma(sp, s_misc[:, 0:DB], sob[:])
        fw.dma(sp, s_misc[:, 2 * DB:2 * DB + 3 * DA], zs[:, 0:3 * DA])
        bT2 = pb()
        pT2 = bT2.bitcast(BF16)
        for c in range(5):
            fw.transpose(pT2[:, 128 * c:128 * c + NS], casb[:, 128 * c:128 * c + 128], ident_b[0:NS, 0:NS], inc=(c == 4))
        for c in range(5):
            fw.copy(act, caT[:, c, 0:NS], pT2[:, 128 * c:128 * c + NS])
        sqk = sb("sqk", [NS, 2, DB], F32)
        for wi, wsrc in enumerate((wq_b, wk_b)):
            for (c0, n) in ((0, 512), (512, 128)):
                bk_ = pb()
                for c in range(5):
                    fw.mm(bk_[0:NS, 0:n], caT[:, c, 0:NS], wsrc[:, c, c0:c0 + n], start=(c == 0), stop=(c == 4))
                fw.copy(dve, sqk[:, wi, c0:c0 + n], bk_[0:NS, 0:n])
        fw.dma(sp, V(s_q.h.ap().rearrange("(b h) e -> b (h e)", h=NHB), s_q.buf), sqk[:, 0, :])
        fw.dma(sp, V(s_k.h.ap().rearrange("(b h) e -> b (h e)", h=NHB), s_k.buf), sqk[:, 1, :])
        fw.dma(sp, V(s_v.h.ap().rearrange("(b h) e -> b (h e)", h=NHB), s_v.buf), zs[:, 3 * DA:3 * DA + DB])
        gbr = sb("gbr", [NS, 8], F32)
        sm0 = sb("sm0", [NS, NHB], F32)
        fw.dma(sp, gbr[:], V(gate_bias.h.ap().rearrange("g o -> (g o)").partition_broadcast(NS), gate_bias.buf))
        fw.dma(sp, sm0[:], s_m_d[:, :])
        sg = sb("sg", [NS, 8, NHB], F32)
        fw.tt(dve, sg[:, 0, :], zs[:, DIN - 8:DIN - 4], gbr[:, 0:4], ALU.add)
        fw.tt(dve, sg[:, 1, :], zs[:, DIN - 4:DIN], gbr[:, 4:8], ALU.add)
        fw.activation(sg[:, 1, :], sg[:, 1, :], AF.Exp, scale=-1.0)
        fw.activation(sg[:, 1, :], sg[:, 1, :], AF.Ln, bias=1.0)
        fw.tt(dve, sg[:, 2, :], sm0[:], sg[:, 1, :], ALU.subtract)
        fw.tt(dve, sg[:, 3, :], sg[:, 2, :], sg[:, 0, :], ALU.max)
        fw.tt(dve, sg[:, 4, :], sg[:, 2, :], sg[:, 3, :], ALU.subtract)
        fw.tt(dve, sg[:, 5, :], sg[:, 0, :], sg[:, 3, :], ALU.subtract)
        fw.ts(dve, sg[:, 6, :], sg[:, 3, :], -1.0, None, ALU.mult)
        ssc = sb("ssc", [NS, NHB, 4], F32)
        fw.activation(ssc[:, :, 0], sg[:, 4, :], AF.Exp)
        fw.activation(ssc[:, :, 1], sg[:, 5, :], AF.Exp)
        fw.activation(ssc[:, :, 2], sg[:, 6, :], AF.Exp)
        fw.copy(dve, ssc[:, :, 3], sg[:, 3, :])
        fw.dma(sp, o_sm[:, :], sg[:, 3, :])
        fw.dma(sp, V(s_sc.h.ap().rearrange("(b h) x -> b (h x)", h=NHB), s_sc.buf), ssc.re("p h x -> p (h x)"))

    Co_a = sb("Co_a", [128, NHB, EB], F32)
    Co_b = sb("Co_b", [33, NHB, EB], F32)
    for hh in range(NHB if do_cout else 0):
        b_ = pb()
        fw.transpose(b_[:, 0:128], CTa[:, hh, 0:128], ident_f[:])
        fw.transpose(b_[:, 128:160], CTb[:, hh, 0:128], ident_f[0:32, 0:32])
        fw.transpose(b_[0:33, 256:384], CTa[:, hh, 128:EB + 1], ident_f[:])
        fw.transpose(b_[0:33, 384:416], CTb[:, hh, 128:EB + 1], ident_f[0:32, 0:32])
        fw.copy(dve, Co_a[:, hh, :], b_[:, 0:EB])
        fw.copy(dve, Co_b[:, hh, :], b_[0:33, 256:256 + EB])
        fw.dma(sp, o_C[hh, 0:128, :], Co_a[:, hh, :])
        fw.dma(sp, o_C[hh, 128:EB, :], Co_b[0:32, hh, :])
        fw.dma(sp, o_n[hh:hh + 1, :], Co_b[32:33, hh, :])
    fw.dma(sp, o_m[:, :], m_st[:])
    if dbg:
        for mt in range(SEG // 128):
            pass

    pop()
    pop()
    push()
    if do_attn:
        rrep = dscr("rrep", [18 * 128 * 512 + 512], F32)
        oz_scr = [dscr("oz_scr%d" % b_, [SEG, 390], F32) for b_ in range(3)]
        rb = sb("rb", [32, 6], F32)
        ohb = sb("ohb_s", [32, 3, 512], F32)
        mvec = sb("mvec_s", [1, 3, 512], F32)
        ones1 = sb("ones1", [1, 128], F32)
        ag = sb("ag", [128, 3], F32)
        fw.dma(sp, rb[:], rel_bias_d[:])
        for b_ in range(3):
            fw.dma(sp, ohb[:, b_, :], ohb_d[b_])
            fw.dma(sp, mvec[:, b_, :], mvec_d[b_])
            fw.dma(sp, ag[:, b_:b_ + 1], attn_g_d[128 * b_:128 * b_ + 128, :])
        fw.memset(dve, ones1[:], 1.0)
        vfs = [sb("vfs%d" % i, [128, 512], F32) for i in range(2)]
        for b_ in range(3):
            for h in range(6):
                bk_ = pb()
                fw.mm(bk_[:, 0:512], V(rb.h[:, h:h + 1].to_broadcast([32, 128]), rb.buf), ohb[:, b_, :], start=True, stop=False)
                fw.mm(bk_[:, 0:512], ones1[:], mvec[:, b_, :], start=False, stop=True)
                vf = vfs[h % 2]
                fw.copy(act if h % 2 else dve, vf[:], bk_[:, 0:512])
                o0 = (b_ * 6 + h) * 128 * 512
                fw.dma(sp, rrep.ap(o0, [[512, 128], [1, 512]]), vf[:])
        biasT = sb("biasT", [128, 6, 2, 128], F32)
        stmp_2 = [sb("stmp%d" % i_, [128, 512], F32) for i_ in range(2)]
        PT_2 = [sb("PT%d" % i_, [128, 3, 512], BF16) for i_ in range(2)]
        Vp = [sb("Vp%d" % i, [128, 6, 65], BF16) for i in range(2)]
        Vo = [sb("Vo%d" % i, [128, 6, 65], BF16) for i in range(2)]
        ozs = [sb("ozs%d" % i, [128, 390], F32) for i in range(2)]
        srcs = []
        for b_, dil in enumerate((1, 4, 16)):
            if VAR == 'setup' or (VAR.startswith('br') and str(b_) not in VAR):
                continue
            unit = 128 * dil
            for n_ in range(SEG // unit):
                for r_ in range(dil):
                    srcs.append((b_, dil, n_ * unit + r_, n_ == 0 and r_ == 0))

        def att1(t_):
            b_, dil, q0, first = srcs[t_]
            unit = 128 * dil
            pp = t_ % 2
            stmp, PT = stmp_2[pp], PT_2[pp]
            if first:
                for h in range(6):
                    o0 = (b_ * 6 + h) * 128 * 512
                    fw.dma(sp, biasT[:, h, 0, :], rrep.ap(o0 + 255, [[511, 128], [1, 128]]))
                    fw.dma(sp, biasT[:, h, 1, :], rrep.ap(o0 + 127, [[511, 128], [1, 128]]))
            k_own = SEG + q0
            k_prev = k_own - unit
            fw.dma(sp, Vp[pp].re("p h e -> p (h e)"), v_scr[k_prev:k_prev + 127 * dil + 1:dil, :])
            fw.dma(sp, Vo[pp].re("p h e -> p (h e)"), v_scr[k_own:k_own + 127 * dil + 1:dil, :])
            for c in range(3):
                for hh in range(2):
                    bank = pb()
                    for blk, k0 in enumerate((k_prev, k_own)):
                        fw.mm(bank[:, 128 * blk:128 * blk + 128], KT[64 * hh:64 * hh + 64, c, k0:k0 + 127 * dil + 1:dil],
                              QT[64 * hh:64 * hh + 64, c, q0:q0 + 127 * dil + 1:dil], start=True, stop=True)
                    fw.tt(dve, stmp[:, 256 * hh:256 * hh + 256], bank[:, 0:256],
                          V(biasT.h[:, 2 * c + hh, :, :].rearrange("p b q -> p (b q)"), biasT.buf), ALU.add)
                    fw.activation(PT[:, c, 256 * hh:256 * hh + 256], stmp[:, 256 * hh:256 * hh + 256], AF.Exp)

        def att2(t_):
            b_, dil, q0, first = srcs[t_]
            pp = t_ % 2
            PT = PT_2[pp]
            boz = pb()
            for h in range(6):
                c, hh = divmod(h, 2)
                fw.mm(boz[:, 65 * h:65 * h + 65], PT[:, c, (2 * hh) * 128:(2 * hh) * 128 + 128], Vp[pp][:, h, :], start=True, stop=False)
                fw.mm(boz[:, 65 * h:65 * h + 65], PT[:, c, (2 * hh + 1) * 128:(2 * hh + 1) * 128 + 128], Vo[pp][:, h, :], start=False, stop=True)
            fw.copy(act, ozs[pp][:], boz[:, 0:390])
            fw.dma(sp, oz_scr[b_][q0:q0 + 127 * dil + 1:dil, :], ozs[pp][:])

        for t_ in range(len(srcs)):
            att1(t_)
            if t_ > 0:
                att2(t_ - 1)
        if srcs:
            att2(len(srcs) - 1)
        ozl = [sb("ozl%d" % i, [128, 6, 65], F32) for i in range(3)]
        osum = sb("osum", [128, 6, 65], F32)
        rz = sb("rz", [128, 6, 1], F32)
        on = sb("on", [128, 6, 64], F32)
        onb = sb("onb", [128, DA], BF16)
        junkb = sb("junkb", [128, DA], BF16)
        ssa = sb("ssa", [128, 2], F32)
        for mt in range(SEG // 128 if VAR == '' else 0):
            for b_ in range(3):
                fw.dma(sp, ozl[b_].re("p h e -> p (h e)"), oz_scr[b_][128 * mt:128 * mt + 128, :])
            fw.tt(dve, osum[:], ozl[0][:], ozl[1][:], ALU.add)
            fw.tt(dve, osum[:], osum[:], ozl[2][:], ALU.add)
            fw.recip(rz[:], osum[:, :, 64:65])
            fw.tt(dve, on[:], osum[:, :, 0:64], V(rz.h[:].to_broadcast([128, 6, 64]), rz.buf), ALU.mult)
            onf = on.re("p h e -> p (h e)")
            fw.stt(junkb[:], onf, 1.0, onf, ALU.mult, ALU.mult, accum_out=ssa[:, 0:1])
            fw.ts(dve, ssa[:, 1:2], ssa[:, 0:1], 1.0 / DA, EPS, ALU.mult, ALU.add)
            fw.activation(ssa[:, 1:2], ssa[:, 1:2], AF.Ln)
            fw.activation(ssa[:, 1:2], ssa[:, 1:2], AF.Exp, scale=-0.5)
            fw.ts(dve, onb[:], onf, ssa[:, 1:2], None, ALU.mult)
            bt_ = pb()
            pTa = bt_.bitcast(BF16)
            for c in range(3):
                fw.transpose(pTa[:, 128 * c:128 * c + 128], onb[:, 128 * c:128 * c + 128], ident_b[:])
            for c in range(3):
                fw.ts(dve, outaT[:, c, 128 * mt:128 * mt + 128], pTa[:, 128 * c:128 * c + 128], ag[:, c:c + 1], None, ALU.mult)
        if do_sample:
            h_scr = dscr("h_scr", [NS * NHB * 2, 80], F32)
            qp = sb("qp", [128, EB], F32)
            kp = sb("kp", [128, EB], F32)
            vp = sb("vp", [128, 80], F32)
            scp = sb("scp", [128, 4], F32)
            n_p = sb("n_p", [128, EB], F32)
            for half in range(2):
                fw.dma(sp, qp[half:128:2, :], s_q[:, :])
                fw.dma(sp, kp[half:128:2, :], s_k[:, :])
                fw.dma(sp, vp[half:128:2, :], s_v[:, 80 * half:80 * half + 80])
                fw.dma(sp, scp[half:128:2, :], s_sc[:, :])
                fw.dma(sp, n_p[half:128:2, :], V(s_n_d.h.ap().rearrange("b h e -> (b h) e"), s_n_d.buf))
            vw = sb("vw", [128, 80], F32)
            fw.ts(dve, vw[:], vp[:], scp[:, 1:2], None, ALU.mult)
            kwp = sb("kwp", [128, EB], F32)
            fw.ts(dve, kwp[:], kp[:], scp[:, 1:2], None, ALU.mult)
            fw.stt(n_p[:], n_p[:], scp[:, 0:1], kwp[:], ALU.mult, ALU.add)
            fw.dma(sp, V(o_sn.h.ap().rearrange("b (h e) -> (b h) e", h=NHB), o_sn.buf), n_p[0:128:2, :])
            den = sb("den", [128, 4], F32)
            fw.stt(kwp[:], n_p[:], 1.0, qp[:], ALU.mult, ALU.mult, accum_out=den[:, 0:1])
            fw.stt(den[:, 1:2], den[:, 0:1], -1.0, den[:, 0:1], ALU.mult, ALU.max)
            fw.ts(dve, den[:, 1:2], den[:, 1:2], scp[:, 2:3], None, ALU.max)
            fw.recip(den[:, 2:3], den[:, 1:2])
            CH = 10
            Cc = [sb("Cc%d" % i, [128, CH, EB], F32) for i in range(2)]
            Oc = [sb("Oc%d" % i, [128, CH, EB], F32) for i in range(2)]
            hnum = sb("hnum", [128, 80], F32)
            sCv = V(s_C_d.h.ap().rearrange("b h (t v) k -> (b h t) (v k)", t=2), s_C_d.buf)
            oCv = V(o_sC.h.ap().rearrange("b h (t v) k -> (b h t) (v k)", t=2), o_sC.buf)
            for ci in range(80 // CH):
                pq = ci % 2
                cc, oc = Cc[pq], Oc[pq]
                fw.dma(sp, cc.re("p v k -> p (v k)"), V(sCv.ap[:, CH * EB * ci:CH * EB * (ci + 1)], sCv.buf))
                fw.tt(dve, oc[:], V(vw.h[:, CH * ci:CH * ci + CH].unsqueeze(2).to_broadcast([128, CH, EB]), vw.buf),
                      V(kp.h[:].unsqueeze(1).to_broadcast([128, CH, EB]), kp.buf), ALU.mult)
                fw.stt(cc.re("p v k -> p (v k)"), cc.re("p v k -> p (v k)"), scp[:, 0:1], oc.re("p v k -> p (v k)"), ALU.mult, ALU.add)
                fw.dma(sp, V(oCv.ap[:, CH * EB * ci:CH * EB * (ci + 1)], oCv.buf), cc.re("p v k -> p (v k)"))
                fw.tt(dve, oc[:], cc[:], V(qp.h[:].unsqueeze(1).to_broadcast([128, CH, EB]), qp.buf), ALU.mult)
                fw.reduce(hnum[:, CH * ci:CH * ci + CH], oc[:], ALU.add)
            fw.ts(dve, hnum[:], hnum[:], den[:, 2:3], None, ALU.mult)
            fw.dma(sp, h_scr[:, :], hnum[:])
            hs = sb("hs", [NS, NHB, EB], F32)
            hs2 = sb("hs2", [NS, NHB, EB], F32)
            fw.dma(sp, hs.re("p h e -> p (h e)"), V(h_scr.h.ap().rearrange("(b x) e -> b (x e)", b=NS), h_scr.buf))
            hss = sb("hss", [NS, 2, NHB], F32)
            fw.tt(dve, hs2[:], hs[:], hs[:], ALU.mult)
            fw.reduce(hss[:, 0, :], hs2[:], ALU.add)
            fw.ts(dve, hss[:, 1, :], hss[:, 0, :], 1.0 / EB, EPS, ALU.mult, ALU.add)
            fw.activation(hss[:, 1, :], hss[:, 1, :], AF.Ln)
            fw.activation(hss[:, 1, :], hss[:, 1, :], AF.Exp, scale=-0.5)
            fw.tt(dve, hs2[:], hs[:], V(hss.h[:, 1, :].unsqueeze(2).to_broadcast([NS, NHB, EB]), hss.buf), ALU.mult)
            mgr = sb("mgr", [NS, 2, DB], F32)
            fw.dma(sp, mgr[:, 0, :], V(mhg_d.h.ap().rearrange("d o -> (d o)").partition_broadcast(NS), mhg_d.buf))
            fw.dma(sp, mgr[:, 1, :], V(skip_d.h.ap().rearrange("d o -> (d o)").partition_broadcast(NS), skip_d.buf))
            smi = sb("smi", [NS, 2 * DB + 3 * DA], F32)
            fw.dma(sp, smi[:], s_misc[:, :])
            ob_s = sb("ob_s", [NS, DB], F32)
            hs2f = hs2.re("p h e -> p (h e)")
            fw.tt(dve, ob_s[:], hs2f, mgr[:, 0, :], ALU.mult)
            fw.tt(dve, hs2f, smi[:, DB:2 * DB], mgr[:, 1, :], ALU.mult)
            fw.tt(dve, ob_s[:], ob_s[:], hs2f, ALU.add)
            ob_b = sb("ob_b", [NS, DB], BF16)
            fw.tt(dve, ob_b[:], ob_s[:], smi[:, 0:DB], ALU.mult)
            bT3 = pb()
            pT3 = bT3.bitcast(BF16)
            for c in range(5):
                fw.transpose(pT3[:, 128 * c:128 * c + NS], ob_b[:, 128 * c:128 * c + 128], ident_b[0:NS, 0:NS], inc=(c == 4))
            for c in range(5):
                fw.copy(act, outbT[:, c, SEG:SEG + NS], pT3[:, 128 * c:128 * c + NS])

            ohs = sb("ohs_s", [32, 3, 128], F32)
            e16 = sb("e16_s", [NS, NS * 128], F32)
            ecol = sb("ecol_s", [128, NS * NS], F32)
            fw.dma(sp, ohs[:], V(ohs_d.h.ap().rearrange("b k i -> k b i"), ohs_d.buf))
            fw.dma(sp, e16[:], e16_d[:, :])
            fw.dma(sp, ecol[:], ecol_d[:, :])
            bS = sb("bS", [128, 3, 6], F32)
            for b_ in range(3):
                bk_ = pb()
                fw.mm(bk_[:, 0:6], ohs[:, b_, :], rb[:], start=True, stop=True)
                fw.copy(dve, bS[:, b_, :], bk_[:, 0:6])
            qa = sb("qa", [NS, DA], F32)
            ka = sb("ka", [NS, DA], F32)
            va = sb("va_s", [NS, DA], F32)
            fw.copy(dve, qa[:], smi[:, 2 * DB:2 * DB + DA])
            fw.copy(dve, ka[:], smi[:, 2 * DB + DA:2 * DB + 2 * DA])
            fw.copy(dve, va[:], smi[:, 2 * DB + 2 * DA:2 * DB + 3 * DA])
            Kg = [sb("Kg%d" % i, [128, 6, 64], F32) for i in range(2)]
            Vg = [sb("Vg%d" % i, [128, 6, 64], F32) for i in range(2)]
            prd = sb("prd", [128, 6, 64], F32)
            pvz = [sb("pvz%d" % i, [128, DA + 6], F32) for i in range(2)]
            sS = sb("sS", [128, 6], F32)
            bacc = pb()
            reserved.append(bacc)
            nn = 0
            for b in range(NS):
                bq_ = pb()
                fw.mm(bq_[:, 0:DA], e16[:, 128 * b:128 * b + 128], qa[:], start=True, stop=True)
                for b_, dil in enumerate((1, 4, 16)):
                    pq = nn % 2
                    fw.dma(sp, Kg[pq].re("p h e -> p (h e)"), cwk_d[b, 2048 - 128 * dil:2048:dil, :])
                    fw.dma(sp, Vg[pq].re("p h e -> p (h e)"), cwv_d[b, 2048 - 128 * dil:2048:dil, :])
                    fw.tt(dve, prd.re("p h e -> p (h e)"), Kg[pq].re("p h e -> p (h e)"), bq_[:, 0:DA], ALU.mult)
                    fw.reduce(sS[:], prd[:], ALU.add)
                    fw.tt(dve, sS[:], sS[:], bS[:, b_, :], ALU.add)
                    pz = pvz[pq]
                    fw.activation(pz[:, DA:DA + 6], sS[:], AF.Exp)
                    fw.tt(dve, V(pz.h[:, 0:DA].rearrange("p (h e) -> p h e", h=6), pz.buf), Vg[pq][:],
                          V(pz.h[:, DA:DA + 6].unsqueeze(2).to_broadcast([128, 6, 64]), pz.buf), ALU.mult)
                    fw.mm(bacc[0:NS, 0:DA + 6], ecol[:, NS * b:NS * b + NS], pz[:], start=(nn == 0), stop=(nn == 3 * NS - 1), inc=True)
                    nn += 1
            reserved.remove(bacc)
            b0r = sb("b0r", [NS, 6], F32)
            fw.dma(sp, b0r[:], V(rel_bias_d.h.ap()[0:1, :].rearrange("o h -> (o h)").partition_broadcast(NS), rel_bias_d.buf))
            qk = sb("qk", [NS, 6, 64], F32)
            s0 = sb("s0", [NS, 6], F32)
            fw.tt(dve, qk.re("p h e -> p (h e)"), qa[:], ka[:], ALU.mult)
            fw.reduce(s0[:], qk[:], ALU.add)
            fw.tt(dve, s0[:], s0[:], b0r[:], ALU.add)
            fw.activation(s0[:], s0[:], AF.Exp)
            fw.ts(dve, s0[:], s0[:], 3.0, None, ALU.mult)
            oacc = sb("oacc", [NS, DA + 6], F32)
            fw.copy(dve, oacc[:], bacc[0:NS, 0:DA + 6])
            fw.tt(dve, qk[:], V(va.h[:].rearrange("p (h e) -> p h e", h=6), va.buf),
                  V(s0.h[:].unsqueeze(2).to_broadcast([NS, 6, 64]), s0.buf), ALU.mult)
            fw.tt(dve, oacc[:, 0:DA], oacc[:, 0:DA], qk.re("p h e -> p (h e)"), ALU.add)
            fw.tt(dve, oacc[:, DA:DA + 6], oacc[:, DA:DA + 6], s0[:], ALU.add)
            rzs = sb("rzs", [NS, 6], F32)
            fw.recip(rzs[:], oacc[:, DA:DA + 6])
            fw.tt(dve, qk[:], V(oacc.h[:, 0:DA].rearrange("p (h e) -> p h e", h=6), oacc.buf),
                  V(rzs.h[:].unsqueeze(2).to_broadcast([NS, 6, 64]), rzs.buf), ALU.mult)
            qkf = qk.re("p h e -> p (h e)")
            sa2 = sb("sa2", [NS, 2], F32)
            jks = sb("jks", [NS, DA], F32)
            fw.stt(jks[:], qkf, 1.0, qkf, ALU.mult, ALU.mult, accum_out=sa2[:, 0:1])
            fw.ts(dve, sa2[:, 1:2], sa2[:, 0:1], 1.0 / DA, EPS, ALU.mult, ALU.add)
            fw.activation(sa2[:, 1:2], sa2[:, 1:2], AF.Ln)
            fw.activation(sa2[:, 1:2], sa2[:, 1:2], AF.Exp, scale=-0.5)
            oab = sb("oab", [NS, DA], BF16)
            fw.ts(dve, oab[:], qkf, sa2[:, 1:2], None, ALU.mult)
            bT4 = pb()
            pT4 = bT4.bitcast(BF16)
            for c in range(3):
                fw.transpose(pT4[:, 128 * c:128 * c + NS], oab[:, 128 * c:128 * c + 128], ident_b[0:NS, 0:NS], inc=(c == 2))
            for c in range(3):
                fw.ts(dve, outaT[:, c, SEG:SEG + NS], pT4[:, 128 * c:128 * c + NS], ag[:, c:c + 1], None, ALU.mult)

    else:
        fw.memset(dve, outaT[:], 0.0)

    pop()
    pop()
    push()
    x1_scr = dscr("x1_scr", [SEG + 128, D], F32)
    h2_scr = dscr("h2_scr", [SEG // 128 + 1, 128, D], BF16)
    w_out_b = sb("w_out_b", [128, 8, D], BF16)
    wst1 = [sb("wst1_%d" % i, [128, D], F32) for i in range(2)]
    for c in range(8):
        st = wst1[c % 2]
        fw.dma(sp, st[:], w_out_d[128 * c:128 * c + 128, :])
        fw.copy(act if c % 2 else dve, w_out_b[:, c, :], st[:])
    xt1 = [sb("xt1_%d" % i, [128, D], F32) for i in range(2)]
    x1t = [sb("x1t_%d" % i, [128, D], F32) for i in range(2)]
    xn1 = sb("xn1", [128, D], BF16)
    junk1 = sb("junk1", [128, D], BF16)
    h2t = [sb("h2t_%d" % i, [128, D], BF16) for i in range(2)]
    ss1 = sb("ss1", [128, 2], F32)
    NTC = SEG // 128 + (1 if do_sample else 0)
    for mt in range(NTC):
        p = mt % 2
        nr = 128 if mt < SEG // 128 else NS
        if mt < SEG // 128:
            fw.dma(sp, xt1[p][:], xw[128 * (MAIN0 + mt):128 * (MAIN0 + mt) + 128, :])
        else:
            fw.dma(sp, xt1[p][0:nr, :], xs_d[:, :])
        b0, b1_ = pb(), pb()
        for half, bk_ in enumerate((b0, b1_)):
            for c in range(8):
                src_ = outaT[:, c, 128 * mt:128 * mt + nr] if c < 3 else outbT[:, c - 3, 128 * mt:128 * mt + nr]
                fw.mm(bk_[0:nr, 0:512], src_, w_out_b[:, c, 512 * half:512 * half + 512], start=(c == 0), stop=(c == 7))
        fw.tt(dve, x1t[p][0:nr, 0:512], b0[0:nr, 0:512], xt1[p][0:nr, 0:512], ALU.add)
        fw.tt(dve, x1t[p][0:nr, 512:1024], b1_[0:nr, 0:512], xt1[p][0:nr, 512:1024], ALU.add)
        fw.dma(sp, x1_scr[128 * mt:128 * mt + nr, :], x1t[p][0:nr, :])
        fw.stt(junk1[0:nr, :], x1t[p][0:nr, :], 1.0, x1t[p][0:nr, :], ALU.mult, ALU.mult, accum_out=ss1[0:nr, 0:1])
        fw.ts(dve, ss1[0:nr, 1:2], ss1[0:nr, 0:1], 1.0 / D, EPS, ALU.mult, ALU.add)
        fw.activation(ss1[0:nr, 1:2], ss1[0:nr, 1:2], AF.Ln)
        fw.activation(ss1[0:nr, 1:2], ss1[0:nr, 1:2], AF.Exp, scale=-0.5)
        fw.ts(dve, xn1[0:nr, :], x1t[p][0:nr, :], ss1[0:nr, 1:2], None, ALU.mult)
        bT_ = pb()
        pT_ = bT_.bitcast(BF16)
        for c in range(8):
            fw.transpose(pT_[:, 128 * c:128 * c + nr], xn1[0:nr, 128 * c:128 * c + 128], ident_b[0:nr, 0:nr], inc=(c == 7))
        fw.copy(act, h2t[p][:], pT_[:, 0:1024])
        fw.dma(sp, h2_scr[mt], h2t[p][:])

    pop()
    pop()
    push()
    w1b = sb("w1b", [128, 8, DFF], BF16)
    w2b = sb("w2b", [128, 32, D], BF16)
    g2 = sb("g2", [128, 8], F32)
    fing = sb("fing", [128, D], F32)
    for c in range(8):
        fw.dma(sp, g2[:, c:c + 1], norm2_g_d[128 * c:128 * c + 128, :])
    fw.dma(sp, fing[:], V(final_g_d.h.ap().to_broadcast([128, D]), final_g_d.buf))
    wst2 = [sb("wst2_%d" % i, [128, 2048], F32) for i in range(2)]
    n_ = 0
    for c in range(8):
        for hf in range(2):
            st = wst2[n_ % 2]
            fw.dma(sp, st[:], w_ff1_d[128 * c:128 * c + 128, 2048 * hf:2048 * hf + 2048])
            fw.scale_copy(act if n_ % 2 else dve, w1b[:, c, 2048 * hf:2048 * hf + 2048], st[:], g2[:, c:c + 1])
            n_ += 1
    for c in range(16):
        st = wst2[n_ % 2]
        fw.dma(sp, st.re("p (a b) -> p a b", a=2), V(w_ff2_d.h.ap()[256 * c:256 * c + 256, :].rearrange("(a p) d -> p a d", a=2), w_ff2_d.buf))
        fw.copy(act if n_ % 2 else dve, w2b[:, 2 * c:2 * c + 2, :], st.re("p (a b) -> p a b", a=2))
        n_ += 1
    GT_ = 256
    h2g = [sb("h2g_%d" % i, [128, 8, GT_], BF16) for i in range(2)]
    aT = sb("aT", [128, 32, GT_], BF16)
    fw.memset(dve, h2g[0][:], 0.0)
    fw.memset(dve, h2g[1][:], 0.0)
    rl = [sb("rl_%d" % i, [128, 2, GT_], BF16) for i in range(2)]
    x1l = [sb("x1l_%d" % i, [128, D], F32) for i in range(2)]
    x2t = [sb("x2t_%d" % i, [128, D], F32) for i in range(2)]
    yt = [sb("yt_%d" % i, [128, D], F32) for i in range(2)]
    junk2 = sb("junk2", [128, D], BF16)
    ss2 = sb("ss2", [128, 2], F32)
    groups = [[g_ * (GT_ // 128) + t_ for t_ in range(GT_ // 128)] for g_ in range(SEG // GT_)]
    if do_sample:
        groups.append([SEG // 128])
    for g_, tiles_ in enumerate(groups):
        p = g_ % 2
        for t_, mt in enumerate(tiles_):
            fw.dma(sp, h2g[p][:, :, 128 * t_:128 * t_ + 128], V(h2_scr.h.ap()[mt].rearrange("p (c t) -> p c t", c=8), h2_scr.buf))
        for j2 in range(16):
            bk_ = pb()
            for jj in range(2):
                j = 2 * j2 + jj
                for c in range(8):
                    fw.mm(bk_[:, GT_ * jj:GT_ * jj + GT_], w1b[:, c, 128 * j:128 * j + 128], h2g[p][:, c, :], start=(c == 0), stop=(c == 7))
            r_ = rl[j2 % 2]
            fw.activation(r_.re("p a t -> p (a t)"), bk_[:, 0:2 * GT_], AF.Relu)
            fw.tt(dve, aT[:, 2 * j2:2 * j2 + 2, :], r_[:], r_[:], ALU.mult)
        for t_, mt in enumerate(tiles_):
            q = mt % 2
            nr = 128 if mt < SEG // 128 else NS
            fw.dma(sp, x1l[q][0:nr, :], x1_scr[128 * mt:128 * mt + nr, :])
            b0, b1_ = pb(), pb()
            for half, bk_ in enumerate((b0, b1_)):
                for c in range(32):
                    fw.mm(bk_[0:nr, 0:512], aT[:, c, 128 * t_:128 * t_ + nr], w2b[:, c, 512 * half:512 * half + 512], start=(c == 0), stop=(c == 31))
            fw.tt(dve, x2t[q][0:nr, 0:512], b0[0:nr, 0:512], x1l[q][0:nr, 0:512], ALU.add)
            fw.tt(dve, x2t[q][0:nr, 512:1024], b1_[0:nr, 0:512], x1l[q][0:nr, 512:1024], ALU.add)
            fw.stt(junk2[0:nr, :], x2t[q][0:nr, :], 1.0, x2t[q][0:nr, :], ALU.mult, ALU.mult, accum_out=ss2[0:nr, 0:1])
            fw.ts(dve, ss2[0:nr, 1:2], ss2[0:nr, 0:1], 1.0 / D, EPS, ALU.mult, ALU.add)
            fw.activation(ss2[0:nr, 1:2], ss2[0:nr, 1:2], AF.Ln)
            fw.activation(ss2[0:nr, 1:2], ss2[0:nr, 1:2], AF.Exp, scale=-0.5)
            fw.stt(yt[q][0:nr, :], x2t[q][0:nr, :], ss2[0:nr, 1:2], fing[0:nr, :], ALU.mult, ALU.mult)
            if mt < SEG // 128:
                fw.dma(sp, o_y[128 * mt:128 * mt + 128, :], yt[q][:])
            else:
                fw.dma(sp, o_ys[:, :], yt[q][0:nr, :])
    pop()
    pop()
    fw.finish(sp)
    fw.emit()
    return nc


def _get_nc():
    if "nc" not in _NC_CACHE:
        _NC_CACHE["nc"] = build_nc()
    return _NC_CACHE["nc"]


def _t5_bucket_np(dist):
    dist = np.asarray(dist, np.int64)
    df = np.maximum(dist, 1).astype(np.float32)
    large = 16 + (np.log(df / np.float32(16)) / np.float32(np.log(2048 / 16)) * np.float32(16)).astype(np.int32)
    large = np.minimum(large, 31)
    return np.where(dist < 16, dist, large)


def _consts():
    c = {}
    k = np.arange(128)[:, None]
    q = np.arange(128)[None, :]
    c["negmask"] = np.where(k > q, np.float32(MASKV), np.float32(0.0)).astype(np.float32)
    sel = np.zeros((4, 4, 128), np.float32)
    for h in range(4):
        sel[h, h, :] = 1.0
    c["sel"] = sel.reshape(4, 512)
    ohb = np.zeros((3, 32, 512), np.float32)
    mv = np.full((3, 1, 512), np.float32(MASKV), np.float32)
    for bi, dil in enumerate((1, 4, 16)):
        for j in range(0, 129):
            ohb[bi, int(_t5_bucket_np(j * dil)), j + 127] = 1.0
            mv[bi, 0, j + 127] = 0.0
    c["ohb"] = ohb
    c["mvec"] = mv
    ohs = np.zeros((3, 32, 128), np.float32)
    for bi, dil in enumerate((1, 4, 16)):
        for i in range(128):
            ohs[bi, int(_t5_bucket_np((128 - i) * dil)), i] = 1.0
    c["ohs"] = ohs
    e16 = np.zeros((NS, NS, 128), np.float32)
    ecol = np.zeros((128, NS, NS), np.float32)
    for b in range(NS):
        e16[b, b, :] = 1.0
        ecol[:, b, b] = 1.0
    c["e16"] = e16.reshape(NS, NS * 128)
    c["ecol"] = ecol.reshape(128, NS * NS)
    return c


def make_in_maps(x_prompt, x_sample, cache_win_k, cache_win_v, state_conv, state_C, state_n, state_m,
                 rel_bias, norm1_g, w_in, gate_bias, conv_w, conv_b, wq_head, wk_head,
                 attn_out_g, mh_norm_g, skip, w_out, norm2_g, w_ff1, w_ff2, final_g):
    f = lambda a: np.ascontiguousarray(np.asarray(a, dtype=np.float32))
    x_prompt = f(x_prompt)
    cst = _consts()
    shared = {
        "w_in": f(w_in[0]), "norm1_g": f(norm1_g[0]).reshape(D, 1), "gate_bias": f(gate_bias[0]).reshape(8, 1),
        "conv_wT": f(np.asarray(conv_w[0]).T), "conv_w": f(conv_w[0]), "conv_b": f(conv_b[0]).reshape(DB, 1),
        "wq": f(wq_head[0]), "wk": f(wk_head[0]), "mhg": f(mh_norm_g[0]).reshape(DB, 1),
        "skip": f(skip[0]).reshape(DB, 1),
        "rel_bias": f(rel_bias), "attn_g": f(attn_out_g[0]).reshape(DA, 1), "w_out": f(w_out[0]),
        "norm2_g": f(norm2_g[0]).reshape(D, 1), "w_ff1": f(w_ff1[0]), "w_ff2": f(w_ff2[0]),
        "final_g": f(final_g).reshape(1, D),
    }
    shared.update(cst)
    in_maps = []
    for c in range(NCORES):
        b, s = c // 4, c % 4
        lo = SEG * s - (WIN - SEG)
        xwin = np.zeros((WIN, D), np.float32)
        a0 = max(lo, 0)
        xwin[a0 - lo:] = x_prompt[b, a0:lo + WIN]
        valid = np.array([(lo + 128 * t) >= 0 for t in range(NT)])
        m = dict(shared)
        m["xw"] = xwin
        m["tmA"] = np.tile(np.where(valid, 0.0, NEG).astype(np.float32)[None, :], (4, 1))
        m["tmV"] = np.tile(np.where(valid, -1.0, 0.0).astype(np.float32)[None, :], (4, 1))
        m["tmO"] = np.tile(np.where(valid, 1.0, 0.0).astype(np.float32)[None, :], (128, 1))
        sl = slice(NS * c, NS * c + NS)
        m["xs"] = f(x_sample[sl, 0])
        m["cwk"] = f(cache_win_k[0, sl]).reshape(NS, 2048, DA)
        m["cwv"] = f(cache_win_v[0, sl]).reshape(NS, 2048, DA)
        m["s_conv"] = f(state_conv[0, sl])
        m["s_C"] = f(state_C[0, sl])
        m["s_n"] = f(state_n[0, sl])
        m["s_m"] = f(state_m[0, sl])
        in_maps.append(m)
    return in_maps


def _filter(nc_inputs, m):
    return {k: v for k, v in m.items() if k in nc_inputs}


def kernel(**inputs):
    nc = _get_nc()
    in_maps = make_in_maps(**inputs)
    names = _NC_CACHE["in_names"]
    in_maps = [_filter(names, m) for m in in_maps]
    res = run_bass_kernel_spmd(nc, in_maps, core_ids=list(range(NCORES)))
    R = res.results
    return assemble(R)


def assemble(R):
    f = np.float32
    y_prompt = np.stack([np.concatenate([R[4 * b + s]["o_y"] for s in range(4)], 0) for b in range(2)]).astype(f)
    y_sample = np.concatenate([R[c]["o_ys"] for c in range(NCORES)], 0).reshape(128, 1, D).astype(f)
    last = [3, 7]
    p_k = np.stack([R[c]["o_wk"] for c in last]).reshape(1, 2, 2048, 6, 64).astype(f)
    p_v = np.stack([R[c]["o_wv"] for c in last]).reshape(1, 2, 2048, 6, 64).astype(f)
    p_conv = np.stack([R[c]["o_conv"] for c in last]).reshape(1, 2, 3, DB).astype(f)
    p_C = np.stack([R[c]["o_C"] for c in last]).reshape(1, 2, NHB, EB, EB).astype(f)
    p_n = np.stack([R[c]["o_n"] for c in last]).reshape(1, 2, NHB, EB).astype(f)
    p_m = np.stack([R[c]["o_m"] for c in last]).reshape(1, 2, NHB).astype(f)
    cat = lambda k: np.concatenate([R[c][k] for c in range(NCORES)], 0)
    s_k = cat("o_swk").reshape(1, 128, 2048, 6, 64).astype(f)
    s_v = cat("o_swv").reshape(1, 128, 2048, 6, 64).astype(f)
    s_conv = cat("o_sconv").reshape(1, 128, 3, DB).astype(f)
    s_C = cat("o_sC").reshape(1, 128, NHB, EB, EB).astype(f)
    s_n = cat("o_sn").reshape(1, 128, NHB, EB).astype(f)
    s_m = cat("o_sm").reshape(1, 128, NHB).astype(f)
    return (y_prompt, y_sample, p_k, p_v, p_conv, p_C, p_n, p_m, s_k, s_v, s_conv, s_C, s_n, s_m)
```

```python
import numpy as np
import os
from contextlib import ExitStack
VAR = ''
import concourse.bass as bass
import concourse.mybir as mybir
from concourse.bass_utils import run_bass_kernel_spmd

F32 = mybir.dt.float32
BF16 = mybir.dt.bfloat16
AF = mybir.ActivationFunctionType
ALU = mybir.AluOpType
AX = mybir.AxisListType

D = 1024
DIN = 2440
DA = 384
DB = 640
NHB = 4
EB = 160
DFF = 4096
NCORES = 8
SEG = 2048
WIN = 8192
NT = WIN // 128
MAIN0 = 48
HALO0 = 32
NS = 16
EPS = 1e-6
NEG = -1e30
MASKV = -30000.0


class Eng:
    def __init__(self, fw, name, handle, sem):
        self.fw, self.name, self.h, self.sem = fw, name, handle, sem
        self.count = 0
        self.prog = []
        self.waited = {}
        self.dsems = []
        self.dtot = []
        self.dnext = 0

    def wait(self, sem, val):
        if val <= 0:
            return
        k = id(sem)
        if self.waited.get(k, 0) >= val:
            return
        self.waited[k] = val
        self.prog.append(lambda h, sem=sem, val=val: h.wait_ge(sem, val))


class Buf:
    def __init__(self, name):
        self.name = name
        self.w = {}
        self.r = {}
        self.excl = False


class V:
    def __init__(self, ap, buf):
        self.ap = ap
        self.bufs = list(buf) if isinstance(buf, (list, tuple)) else [buf]

    @property
    def buf(self):
        return self.bufs if len(self.bufs) > 1 else self.bufs[0]


class T:
    def __init__(self, handle, name):
        self.h = handle
        self._buf = Buf(name)
        self.subs = None

    @property
    def buf(self):
        return self.subs if self.subs else self._buf

    def split(self, n):
        self.subs = [Buf("%s.%d" % (self._buf.name, k)) for k in range(n)]
        return self

    def sub(self, k):
        t = T.__new__(T)
        t.h = self.h
        t._buf = self.subs[k]
        t.subs = None
        return t

    def __getitem__(self, key):
        return V(self.h[key], self.buf)

    def ap(self, offset, pat):
        return V(bass.AP(self.h, offset, pat), self.buf)

    def re(self, pat, **kw):
        return V(self.h.ap().rearrange(pat, **kw) if hasattr(self.h, "ap") else self.h[:].rearrange(pat, **kw), self.buf)

    def bitcast(self, dt):
        t = T.__new__(T)
        t.h = self.h.bitcast(dt)
        t._buf = self._buf
        t.subs = self.subs
        return t


class FW:
    def __init__(self, nc):
        self.nc = nc
        mk = lambda n, h: Eng(self, n, h, nc.alloc_semaphore("sem_" + n))
        self.pe = mk("pe", nc.tensor)
        self.act = mk("act", nc.scalar)
        self.dve = mk("dve", nc.vector)
        self.pool = mk("pool", nc.gpsimd)
        self.sp = mk("sp", nc.sync)
        self.engs = [self.pe, self.act, self.dve, self.pool, self.sp]
        for e in (self.sp, self.pool, self.act):
            n = 12
            e.dsems = [nc.alloc_semaphore("dsem_%s_%d" % (e.name, i)) for i in range(n)]
            e.dtot = [0] * n
        self.same_engine_sync = True

    def _deps(self, eng, reads, writes):
        for v in reads:
            for b in v.bufs:
                for sem, val in b.w.values():
                    self._w(eng, sem, val, True)
                if b.excl:
                    for sem, val in b.r.values():
                        self._w(eng, sem, val, False)
        for v in writes:
            for b in v.bufs:
                for sem, val in list(b.w.values()) + list(b.r.values()):
                    self._w(eng, sem, val, False)

    def _w(self, eng, sem, val, raw):
        if sem is eng.sem:
            if eng is self.pe or not self.same_engine_sync or not raw:
                return
        eng.wait(sem, val)

    def _mark(self, sem, val, reads, writes):
        for v in reads:
            for b in v.bufs:
                b.r[id(sem)] = (sem, val)
        for v in writes:
            for b in v.bufs:
                b.w[id(sem)] = (sem, val)

    def op(self, eng, fn, reads, writes, inc=True):
        reads = [v for v in reads if isinstance(v, V)]
        self._deps(eng, reads, writes)
        if inc:
            eng.count += 1
            c = eng.count
            eng.prog.append(lambda h, fn=fn, s=eng.sem: fn(h).then_inc(s, 1))
            self._mark(eng.sem, c, reads, writes)
        else:
            eng.prog.append(lambda h, fn=fn: fn(h))

    def dma(self, q, out, in_, **kw):
        self._deps(q, [in_], [out])
        j = q.dnext
        q.dnext = (j + 1) % len(q.dsems)
        sem = q.dsems[j]
        q.wait(sem, q.dtot[j])
        q.dtot[j] += 16
        tot = q.dtot[j]
        q.prog.append(lambda h, o=out.ap, i=in_.ap, s=sem, kw=kw: h.dma_start(out=o, in_=i, **kw).then_inc(s, 16))
        self._mark(sem, tot, [in_], [out])

    def finish(self, eng, skip=()):
        for e in self.engs:
            if e is not eng:
                eng.wait(e.sem, e.count)
            if e in skip:
                continue
            for s, t in zip(e.dsems, e.dtot):
                eng.wait(s, t)

    def barrier(self):
        for e in self.engs:
            self.finish(e, skip=(self.pool,))

    def emit(self):
        nc = self.nc
        with nc.Block() as block:
            @block.tensor
            def _(h):
                for f in self.pe.prog:
                    f(h)

            @block.scalar
            def _(h):
                for f in self.act.prog:
                    f(h)

            @block.vector
            def _(h):
                for f in self.dve.prog:
                    f(h)

            @block.gpsimd
            def _(h):
                for f in self.pool.prog:
                    f(h)

            @block.sync
            def _(h):
                for f in self.sp.prog:
                    f(h)

    def mm(self, out, lhsT, rhs, start=True, stop=True, inc=None):
        if inc is None:
            inc = stop
        self.op(self.pe, lambda h, o=out.ap, l=lhsT.ap, r=rhs.ap: h.matmul(o, l, r, start=start, stop=stop, skip_group_check=True),
                [lhsT, rhs], [out], inc=inc)

    def transpose(self, out, in_, ident, inc=True):
        self.op(self.pe, lambda h, o=out.ap, i=in_.ap, d=ident.ap: h.transpose(o, i, d), [in_, ident], [out], inc=inc)

    def activation(self, out, in_, func, bias=0.0, scale=1.0, accum_out=None, eng=None):
        rd = [in_, bias, scale]
        wr = [out] + ([accum_out] if accum_out is not None else [])
        b = bias.ap if isinstance(bias, V) else bias
        s = scale.ap if isinstance(scale, V) else scale
        kw = {}
        if accum_out is not None:
            kw["accum_out"] = accum_out.ap
        self.op(self.act, lambda h, o=out.ap, i=in_.ap: h.activation(o, i, func, bias=b, scale=s, **kw), rd, wr)

    def tt(self, eng, out, in0, in1, op):
        self.op(eng, lambda h, o=out.ap, a=in0.ap, b=in1.ap: h.tensor_tensor(o, a, b, op), [in0, in1], [out])

    def ts(self, eng, out, in0, s1, s2, op0, op1=None, accum_out=None):
        a1 = s1.ap if isinstance(s1, V) else s1
        a2 = s2.ap if isinstance(s2, V) else s2
        kw = {}
        if op1 is not None:
            kw["op1"] = op1
        wr = [out]
        if accum_out is not None:
            kw["accum_out"] = accum_out.ap
            wr.append(accum_out)
        self.op(eng, lambda h, o=out.ap, a=in0.ap: h.tensor_scalar(o, a, a1, a2, op0, **kw), [in0, s1, s2], wr)

    def stt(self, out, in0, scalar, in1, op0, op1, accum_out=None):
        sc = scalar.ap if isinstance(scalar, V) else scalar
        kw = {}
        wr = [out]
        if accum_out is not None:
            kw["accum_out"] = accum_out.ap
            wr.append(accum_out)
        self.op(self.dve, lambda h, o=out.ap, a=in0.ap, b=in1.ap: h.scalar_tensor_tensor(o, a, sc, b, op0, op1, **kw),
                [in0, scalar, in1], wr)

    def scale_copy(self, eng, out, in_, sc):
        if eng is self.act:
            self.activation(out, in_, AF.Copy, scale=sc)
        else:
            self.ts(eng, out, in_, sc, None, ALU.mult)

    def copy(self, eng, out, in_):
        if eng is self.act:
            self.op(eng, lambda h, o=out.ap, i=in_.ap: h.copy(o, i), [in_], [out])
        else:
            self.op(eng, lambda h, o=out.ap, i=in_.ap: h.tensor_copy(o, i), [in_], [out])

    def memset(self, eng, out, val):
        self.op(eng, lambda h, o=out.ap: h.memset(o, val), [], [out])

    def reduce(self, out, in_, op, axis=AX.X):
        self.op(self.dve, lambda h, o=out.ap, i=in_.ap: h.tensor_reduce(o, i, axis, op), [in_], [out])

    def recip(self, out, in_):
        self.op(self.dve, lambda h, o=out.ap, i=in_.ap: h.reciprocal(o, i), [in_], [out])

    def scan(self, out, d0, d1, initial, op0, op1):
        ini = initial.ap if isinstance(initial, V) else initial
        self.op(self.dve, lambda h, o=out.ap, a=d0.ap, b=d1.ap: h.tensor_tensor_scan(o, a, b, ini, op0, op1),
                [d0, d1, initial], [out])


_NC_CACHE = {}


def build_nc(stage=99, prefix_super=True, dbg=False, do_sample=True, do_attn=True, do_ffn=True, tile_list=None, do_cout=True):
    nc = bass.Bass("TRN2", target_bir_lowering=False)
    fw = FW(nc)
    pe, act, dve, pool, sp = fw.pe, fw.act, fw.dve, fw.pool, fw.sp

    in_names = _NC_CACHE.setdefault("in_names", set())

    def din(name, shape, dt=F32):
        in_names.add(name)
        return T(nc.dram_tensor(name, list(shape), dt, kind="ExternalInput"), name)

    def dout(name, shape, dt=F32):
        return T(nc.dram_tensor(name, list(shape), dt, kind="ExternalOutput"), name)

    def dscr(name, shape, dt=F32):
        return T(nc.dram_tensor(name, list(shape), dt, kind="Internal"), name)

    stacks = []

    def push():
        stacks.append(ExitStack())

    def pop():
        fw.barrier()
        stacks.pop().close()

    def sb(name, shape, dt=F32):
        return T(stacks[-1].enter_context(nc.sbuf_tensor(name, list(shape), dt)), name)

    push()

    xw = din("xw", [WIN, D])
    tmA = din("tmA", [4, NT])
    tmV = din("tmV", [4, NT])
    tmO = din("tmO", [128, NT])
    negmask_d = din("negmask", [128, 128])
    sel_d = din("sel", [4, 4 * 128])
    w_in = din("w_in", [D, DIN])
    norm1_g = din("norm1_g", [D, 1])
    gate_bias = din("gate_bias", [8, 1])
    conv_wT = din("conv_wT", [DB, 4])
    conv_b = din("conv_b", [DB, 1])
    wq_d = din("wq", [NHB, EB, EB])
    wk_d = din("wk", [NHB, EB, EB])
    mhg_d = din("mhg", [DB, 1])
    skip_d = din("skip", [DB, 1])

    rel_bias_d = din("rel_bias", [32, 6])
    ohb_d = din("ohb", [3, 32, 512])
    mvec_d = din("mvec", [3, 1, 512])
    attn_g_d = din("attn_g", [DA, 1])
    w_out_d = din("w_out", [D, D])
    norm2_g_d = din("norm2_g", [D, 1])
    w_ff1_d = din("w_ff1", [D, DFF])
    w_ff2_d = din("w_ff2", [DFF, D])
    final_g_d = din("final_g", [1, D])

    xs_d = din("xs", [NS, D])
    cwk_d = din("cwk", [NS, 2048, DA])
    cwv_d = din("cwv", [NS, 2048, DA])
    s_conv_d = din("s_conv", [NS, 3, DB])
    s_C_d = din("s_C", [NS, NHB, EB, EB])
    s_n_d = din("s_n", [NS, NHB, EB])
    s_m_d = din("s_m", [NS, NHB])
    conv_w_d = din("conv_w", [4, DB])
    ohs_d = din("ohs", [3, 32, 128])
    e16_d = din("e16", [NS, NS * 128])
    ecol_d = din("ecol", [128, NS * NS])

    o_y = dout("o_y", [SEG, D])
    o_ys = dout("o_ys", [NS, D])
    o_swk = dout("o_swk", [NS, 2048, DA])
    o_swv = dout("o_swv", [NS, 2048, DA])
    o_sconv = dout("o_sconv", [NS, 3, DB])
    o_sC = dout("o_sC", [NS, NHB, EB, EB])
    o_sn = dout("o_sn", [NS, NHB * EB])
    o_sm = dout("o_sm", [NS, NHB])
    o_wk = dout("o_wk", [SEG, DA])
    o_wv = dout("o_wv", [SEG, DA])
    o_conv = dout("o_conv", [3, DB])
    o_C = dout("o_C", [NHB, EB, EB])
    o_n = dout("o_n", [NHB, EB])
    o_m = dout("o_m", [NHB, 1])
    if dbg:
        o_dbg = dout("o_dbg", [SEG, DB])

    ps = [T(nc.alloc_psum_tensor("ps%d" % i, [128, 512], F32), "ps%d" % i) for i in range(8)]
    for b_ in ps:
        b_._buf.excl = True
    psn = [0]

    reserved = []

    def pb():
        while True:
            b = ps[psn[0] % 8]
            psn[0] += 1
            if b not in reserved:
                return b

    ident_f = sb("ident_f", [128, 128], F32)
    ident_b = sb("ident_b", [128, 128], BF16)
    iot = sb("iot", [128, 128], F32)
    fw.op(pool, lambda h, o=iot[:].ap: h.iota(o, [[1, 128]], base=0, channel_multiplier=-1,
                                              allow_small_or_imprecise_dtypes=True), [], [iot[:]])
    fw.ts(dve, ident_f[:], iot[:], 0.0, None, ALU.is_equal)
    fw.copy(dve, ident_b[:], ident_f[:])
    negmask = sb("negmask_s", [128, 128], F32)
    fw.dma(sp, negmask[:], negmask_d[:])
    sel = sb("sel_s", [4, 4 * 128], F32)
    fw.dma(sp, sel[:], sel_d[:])
    tmA_s = sb("tmA_s", [4, NT], F32)
    tmV_s = sb("tmV_s", [4, NT], F32)
    tmO_s = sb("tmO_s", [128, NT], F32)
    fw.dma(sp, tmA_s[:], tmA[:])
    fw.dma(sp, tmV_s[:], tmV[:])
    fw.dma(sp, tmO_s[:], tmO[:])
    ones4 = sb("ones4", [4, 128], F32)
    fw.memset(dve, ones4[:], 1.0)
    gb_i = sb("gb_i", [4, 1], F32)
    gb_fn = sb("gb_fn", [4, 1], F32)
    fw.dma(sp, gb_i[:], gate_bias[0:4, :])
    fw.dma(sp, gb_fn[:], gate_bias[4:8, :])
    fw.ts(dve, gb_fn[:], gb_fn[:], -1.0, None, ALU.mult)
    cw = sb("cw", [128, 5, 4], F32)
    cb = sb("cb", [128, 5], F32)
    mhg = sb("mhg_s", [128, 5], F32)
    skp = sb("skp_s", [128, 5], F32)
    for c in range(5):
        fw.dma(sp, cw[:, c, :], conv_wT[128 * c:128 * c + 128, :])
        fw.dma(sp, cb[:, c:c + 1], conv_b[128 * c:128 * c + 128, :])
        fw.dma(sp, mhg[:, c:c + 1], mhg_d[128 * c:128 * c + 128, :])
        fw.dma(sp, skp[:, c:c + 1], skip_d[128 * c:128 * c + 128, :])

    if do_sample:
        for (src, dst) in ((cwk_d, o_swk), (cwv_d, o_swv)):
            for g in range(NS):
                for hh in range(4):
                    fw.dma(pool, dst[g, 512 * hh:min(512 * hh + 512, 2047), :],
                           src[g, 512 * hh + 1:min(512 * hh + 513, 2048), :])
    push()
    outaT = sb("outaT", [128, 3, SEG + 128], BF16)
    outbT = sb("outbT", [128, 5, SEG + 128], BF16)
    push()
    KT = sb("KT", [128, 3, 2 * SEG], BF16)
    QT = sb("QT", [128, 3, SEG], BF16)
    push()
    w_in_b = sb("w_in_b", [128, 8, DIN], BF16)
    g1 = sb("g1", [128, 8], F32)
    wq_b = sb("wq_b", [128, 5, DB], BF16)
    wk_b = sb("wk_b", [128, 5, DB], BF16)
    v_scr = dscr("v_scr", [2 * SEG, 6 * 65], BF16)

    CTa = sb("CTa", [128, NHB, EB + 1], F32)
    CTb = sb("CTb", [32, NHB, EB + 1], F32)
    CTa_b = sb("CTa_b", [128, NHB, EB + 1], BF16)
    CTb_b = sb("CTb_b", [32, NHB, EB + 1], BF16)
    m_st = sb("m_st", [4, 1], F32)
    fw.memset(dve, CTa[:], 0.0)
    fw.memset(dve, CTb[:], 0.0)
    CTa.split(NHB)
    CTb.split(NHB)
    fw.memset(dve, CTa_b[:], 0.0)
    fw.memset(dve, CTb_b[:], 0.0)
    fw.memset(dve, m_st[:], 0.0)

    xt = [sb("xt%d" % i, [128, D], F32) for i in range(2)]
    xn_2 = [sb("xn_%d" % i_, [128, D], BF16) for i_ in range(2)]
    xn = xn_2[0]
    ss_2 = [sb("ss_%d" % i_, [128, 1], F32) for i_ in range(2)]
    ss = ss_2[0]
    rstd_2 = [sb("rstd_%d" % i_, [128, 1], F32) for i_ in range(2)]
    rstd = rstd_2[0]
    junk = sb("junk", [128, D], BF16)
    push()
    for c in range(8):
        fw.dma(sp, g1[:, c:c + 1], norm1_g[128 * c:128 * c + 128, :])
    wstage = [sb("wstage%d" % i, [128, 1600], F32) for i in range(2)]
    HW_ = DIN // 2
    for c in range(8):
        for hf in range(2):
            st = wstage[hf]
            fw.dma(sp, st[:, 0:HW_], w_in[128 * c:128 * c + 128, HW_ * hf:HW_ * hf + HW_])
            if hf == 0:
                fw.ts(dve, st[:, 0:DA], st[:, 0:DA], 0.125, None, ALU.mult)
            fw.scale_copy(act if hf else dve, w_in_b[:, c, HW_ * hf:HW_ * hf + HW_], st[:, 0:HW_], g1[:, c:c + 1])
    for (src, dst, scl) in ((wq_d, wq_b, 1.0), (wk_d, wk_b, float(EB) ** -0.5)):
        stv = wstage[0] if src is wq_d else wstage[1]
        fw.memset(dve, stv[:, 0:5 * 320], 0.0)
        for c in range(5):
            lo, hi = 128 * c, 128 * c + 128
            for hh in range(NHB):
                a0, a1 = max(lo, EB * hh), min(hi, EB * hh + EB)
                if a0 >= a1:
                    continue
                slot = hh - (lo // EB)
                fw.dma(sp, stv[a0 - lo:a1 - lo, 320 * c + 160 * slot:320 * c + 160 * slot + 160],
                       src[hh, a0 - EB * hh:a1 - EB * hh, :])
        fw.memset(dve, dst[:], 0.0)
        for c in range(5):
            h0 = (128 * c) // EB
            nh = 2 if (128 * c + 127) // EB > h0 else 1
            fw.ts(dve, dst[:, c, EB * h0:EB * h0 + EB * nh], stv[:, 320 * c:320 * c + EB * nh], scl, None, ALU.mult)

    pop()
    i4 = sb("i4", [4, 4], F32)
    fw.copy(dve, i4[:], ident_f[0:4, 0:4])
    def load_x(i):
        fw.dma(sp, xt[i % 2][:], xw[128 * i:128 * i + 128, :])

    xlast = sb("xlast", [128, 5, 3], F32)
    fw.memset(dve, xlast[:], 0.0)
    NPRE = MAIN0 // 4 if prefix_super else 0
    if NPRE:
        push()
        hT4 = sb("hT4", [128, 8, 512], BF16)
        xT4 = sb("xT4", [128, 5, 515], F32)
        fw.memset(dve, xT4[:], 0.0)
        xT4.split(5)
        cT4 = sb("cT4", [128, 5, 512], F32).split(5)
        caT4 = sb("caT4", [128, 5, 512], BF16)
        vext4 = sb("vext4", [128, 4, NHB, EB + 1], BF16)
        fw.memset(dve, vext4[:], 1.0)
        vext4.split(4)
        ktok4 = sb("ktok4", [128, 4, DB], BF16).split(4)
        kw4 = sb("kw4", [128, 4, NHB, EB], BF16).split(4)
        gi4 = sb("gi4", [4, 512], F32)
        gl4 = sb("gl4", [4, 512], F32)
        ga4 = sb("ga4", [4, 512], F32)
        gu4 = sb("gu4", [4, 512], F32)
        gM4 = sb("gM4", [4, 512], F32)
        gdl4 = sb("gdl4", [4, 1], F32)
        TM4 = sb("TM4", [128, 16], F32)
        TMe4 = sb("TMe4", [128, 16], F32)
        dec4 = sb("dec4", [128, 4], F32)
        vatt4 = [sb("vatt4_%d" % i_, [128, 6, 65], BF16) for i_ in range(1)] * 2
        load_x(0)
        for g in range(NPRE):
            halo = 4 * g >= HALO0
            for t in range(4):
                i = 4 * g + t
                p = i % 2
                xn, ss, rstd = xn_2[p], ss_2[p], rstd_2[p]
                if i + 1 < NT:
                    load_x(i + 1)
                fw.stt(junk[:], xt[p][:], 1.0, xt[p][:], ALU.mult, ALU.mult, accum_out=ss[:])
                fw.ts(dve, rstd[:], ss[:], 1.0 / D, EPS, ALU.mult, ALU.add)
                fw.activation(rstd[:], rstd[:], AF.Ln)
                fw.activation(rstd[:], rstd[:], AF.Exp, scale=-0.5)
                fw.ts(dve, xn[:], xt[p][:], rstd[:], None, ALU.mult)
                bT = pb()
                pT = bT.bitcast(BF16)
                for c in range(8):
                    fw.transpose(pT[:, 128 * c:128 * c + 128], xn[:, 128 * c:128 * c + 128], ident_b[:], inc=(c == 7))
                fw.copy(act, hT4[:, :, 128 * t:128 * t + 128], V(pT.h[:, 0:1024].rearrange("p (c t) -> p c t", c=8), pT.buf))
            for t in range(4):
                i = 4 * g + t
                for (c0, n) in ((3 * DA, 512), (3 * DA + 512, 128)):
                    b_ = pb()
                    for c in range(8):
                        fw.mm(b_[:, 0:n], hT4[:, c, 128 * t:128 * t + 128], w_in_b[:, c, c0:c0 + n], start=(c == 0), stop=(c == 7))
                    if n == 512:
                        fw.copy(act, V(vext4.h[:, t, 0:3, 0:EB], vext4.subs[t]), V(b_.h[:, 0:480].rearrange("p (h e) -> p h e", h=3), b_.buf))
                        fw.copy(dve, V(vext4.h[:, t, 3, 0:32], vext4.subs[t]), b_[:, 480:512])
                    else:
                        fw.copy(dve, V(vext4.h[:, t, 3, 32:EB], vext4.subs[t]), b_[:, 0:128])
                if halo:
                    at = i - HALO0
                    b_ = pb()
                    for c in range(8):
                        fw.mm(b_[:, 0:DA], hT4[:, c, 128 * t:128 * t + 128], w_in_b[:, c, 2 * DA:3 * DA], start=(c == 0), stop=(c == 7))
                    va = vatt4[i % 2]
                    fw.copy(act, va[:, :, 0:64], V(b_.h[:, 0:DA].rearrange("p (h e) -> p h e", h=6), b_.buf))
                    fw.copy(dve, va[:, :, 64:65], V(tmO_s.h[:, i:i + 1].unsqueeze(1).to_broadcast([128, 6, 1]), tmO_s.buf))
                    fw.dma(sp, v_scr[128 * at:128 * at + 128, :], va.re("p h e -> p (h e)"))
            for j in range(5):
                b_ = pb()
                for c in range(8):
                    fw.mm(b_[:, 0:512], w_in_b[:, c, 3 * DA + 128 * j:3 * DA + 128 * j + 128], hT4[:, c, :], start=(c == 0), stop=(c == 7))
                fw.copy(act if j % 2 else dve, V(xT4.h[:, j, 3:515], xT4.subs[j]), b_[:, 0:512])
            if halo:
                at0 = 4 * g - HALO0
                for j in range(3):
                    b_ = pb()
                    for c in range(8):
                        fw.mm(b_[:, 0:512], w_in_b[:, c, DA + 128 * j:DA + 128 * j + 128], hT4[:, c, :], start=(c == 0), stop=(c == 7))
                    fw.copy(act, KT[:, j, 128 * at0:128 * at0 + 512], b_[:, 0:512])
            bgi, bgf = pb(), pb()
            for (b_, c0) in ((bgi, 3 * DA + 2 * DB), (bgf, 3 * DA + 2 * DB + 4)):
                for c in range(8):
                    fw.mm(b_[0:4, 0:512], w_in_b[:, c, c0:c0 + 4], hT4[:, c, :], start=(c == 0), stop=(c == 7))
            fw.ts(dve, gi4[:], bgi[0:4, 0:512], gb_i[:], tmA_s[:, 4 * g:4 * g + 1], ALU.add, ALU.add)
            fw.activation(gl4[:], bgf[0:4, 0:512], AF.Exp, bias=gb_fn[:], scale=-1.0)
            fw.activation(gl4[:], gl4[:], AF.Ln, bias=1.0)
            fw.ts(dve, gl4[:], gl4[:], tmV_s[:, 4 * g:4 * g + 1], None, ALU.mult)
            for c in range(5):
                fw.activation(cT4.sub(c)[:, c, :], V(xT4.h[:, c, 0:512], xT4.subs[c]), AF.Identity, bias=cb[:, c:c + 1], scale=cw[:, c, 0:1])
            for k in range(1, 4):
                for c in range(5):
                    fw.stt(cT4.sub(c)[:, c, :], V(xT4.h[:, c, k:k + 512], xT4.subs[c]), cw[:, c, k:k + 1], cT4.sub(c)[:, c, :], ALU.mult, ALU.add)
            for c in range(5):
                fw.copy(dve, V(xT4.h[:, c, 0:3], xT4.subs[c]), V(xT4.h[:, c, 512:515], xT4.subs[c]))
            fw.activation(caT4[:], cT4[:], AF.Silu)
            for t in range(4):
                bk1, bk2 = pb(), pb()
                for c in range(5):
                    fw.mm(bk1[:, 0:512], caT4[:, c, 128 * t:128 * t + 128], wk_b[:, c, 0:512], start=(c == 0), stop=(c == 4))
                for c in range(5):
                    fw.mm(bk2[:, 0:128], caT4[:, c, 128 * t:128 * t + 128], wk_b[:, c, 512:640], start=(c == 0), stop=(c == 4))
                fw.copy(act, V(ktok4.h[:, t, 0:512], ktok4.subs[t]), bk1[:, 0:512])
                fw.copy(act, V(ktok4.h[:, t, 512:640], ktok4.subs[t]), bk2[:, 0:128])
            fw.scan(ga4[:], V(ones4.h[:, 0:1].to_broadcast([4, 512]), ones4.buf), gl4[:], 0.0, ALU.mult, ALU.add)
            fw.tt(dve, gu4[:], gi4[:], ga4[:], ALU.subtract)
            fw.scan(gM4[:], gu4[:], gu4[:], m_st[:], ALU.max, ALU.max)
            fw.ts(dve, gi4[:], gu4[:], gM4[:, 511:512], None, ALU.subtract)
            fw.ts(dve, gdl4[:], gM4[:, 511:512], -1.0, m_st[:], ALU.mult, ALU.add)
            bt = pb()
            for t in range(4):
                fw.transpose(bt[:, 4 * t:4 * t + 4], gi4[:, 128 * t:128 * t + 128], ident_f[0:4, 0:4])
            fw.copy(dve, TM4[:], bt[:, 0:16])
            fw.activation(TMe4[:], TM4[:], AF.Exp)
            bd = pb()
            fw.mm(bd[:, 0:4], V(gdl4.h[:, 0:1].to_broadcast([4, 128]), gdl4.buf), i4[:], start=True, stop=True)
            fw.activation(dec4[:], bd[:, 0:4], AF.Exp)
            for t in range(4):
                fw.tt(dve, V(kw4.h[:, t, :, :], kw4.subs[t]), V(ktok4.h[:, t, :].rearrange("p (h e) -> p h e", h=NHB), ktok4.subs[t]),
                      V(TMe4.h[:, 4 * t:4 * t + 4].unsqueeze(2).to_broadcast([128, NHB, EB]), TMe4.buf), ALU.mult)
            for hh in range(NHB):
                bu = pb()
                for t in range(4):
                    fw.mm(bu[:, 0:EB + 1], V(kw4.h[:, t, hh, 0:128], kw4.subs[t]), V(vext4.h[:, t, hh, :], vext4.subs[t]), start=(t == 0), stop=(t == 3))
                for t in range(4):
                    fw.mm(bu[0:32, 256:256 + EB + 1], V(kw4.h[:, t, hh, 128:EB], kw4.subs[t]), V(vext4.h[:, t, hh, :], vext4.subs[t]), start=(t == 0), stop=(t == 3))
                fw.stt(CTa.sub(hh)[:, hh, :], CTa.sub(hh)[:, hh, :], dec4[:, hh:hh + 1], bu[:, 0:EB + 1], ALU.mult, ALU.add)
                fw.stt(CTb.sub(hh)[:, hh, :], CTb.sub(hh)[:, hh, :], dec4[0:32, hh:hh + 1], bu[0:32, 256:256 + EB + 1], ALU.mult, ALU.add)
            fw.tt(dve, m_st[:], ga4[:, 511:512], gM4[:, 511:512], ALU.add)
        fw.copy(act, CTa_b[:], CTa[:])
        fw.copy(act, CTb_b[:], CTb[:])
        fw.copy(dve, xlast[:], xT4[:, :, 0:3])
        pop()
    push()
    obT = sb("obT", [128, 5, 128], BF16)
    caT_2 = [sb("caT_%d" % i_, [128, 5, 128], BF16) for i_ in range(2)]
    caT = caT_2[0]
    hT_2 = [sb("hT_%d" % i_, [128, 8, 128], BF16) for i_ in range(2)]
    hT = hT_2[0]
    kvst = [sb("kvst%d" % i, [128, 2 * DA], F32) for i in range(2)]
    xbst = [sb("xbst%d" % i, [128, DB], F32) for i in range(1)]
    vext_2 = [sb("vext_%d" % i_, [128, NHB, EB + 1], BF16) for i_ in range(2)]
    vext = vext_2[0]
    vatt = [sb("vatt%d" % i, [128, 6, 65], BF16) for i in range(2)]
    xbT = [sb("xbT%d" % i, [128, 5, 131], F32) for i in range(2)]
    fw.memset(dve, xbT[0][:], 0.0)
    fw.memset(dve, xbT[1][:], 0.0)
    fw.memset(dve, vext_2[0][:], 1.0)
    fw.memset(dve, vext_2[1][:], 1.0)
    cT_2 = [sb("cT_%d" % i_, [128, 5, 128], F32) for i_ in range(2)]
    cT = cT_2[0]
    ktok_2 = [sb("ktok_%d" % i_, [128, DB], F32) for i_ in range(2)]
    ktok = ktok_2[0]
    kw_2 = [sb("kw_%d" % i_, [128, NHB, EB], BF16) for i_ in range(2)]
    kw = kw_2[0]
    qTa = sb("qTa", [128, NHB, 128], BF16)
    qTb = sb("qTb", [32, NHB, 128], BF16)
    kTa = sb("kTa", [128, NHB, 128], BF16)
    kTb = sb("kTb", [32, NHB, 128], BF16)
    gi_2 = [sb("gi_%d" % i_, [4, 128], F32) for i_ in range(2)]
    gi = gi_2[0]
    gl_2 = [sb("gl_%d" % i_, [4, 128], F32) for i_ in range(2)]
    gl = gl_2[0]
    ga_2 = [sb("ga_%d" % i_, [4, 128], F32) for i_ in range(2)]
    ga = ga_2[0]
    gu_2 = [sb("gu_%d" % i_, [4, 128], F32) for i_ in range(2)]
    gu = gu_2[0]
    gM_2 = [sb("gM_%d" % i_, [4, 128], F32) for i_ in range(2)]
    gM = gM_2[0]
    gnM_2 = [sb("gnM_%d" % i_, [4, 128], F32) for i_ in range(2)]
    gnM = gnM_2[0]
    G1a_2 = [sb("G1a_%d" % i_, [4, 128], F32) for i_ in range(2)]
    G1a = G1a_2[0]
    G1b_2 = [sb("G1b_%d" % i_, [4, 128], F32) for i_ in range(2)]
    G1b = G1b_2[0]
    G1c_2 = [sb("G1c_%d" % i_, [4, 128], F32) for i_ in range(2)]
    G1c = G1c_2[0]
    gdl_2 = [sb("gdl_%d" % i_, [4, 1], F32) for i_ in range(2)]
    gdl = gdl_2[0]
    TM_2 = [sb("TM_%d" % i_, [128, 16], F32) for i_ in range(2)]
    TM = TM_2[0]
    TMe_2 = [sb("TMe_%d" % i_, [128, 12], F32) for i_ in range(2)]
    TMe = TMe_2[0]
    dec_bc_2 = [sb("dec_bc_%d" % i_, [128, 4], F32) for i_ in range(2)]
    dec_bc = dec_bc_2[0]
    expD = sb("expD", [128, 128], F32)
    AT = sb("AT", [128, 128], BF16)
    inter_s = sb("inter_s", [128, EB + 1], F32)
    numd = sb("numd", [128, EB + 1], F32)
    sc1 = sb("sc1", [128, 8], F32)
    hbn = sb("hbn", [128, DB], BF16)
    gt1 = sb("gt1", [128, 5, 128], F32)

    tile_list = list(range(4 * NPRE, NT)) if tile_list is None else tile_list
    if tile_list and not NPRE:
        load_x(tile_list[0])
    if NPRE:
        fw.copy(dve, xbT[1][:, :, 128:131], xlast[:])
    obT_2 = [obT, sb("obT_b", [128, 5, 128], BF16)]

    def stage1a(i):
            p = i % 2
            xn = xn_2[p]
            hT = hT_2[p]
            ss = ss_2[p]
            rstd = rstd_2[p]
            caT = caT_2[p]
            vext = vext_2[p]
            cT = cT_2[p]
            ktok = ktok_2[p]
            kw = kw_2[p]
            gi = gi_2[p]
            gl = gl_2[p]
            ga = ga_2[p]
            gu = gu_2[p]
            gM = gM_2[p]
            gnM = gnM_2[p]
            G1a = G1a_2[p]
            G1b = G1b_2[p]
            G1c = G1c_2[p]
            gdl = gdl_2[p]
            TM = TM_2[p]
            TMe = TMe_2[p]
            dec_bc = dec_bc_2[p]
            is_main = i >= MAIN0
            obT = obT_2[p]
            if cT.subs is None:
                cT.split(5)
            is_halo = i >= HALO0
            mt = i - MAIN0
            if i + 1 < NT and (i + 1) in tile_list:
                load_x(i + 1)
            fw.stt(junk[:], xt[p][:], 1.0, xt[p][:], ALU.mult, ALU.mult, accum_out=ss[:])
            fw.ts(dve, rstd[:], ss[:], 1.0 / D, EPS, ALU.mult, ALU.add)
            fw.activation(rstd[:], rstd[:], AF.Ln)
            fw.activation(rstd[:], rstd[:], AF.Exp, scale=-0.5)
            fw.ts(dve, xn[:], xt[p][:], rstd[:], None, ALU.mult)
            bT = pb()
            pT = bT.bitcast(BF16)
            for c in range(8):
                fw.transpose(pT[:, 128 * c:128 * c + 128], xn[:, 128 * c:128 * c + 128], ident_b[:], inc=(c == 7))
            fw.copy(act, hT.re("p c t -> p (c t)"), pT[:, 0:1024])


    def stage1b(i):
            p = i % 2
            xn = xn_2[p]
            hT = hT_2[p]
            ss = ss_2[p]
            rstd = rstd_2[p]
            caT = caT_2[p]
            vext = vext_2[p]
            cT = cT_2[p]
            ktok = ktok_2[p]
            kw = kw_2[p]
            gi = gi_2[p]
            gl = gl_2[p]
            ga = ga_2[p]
            gu = gu_2[p]
            gM = gM_2[p]
            gnM = gnM_2[p]
            G1a = G1a_2[p]
            G1b = G1b_2[p]
            G1c = G1c_2[p]
            gdl = gdl_2[p]
            TM = TM_2[p]
            TMe = TMe_2[p]
            dec_bc = dec_bc_2[p]
            is_main = i >= MAIN0
            obT = obT_2[p]
            if cT.subs is None:
                cT.split(5)
            is_halo = i >= HALO0
            mt = i - MAIN0
            def proj_tok(c0, n):
                b = pb()
                for c in range(8):
                    fw.mm(b[:, 0:n], hT[:, c, :], w_in_b[:, c, c0:c0 + n], start=(c == 0), stop=(c == 7))
                return b

            def proj_feat(c0, nchunks, M=128):
                b = pb()
                for j in range(nchunks):
                    for c in range(8):
                        fw.mm(b[0:M, 128 * j:128 * j + 128], w_in_b[:, c, c0 + M * j:c0 + M * j + M], hT[:, c, :],
                              start=(c == 0), stop=(c == 7))
                return b

            if is_halo:
                at = i - HALO0
                bk = proj_feat(DA, 3)
                fw.copy(act, KT[:, :, 128 * at:128 * at + 128], V(bk.h[:, 0:384].rearrange("p (c t) -> p c t", c=3), bk.buf))
                bkv = proj_tok(2 * DA, DA)
                va = vatt[p]
                if is_main:
                    bkk = proj_tok(DA, DA)
                    st = kvst[p]
                    fw.copy(dve, st[:, 0:DA], bkk[:, 0:DA])
                    fw.copy(dve, st[:, DA:2 * DA], bkv[:, 0:DA])
                    r0 = 128 * mt
                    fw.dma(sp, o_wk[r0:r0 + 128, :], st[:, 0:DA])
                    fw.dma(sp, o_wv[r0:r0 + 128, :], st[:, DA:2 * DA])
                fw.copy(act, va[:, :, 0:64], V(bkv.h[:, 0:DA].rearrange("p (h e) -> p h e", h=6), bkv.buf))
                fw.copy(dve, va[:, :, 64:65], V(tmO_s.h[:, i:i + 1].unsqueeze(1).to_broadcast([128, 6, 1]), tmO_s.buf))
                fw.dma(sp, v_scr[128 * at:128 * at + 128, :], va.re("p h e -> p (h e)"))
            if is_main:
                bq = proj_feat(0, 3)
                fw.copy(act, QT[:, :, 128 * mt:128 * mt + 128], V(bq.h[:, 0:384].rearrange("p (c t) -> p c t", c=3), bq.buf))
                b1 = proj_feat(3 * DA + DB, 4)
                fw.activation(obT[:, 0:4, :], V(b1.h[:, 0:512].rearrange("p (c t) -> p c t", c=4), b1.buf), AF.Sigmoid)
                b2 = proj_feat(3 * DA + DB + 512, 1)
                fw.activation(obT[:, 4, :], b2[:, 0:128], AF.Sigmoid)

            if stage < 1:
                return
            xT = xbT[p]
            xTp = xbT[1 - p]
            fw.copy(dve, xT[:, :, 0:3], xTp[:, :, 128:131])
            if stage < 1.2:
                return
            b1 = proj_feat(3 * DA, 4)
            if stage < 1.4:
                return
            fw.copy(act, xT[:, 0:4, 3:131], V(b1.h[:, 0:512].rearrange("p (c t) -> p c t", c=4), b1.buf))
            if stage < 1.6:
                return
            b2 = proj_feat(3 * DA + 512, 1)
            if stage < 1.7:
                return
            if stage < 1.8:
                fw.copy(act, junk[:, 0:128], b2[:, 0:128])
                return
            if VAR == 'dve':
                fw.copy(dve, xT[:, 4, 3:131], b2[:, 0:128])
            elif VAR == 'col0':
                fw.copy(act, xT[:, 4, 0:128], b2[:, 0:128])
            elif VAR == 'chunk3':
                fw.copy(act, xT[:, 3, 3:131], b2[:, 0:128])
            else:
                fw.copy(act, xT[:, 4, 3:131], b2[:, 0:128])
            if stage < 2:
                return
            bg = pb()
            for (j, c0) in ((0, 3 * DA + 2 * DB), (1, 3 * DA + 2 * DB + 4)):
                for c in range(8):
                    fw.mm(bg[0:4, 128 * j:128 * j + 128], w_in_b[:, c, c0:c0 + 4], hT[:, c, :], start=(c == 0), stop=(c == 7))
            if stage < 2.1:
                fw.copy(dve, gi[:], bg[0:4, 0:128])
                return
            fw.ts(dve, gi[:], bg[0:4, 0:128], gb_i[:], tmA_s[:, i:i + 1], ALU.add, ALU.add)
            if stage < 2.2:
                return
            fw.activation(gl[:], bg[0:4, 128:256], AF.Exp, bias=gb_fn[:], scale=-1.0)
            if stage < 2.3:
                return
            fw.activation(gl[:], gl[:], AF.Ln, bias=1.0)
            if stage < 2.4:
                return
            fw.ts(dve, gl[:], gl[:], tmV_s[:, i:i + 1], None, ALU.mult)
            if stage < 3:
                return
            bx = proj_tok(3 * DA, 512)
            bx2 = proj_tok(3 * DA + 512, 128)
            for hh in range(NHB):
                c0 = EB * hh
                if c0 + EB <= 512:
                    fw.copy(act if hh % 2 else dve, vext[:, hh, 0:EB], bx[:, c0:c0 + EB])
                else:
                    fw.copy(dve, vext[:, hh, 0:512 - c0], bx[:, c0:512])
                    fw.copy(act, vext[:, hh, 512 - c0:EB], bx2[:, 0:c0 + EB - 512])
            if i == NT - 1:
                xs_ = xbst[0]
                fw.copy(dve, xs_[:, 0:512], bx[:, 0:512])
                fw.copy(dve, xs_[:, 512:640], bx2[:, 0:128])
                fw.dma(sp, o_conv[:, :], xs_[125:128, :])
            if stage < 4:
                return
            for c in range(5):
                fw.activation(cT.sub(c)[:, c, :], xT[:, c, 0:128], AF.Identity, bias=cb[:, c:c + 1], scale=cw[:, c, 0:1])
            for k in range(1, 4):
                for c in range(5):
                    fw.stt(cT.sub(c)[:, c, :], xT[:, c, k:k + 128], cw[:, c, k:k + 1], cT.sub(c)[:, c, :], ALU.mult, ALU.add)
            fw.activation(caT[:], cT[:], AF.Silu)
            if stage < 5:
                return
            bk1 = pb()
            bk2 = pb()
            for c in range(5):
                fw.mm(bk1[:, 0:512], caT[:, c, :], wk_b[:, c, 0:512], start=(c == 0), stop=(c == 4))
            for c in range(5):
                fw.mm(bk2[:, 0:128], caT[:, c, :], wk_b[:, c, 512:640], start=(c == 0), stop=(c == 4))
            fw.copy(act, ktok[:, 0:512], bk1[:, 0:512])
            fw.copy(act, ktok[:, 512:640], bk2[:, 0:128])


    def stage2a(i):
            p = i % 2
            xn = xn_2[p]
            hT = hT_2[p]
            ss = ss_2[p]
            rstd = rstd_2[p]
            caT = caT_2[p]
            vext = vext_2[p]
            cT = cT_2[p]
            ktok = ktok_2[p]
            kw = kw_2[p]
            gi = gi_2[p]
            gl = gl_2[p]
            ga = ga_2[p]
            gu = gu_2[p]
            gM = gM_2[p]
            gnM = gnM_2[p]
            G1a = G1a_2[p]
            G1b = G1b_2[p]
            G1c = G1c_2[p]
            gdl = gdl_2[p]
            TM = TM_2[p]
            TMe = TMe_2[p]
            dec_bc = dec_bc_2[p]
            is_main = i >= MAIN0
            obT = obT_2[p]
            if cT.subs is None:
                cT.split(5)
            is_halo = i >= HALO0
            mt = i - MAIN0
            if stage < 6:
                return
            fw.scan(ga[:], ones4[:], gl[:], 0.0, ALU.mult, ALU.add)
            fw.tt(dve, gu[:], gi[:], ga[:], ALU.subtract)
            fw.scan(gM[:], gu[:], gu[:], m_st[:], ALU.max, ALU.max)
            fw.ts(dve, gnM[:], gM[:], -1.0, None, ALU.mult)
            fw.ts(dve, G1a[:], gu[:], gM[:, 127:128], None, ALU.subtract)
            fw.ts(dve, G1b[:], gM[:], -1.0, m_st[:], ALU.mult, ALU.add)
            fw.stt(G1c[:], ga[:], -1.0, gM[:], ALU.mult, ALU.subtract)
            fw.ts(dve, gdl[:], gM[:, 127:128], -1.0, m_st[:], ALU.mult, ALU.add)
            if stage < 7:
                return
            bt = pb()
            for (j_, g_) in enumerate((G1a, G1b, G1c, gu)):
                fw.transpose(bt[:, 4 * j_:4 * j_ + 4], g_[:], ident_f[0:4, 0:4])
            fw.copy(dve, TM[:], bt[:, 0:16])
            fw.activation(TMe[:], TM[:, 0:12], AF.Exp)
            if stage < 8:
                return
            bd = pb()
            fw.mm(bd[:, 0:4], V(gdl.h[:, 0:1].to_broadcast([4, 128]), gdl.buf), i4[:], start=True, stop=True)
            fw.activation(dec_bc[:], bd[:, 0:4], AF.Exp)

            if stage < 9:
                return
            if is_main:
                for (wsrc, da_, db_) in ((wq_b, qTa, qTb), (wk_b, kTa, kTb)):
                    ba = pb()
                    bb = pb()
                    for hh in range(NHB):
                        cs = sorted(set([(EB * hh) // 128, (EB * hh + EB - 1) // 128]))
                        for n_, c in enumerate(cs):
                            fw.mm(ba[:, 128 * hh:128 * hh + 128], wsrc[:, c, EB * hh:EB * hh + 128], caT[:, c, :],
                                  start=(n_ == 0), stop=(n_ == len(cs) - 1))
                        for n_, c in enumerate(cs):
                            fw.mm(bb[0:32, 128 * hh:128 * hh + 128], wsrc[:, c, EB * hh + 128:EB * hh + EB], caT[:, c, :],
                                  start=(n_ == 0), stop=(n_ == len(cs) - 1))
                    fw.copy(act, da_.re("p h t -> p (h t)"), ba[:, 0:512])
                    fw.copy(dve, db_.re("p h t -> p (h t)"), bb[0:32, 0:512])
                for hh in range(NHB):
                    bs = pb()
                    fw.mm(bs[:, 0:128], kTa[:, hh, :], qTa[:, hh, :], start=True, stop=False)
                    fw.mm(bs[:, 0:128], kTb[:, hh, :], qTb[:, hh, :], start=False, stop=True)
                    fw.mm(bs[:, 128:256], sel[:, 128 * hh:128 * hh + 128], gnM[:], start=True, stop=False)
                    fw.mm(bs[:, 128:256], ident_f[:], negmask[:], start=False, stop=True)
                    fw.activation(expD[:], bs[:, 128:256], AF.Exp, bias=TM[:, 12 + hh:13 + hh])
                    fw.tt(dve, AT[:], bs[:, 0:128], expD[:], ALU.mult)
                    bn = pb()
                    fw.mm(bn[:, 0:EB + 1], AT[:], vext[:, hh, :], start=True, stop=True)
                    fw.mm(bn[:, 256:256 + EB + 1], qTa[:, hh, :], CTa_b[:, hh, :], start=True, stop=False)
                    fw.mm(bn[:, 256:256 + EB + 1], qTb[:, hh, :], CTb_b[:, hh, :], start=False, stop=True)
                    fw.activation(inter_s[:], bn[:, 256:256 + EB + 1], AF.Copy, scale=TMe[:, 4 + hh:5 + hh])
                    fw.tt(dve, numd[:], bn[:, 0:EB + 1], inter_s[:], ALU.add)
                    fw.stt(sc1[:, 6:7], numd[:, EB:EB + 1], -1.0, numd[:, EB:EB + 1], ALU.mult, ALU.max)
                    fw.ts(dve, sc1[:, 0:1], sc1[:, 6:7], TMe[:, 8 + hh:9 + hh], None, ALU.max)
                    fw.recip(sc1[:, 1:2], sc1[:, 0:1])
                    fw.stt(junk[:, 0:EB], numd[:, 0:EB], 1.0, numd[:, 0:EB], ALU.mult, ALU.mult, accum_out=sc1[:, 2:3])
                    fw.tt(dve, sc1[:, 3:4], sc1[:, 1:2], sc1[:, 1:2], ALU.mult)
                    fw.tt(dve, sc1[:, 3:4], sc1[:, 3:4], sc1[:, 2:3], ALU.mult)
                    fw.ts(dve, sc1[:, 3:4], sc1[:, 3:4], 1.0 / EB, EPS, ALU.mult, ALU.add)
                    fw.activation(sc1[:, 4:5], sc1[:, 3:4], AF.Ln)
                    fw.activation(sc1[:, 4:5], sc1[:, 4:5], AF.Exp, scale=-0.5)
                    fw.tt(dve, sc1[:, 5:6], sc1[:, 4:5], sc1[:, 1:2], ALU.mult)
                    fw.ts(dve, hbn[:, EB * hh:EB * hh + EB], numd[:, 0:EB], sc1[:, 5:6], None, ALU.mult)
                bh = pb()
                bh2 = pb()
                pTh = bh.bitcast(BF16)
                pTh2 = bh2.bitcast(BF16)
                for c in range(5):
                    dstp = pTh[:, 128 * c:128 * c + 128] if c < 4 else pTh2[:, 0:128]
                    fw.transpose(dstp, hbn[:, 128 * c:128 * c + 128], ident_b[:])
                for c in range(5):
                    srcp = pTh[:, 128 * c:128 * c + 128] if c < 4 else pTh2[:, 0:128]
                    fw.ts(dve, gt1[:, c, :], srcp, mhg[:, c:c + 1], None, ALU.mult)
                    fw.stt(gt1[:, c, :], caT[:, c, :], skp[:, c:c + 1], gt1[:, c, :], ALU.mult, ALU.add)
                fw.tt(dve, outbT[:, :, 128 * mt:128 * mt + 128], gt1[:], obT[:], ALU.mult)


    def stage2b(i):
            p = i % 2
            xn = xn_2[p]
            hT = hT_2[p]
            ss = ss_2[p]
            rstd = rstd_2[p]
            caT = caT_2[p]
            vext = vext_2[p]
            cT = cT_2[p]
            ktok = ktok_2[p]
            kw = kw_2[p]
            gi = gi_2[p]
            gl = gl_2[p]
            ga = ga_2[p]
            gu = gu_2[p]
            gM = gM_2[p]
            gnM = gnM_2[p]
            G1a = G1a_2[p]
            G1b = G1b_2[p]
            G1c = G1c_2[p]
            gdl = gdl_2[p]
            TM = TM_2[p]
            TMe = TMe_2[p]
            dec_bc = dec_bc_2[p]
            is_main = i >= MAIN0
            obT = obT_2[p]
            if cT.subs is None:
                cT.split(5)
            is_halo = i >= HALO0
            mt = i - MAIN0
            if stage < 10:
                return
            fw.tt(dve, kw[:], V(ktok.h[:].rearrange("p (h e) -> p h e", h=NHB), ktok.buf),
                  V(TMe.h[:, 0:4].unsqueeze(2).to_broadcast([128, NHB, EB]), TMe.buf), ALU.mult)
            for hh in range(NHB):
                bu = pb()
                fw.mm(bu[:, 0:EB + 1], kw[:, hh, 0:128], vext[:, hh, :], start=True, stop=True)
                fw.mm(bu[0:32, 256:256 + EB + 1], kw[:, hh, 128:EB], vext[:, hh, :], start=True, stop=True)
                fw.stt(CTa.sub(hh)[:, hh, :], CTa.sub(hh)[:, hh, :], dec_bc[:, hh:hh + 1], bu[:, 0:EB + 1], ALU.mult, ALU.add)
                fw.stt(CTb.sub(hh)[:, hh, :], CTb.sub(hh)[:, hh, :], dec_bc[0:32, hh:hh + 1], bu[0:32, 256:256 + EB + 1], ALU.mult, ALU.add)
            fw.copy(act, CTa_b[:], CTa[:])
            fw.copy(act, CTb_b[:], CTb[:])
            fw.tt(dve, m_st[:], ga[:, 127:128], gM[:, 127:128], ALU.add)


    for idx_, i in enumerate(tile_list):
        stage1a(i)
        if idx_ > 0:
            stage2a(tile_list[idx_ - 1])
        stage1b(i)
        if idx_ > 0:
            stage2b(tile_list[idx_ - 1])
    if tile_list:
        stage2a(tile_list[-1])
        stage2b(tile_list[-1])
    xn, ss, rstd = xn_2[0], ss_2[0], rstd_2[0]
    pop()
    push()
    if do_sample:
        zs = sb("zs", [NS, DIN], F32)
        hT = sb("hTs", [128, 8, NS], BF16)
        caT = sb("caTs", [128, 5, NS], BF16)
        s_q = dscr("s_q", [NS * NHB, EB], F32)
        s_k = dscr("s_k", [NS * NHB, EB], F32)
        s_v = dscr("s_v", [NS * NHB, EB], F32)
        s_sc = dscr("s_sc", [NS * NHB, 4], F32)
        s_misc = dscr("s_misc", [NS, 2 * DB + 3 * DA], F32)
        p = 0
        fw.dma(sp, xt[p][0:NS, :], xs_d[:, :])
        fw.stt(junk[0:NS, :], xt[p][0:NS, :], 1.0, xt[p][0:NS, :], ALU.mult, ALU.mult, accum_out=ss[0:NS, :])
        fw.ts(dve, rstd[0:NS, :], ss[0:NS, :], 1.0 / D, EPS, ALU.mult, ALU.add)
        fw.activation(rstd[0:NS, :], rstd[0:NS, :], AF.Ln)
        fw.activation(rstd[0:NS, :], rstd[0:NS, :], AF.Exp, scale=-0.5)
        fw.ts(dve, xn[0:NS, :], xt[p][0:NS, :], rstd[0:NS, :], None, ALU.mult)
        bT = pb()
        pT = bT.bitcast(BF16)
        for c in range(8):
            fw.transpose(pT[:, 128 * c:128 * c + NS], xn[0:NS, 128 * c:128 * c + 128], ident_b[0:NS, 0:NS], inc=(c == 7))
        for c in range(8):
            fw.copy(act, hT[:, c, 0:NS], pT[:, 128 * c:128 * c + NS])
        for c0 in range(0, DIN, 512):
            n = min(512, DIN - c0)
            bk_ = pb()
            for c in range(8):
                fw.mm(bk_[0:NS, 0:n], hT[:, c, 0:NS], w_in_b[:, c, c0:c0 + n], start=(c == 0), stop=(c == 7))
            fw.copy(act if (c0 // 512) % 2 else dve, zs[:, c0:c0 + n], bk_[0:NS, 0:n])
        fw.dma(sp, o_swk[:, 2047, :], zs[:, DA:2 * DA])
        fw.dma(sp, o_swv[:, 2047, :], zs[:, 2 * DA:3 * DA])
        fw.dma(sp, o_sconv[:, 2, :], zs[:, 3 * DA:3 * DA + DB])
        fw.dma(sp, o_sconv[:, 0:2, :], s_conv_d[:, 1:3, :])
        cwr = sb("cwr", [NS, 4, DB], F32)
        cbr = sb("cbr", [NS, DB], F32)
        scv = sb("scv", [NS, 3, DB], F32)
        fw.dma(sp, cwr.re("p i d -> p (i d)"), V(conv_w_d.h.ap().rearrange("i d -> (i d)").partition_broadcast(NS), conv_w_d.buf))
        fw.dma(sp, cbr[:], V(conv_b.h.ap().rearrange("d o -> (d o)").partition_broadcast(NS), conv_b.buf))
        fw.dma(sp, scv[:], s_conv_d[:, :, :])
        cs = sb("cs", [NS, DB], F32)
        cs2 = sb("cs2", [NS, DB], F32)
        fw.tt(dve, cs[:], zs[:, 3 * DA:3 * DA + DB], cwr[:, 3, :], ALU.mult)
        fw.tt(dve, cs[:], cs[:], cbr[:], ALU.add)
        for k in range(3):
            fw.tt(dve, cs2[:], scv[:, k, :], cwr[:, k, :], ALU.mult)
            fw.tt(dve, cs[:], cs[:], cs2[:], ALU.add)
        cas = sb("cas", [NS, DB], F32)
        casb = sb("casb", [NS, DB], BF16)
        fw.activation(cas[:], cs[:], AF.Silu)
        fw.copy(dve, casb[:], cas[:])
        fw.dma(sp, s_misc[:, DB:2 * DB], cas[:])
        sob = sb("sob", [NS, DB], F32)
        fw.activation(sob[:], zs[:, 3 * DA + DB:3 * DA + 2 * DB], AF.Sigmoid)
        fw.dma(sp, s_misc[:, 0:DB], sob[:])
        fw.dma(sp, s_misc[:, 2 * DB:2 * DB + 3 * DA], zs[:, 0:3 * DA])
        bT2 = pb()
        pT2 = bT2.bitcast(BF16)
        for c in range(5):
            fw.transpose(pT2[:, 128 * c:128 * c + NS], casb[:, 128 * c:128 * c + 128], ident_b[0:NS, 0:NS], inc=(c == 4))
        for c in range(5):
            fw.copy(act, caT[:, c, 0:NS], pT2[:, 128 * c:128 * c + NS])
        sqk = sb("sqk", [NS, 2, DB], F32)
        for wi, wsrc in enumerate((wq_b, wk_b)):
            for (c0, n) in ((0, 512), (512, 128)):
                bk_ = pb()
                for c in range(5):
                    fw.mm(bk_[0:NS, 0:n], caT[:, c, 0:NS], wsrc[:, c, c0:c0 + n], start=(c == 0), stop=(c == 4))
                fw.copy(dve, sqk[:, wi, c0:c0 + n], bk_[0:NS, 0:n])
        fw.dma(sp, V(s_q.h.ap().rearrange("(b h) e -> b (h e)", h=NHB), s_q.buf), sqk[:, 0, :])
        fw.dma(sp, V(s_k.h.ap().rearrange("(b h) e -> b (h e)", h=NHB), s_k.buf), sqk[:, 1, :])
        fw.dma(sp, V(s_v.h.ap().rearrange("(b h) e -> b (h e)", h=NHB), s_v.buf), zs[:, 3 * DA:3 * DA + DB])
        gbr = sb("gbr", [NS, 8], F32)
        sm0 = sb("sm0", [NS, NHB], F32)
        fw.dma(sp, gbr[:], V(gate_bias.h.ap().rearrange("g o -> (g o)").partition_broadcast(NS), gate_bias.buf))
        fw.dma(sp, sm0[:], s_m_d[:, :])
        sg = sb("sg", [NS, 8, NHB], F32)
        fw.tt(dve, sg[:, 0, :], zs[:, DIN - 8:DIN - 4], gbr[:, 0:4], ALU.add)
        fw.tt(dve, sg[:, 1, :], zs[:, DIN - 4:DIN], gbr[:, 4:8], ALU.add)
        fw.activation(sg[:, 1, :], sg[:, 1, :], AF.Exp, scale=-1.0)
        fw.activation(sg[:, 1, :], sg[:, 1, :], AF.Ln, bias=1.0)
        fw.tt(dve, sg[:, 2, :], sm0[:], sg[:, 1, :], ALU.subtract)
        fw.tt(dve, sg[:, 3, :], sg[:, 2, :], sg[:, 0, :], ALU.max)
        fw.tt(dve, sg[:, 4, :], sg[:, 2, :], sg[:, 3, :], ALU.subtract)
        fw.tt(dve, sg[:, 5, :], sg[:, 0, :], sg[:, 3, :], ALU.subtract)
        fw.ts(dve, sg[:, 6, :], sg[:, 3, :], -1.0, None, ALU.mult)
        ssc = sb("ssc", [NS, NHB, 4], F32)
        fw.activation(ssc[:, :, 0], sg[:, 4, :], AF.Exp)
        fw.activation(ssc[:, :, 1], sg[:, 5, :], AF.Exp)
        fw.activation(ssc[:, :, 2], sg[:, 6, :], AF.Exp)
        fw.copy(dve, ssc[:, :, 3], sg[:, 3, :])
        fw.dma(sp, o_sm[:, :], sg[:, 3, :])
        fw.dma(sp, V(s_sc.h.ap().rearrange("(b h) x -> b (h x)", h=NHB), s_sc.buf), ssc.re("p h x -> p (h x)"))

    Co_a = sb("Co_a", [128, NHB, EB], F32)
    Co_b = sb("Co_b", [33, NHB, EB], F32)
    for hh in range(NHB if do_cout else 0):
        b_ = pb()
        fw.transpose(b_[:, 0:128], CTa[:, hh, 0:128], ident_f[:])
        fw.transpose(b_[:, 128:160], CTb[:, hh, 0:128], ident_f[0:32, 0:32])
        fw.transpose(b_[0:33, 256:384], CTa[:, hh, 128:EB + 1], ident_f[:])
        fw.transpose(b_[0:33, 384:416], CTb[:, hh, 128:EB + 1], ident_f[0:32, 0:32])
        fw.copy(dve, Co_a[:, hh, :], b_[:, 0:EB])
        fw.copy(dve, Co_b[:, hh, :], b_[0:33, 256:256 + EB])
        fw.dma(sp, o_C[hh, 0:128, :], Co_a[:, hh, :])
        fw.dma(sp, o_C[hh, 128:EB, :], Co_b[0:32, hh, :])
        fw.dma(sp, o_n[hh:hh + 1, :], Co_b[32:33, hh, :])
    fw.dma(sp, o_m[:, :], m_st[:])
    if dbg:
        for mt in range(SEG // 128):
            pass

    pop()
    pop()
    push()
    if do_attn:
        rrep = dscr("rrep", [18 * 128 * 512 + 512], F32)
        oz_scr = [dscr("oz_scr%d" % b_, [SEG, 390], F32) for b_ in range(3)]
        rb = sb("rb", [32, 6], F32)
        ohb = sb("ohb_s", [32, 3, 512], F32)
        mvec = sb("mvec_s", [1, 3, 512], F32)
        ones1 = sb("ones1", [1, 128], F32)
        ag = sb("ag", [128, 3], F32)
        fw.dma(sp, rb[:], rel_bias_d[:])
        for b_ in range(3):
            fw.dma(sp, ohb[:, b_, :], ohb_d[b_])
            fw.dma(sp, mvec[:, b_, :], mvec_d[b_])
            fw.dma(sp, ag[:, b_:b_ + 1], attn_g_d[128 * b_:128 * b_ + 128, :])
        fw.memset(dve, ones1[:], 1.0)
        vfs = [sb("vfs%d" % i, [128, 512], F32) for i in range(2)]
        for b_ in range(3):
            for h in range(6):
                bk_ = pb()
                fw.mm(bk_[:, 0:512], V(rb.h[:, h:h + 1].to_broadcast([32, 128]), rb.buf), ohb[:, b_, :], start=True, stop=False)
                fw.mm(bk_[:, 0:512], ones1[:], mvec[:, b_, :], start=False, stop=True)
                vf = vfs[h % 2]
                fw.copy(act if h % 2 else dve, vf[:], bk_[:, 0:512])
                o0 = (b_ * 6 + h) * 128 * 512
                fw.dma(sp, rrep.ap(o0, [[512, 128], [1, 512]]), vf[:])
        biasT = sb("biasT", [128, 6, 2, 128], F32)
        stmp_2 = [sb("stmp%d" % i_, [128, 512], F32) for i_ in range(2)]
        PT_2 = [sb("PT%d" % i_, [128, 3, 512], BF16) for i_ in range(2)]
        Vp = [sb("Vp%d" % i, [128, 6, 65], BF16) for i in range(2)]
        Vo = [sb("Vo%d" % i, [128, 6, 65], BF16) for i in range(2)]
        ozs = [sb("ozs%d" % i, [128, 390], F32) for i in range(2)]
        srcs = []
        for b_, dil in enumerate((1, 4, 16)):
            if VAR == 'setup' or (VAR.startswith('br') and str(b_) not in VAR):
                continue
            unit = 128 * dil
            for n_ in range(SEG // unit):
                for r_ in range(dil):
                    srcs.append((b_, dil, n_ * unit + r_, n_ == 0 and r_ == 0))

        def att1(t_):
            b_, dil, q0, first = srcs[t_]
            unit = 128 * dil
            pp = t_ % 2
            stmp, PT = stmp_2[pp], PT_2[pp]
            if first:
                for h in range(6):
                    o0 = (b_ * 6 + h) * 128 * 512
                    fw.dma(sp, biasT[:, h, 0, :], rrep.ap(o0 + 255, [[511, 128], [1, 128]]))
                    fw.dma(sp, biasT[:, h, 1, :], rrep.ap(o0 + 127, [[511, 128], [1, 128]]))
            k_own = SEG + q0
            k_prev = k_own - unit
            fw.dma(sp, Vp[pp].re("p h e -> p (h e)"), v_scr[k_prev:k_prev + 127 * dil + 1:dil, :])
            fw.dma(sp, Vo[pp].re("p h e -> p (h e)"), v_scr[k_own:k_own + 127 * dil + 1:dil, :])
            for c in range(3):
                for hh in range(2):
                    bank = pb()
                    for blk, k0 in enumerate((k_prev, k_own)):
                        fw.mm(bank[:, 128 * blk:128 * blk + 128], KT[64 * hh:64 * hh + 64, c, k0:k0 + 127 * dil + 1:dil],
                              QT[64 * hh:64 * hh + 64, c, q0:q0 + 127 * dil + 1:dil], start=True, stop=True)
                    fw.tt(dve, stmp[:, 256 * hh:256 * hh + 256], bank[:, 0:256],
                          V(biasT.h[:, 2 * c + hh, :, :].rearrange("p b q -> p (b q)"), biasT.buf), ALU.add)
                    fw.activation(PT[:, c, 256 * hh:256 * hh + 256], stmp[:, 256 * hh:256 * hh + 256], AF.Exp)

        def att2(t_):
            b_, dil, q0, first = srcs[t_]
            pp = t_ % 2
            PT = PT_2[pp]
            boz = pb()
            for h in range(6):
                c, hh = divmod(h, 2)
                fw.mm(boz[:, 65 * h:65 * h + 65], PT[:, c, (2 * hh) * 128:(2 * hh) * 128 + 128], Vp[pp][:, h, :], start=True, stop=False)
                fw.mm(boz[:, 65 * h:65 * h + 65], PT[:, c, (2 * hh + 1) * 128:(2 * hh + 1) * 128 + 128], Vo[pp][:, h, :], start=False, stop=True)
            fw.copy(act, ozs[pp][:], boz[:, 0:390])
            fw.dma(sp, oz_scr[b_][q0:q0 + 127 * dil + 1:dil, :], ozs[pp][:])

        for t_ in range(len(srcs)):
            att1(t_)
            if t_ > 0:
                att2(t_ - 1)
        if srcs:
            att2(len(srcs) - 1)
        ozl = [sb("ozl%d" % i, [128, 6, 65], F32) for i in range(3)]
        osum = sb("osum", [128, 6, 65], F32)
        rz = sb("rz", [128, 6, 1], F32)
        on = sb("on", [128, 6, 64], F32)
        onb = sb("onb", [128, DA], BF16)
        junkb = sb("junkb", [128, DA], BF16)
        ssa = sb("ssa", [128, 2], F32)
        for mt in range(SEG // 128 if VAR == '' else 0):
            for b_ in range(3):
                fw.dma(sp, ozl[b_].re("p h e -> p (h e)"), oz_scr[b_][128 * mt:128 * mt + 128, :])
            fw.tt(dve, osum[:], ozl[0][:], ozl[1][:], ALU.add)
            fw.tt(dve, osum[:], osum[:], ozl[2][:], ALU.add)
            fw.recip(rz[:], osum[:, :, 64:65])
            fw.tt(dve, on[:], osum[:, :, 0:64], V(rz.h[:].to_broadcast([128, 6, 64]), rz.buf), ALU.mult)
            onf = on.re("p h e -> p (h e)")
            fw.stt(junkb[:], onf, 1.0, onf, ALU.mult, ALU.mult, accum_out=ssa[:, 0:1])
            fw.ts(dve, ssa[:, 1:2], ssa[:, 0:1], 1.0 / DA, EPS, ALU.mult, ALU.add)
            fw.activation(ssa[:, 1:2], ssa[:, 1:2], AF.Ln)
            fw.activation(ssa[:, 1:2], ssa[:, 1:2], AF.Exp, scale=-0.5)
            fw.ts(dve, onb[:], onf, ssa[:, 1:2], None, ALU.mult)
            bt_ = pb()
            pTa = bt_.bitcast(BF16)
            for c in range(3):
                fw.transpose(pTa[:, 128 * c:128 * c + 128], onb[:, 128 * c:128 * c + 128], ident_b[:])
            for c in range(3):
                fw.ts(dve, outaT[:, c, 128 * mt:128 * mt + 128], pTa[:, 128 * c:128 * c + 128], ag[:, c:c + 1], None, ALU.mult)
        if do_sample:
            h_scr = dscr("h_scr", [NS * NHB * 2, 80], F32)
            qp = sb("qp", [128, EB], F32)
            kp = sb("kp", [128, EB], F32)
            vp = sb("vp", [128, 80], F32)
            scp = sb("scp", [128, 4], F32)
            n_p = sb("n_p", [128, EB], F32)
            for half in range(2):
                fw.dma(sp, qp[half:128:2, :], s_q[:, :])
                fw.dma(sp, kp[half:128:2, :], s_k[:, :])
                fw.dma(sp, vp[half:128:2, :], s_v[:, 80 * half:80 * half + 80])
                fw.dma(sp, scp[half:128:2, :], s_sc[:, :])
                fw.dma(sp, n_p[half:128:2, :], V(s_n_d.h.ap().rearrange("b h e -> (b h) e"), s_n_d.buf))
            vw = sb("vw", [128, 80], F32)
            fw.ts(dve, vw[:], vp[:], scp[:, 1:2], None, ALU.mult)
            kwp = sb("kwp", [128, EB], F32)
            fw.ts(dve, kwp[:], kp[:], scp[:, 1:2], None, ALU.mult)
            fw.stt(n_p[:], n_p[:], scp[:, 0:1], kwp[:], ALU.mult, ALU.add)
            fw.dma(sp, V(o_sn.h.ap().rearrange("b (h e) -> (b h) e", h=NHB), o_sn.buf), n_p[0:128:2, :])
            den = sb("den", [128, 4], F32)
            fw.stt(kwp[:], n_p[:], 1.0, qp[:], ALU.mult, ALU.mult, accum_out=den[:, 0:1])
            fw.stt(den[:, 1:2], den[:, 0:1], -1.0, den[:, 0:1], ALU.mult, ALU.max)
            fw.ts(dve, den[:, 1:2], den[:, 1:2], scp[:, 2:3], None, ALU.max)
            fw.recip(den[:, 2:3], den[:, 1:2])
            CH = 10
            Cc = [sb("Cc%d" % i, [128, CH, EB], F32) for i in range(2)]
            Oc = [sb("Oc%d" % i, [128, CH, EB], F32) for i in range(2)]
            hnum = sb("hnum", [128, 80], F32)
            sCv = V(s_C_d.h.ap().rearrange("b h (t v) k -> (b h t) (v k)", t=2), s_C_d.buf)
            oCv = V(o_sC.h.ap().rearrange("b h (t v) k -> (b h t) (v k)", t=2), o_sC.buf)
            for ci in range(80 // CH):
                pq = ci % 2
                cc, oc = Cc[pq], Oc[pq]
                fw.dma(sp, cc.re("p v k -> p (v k)"), V(sCv.ap[:, CH * EB * ci:CH * EB * (ci + 1)], sCv.buf))
                fw.tt(dve, oc[:], V(vw.h[:, CH * ci:CH * ci + CH].unsqueeze(2).to_broadcast([128, CH, EB]), vw.buf),
                      V(kp.h[:].unsqueeze(1).to_broadcast([128, CH, EB]), kp.buf), ALU.mult)
                fw.stt(cc.re("p v k -> p (v k)"), cc.re("p v k -> p (v k)"), scp[:, 0:1], oc.re("p v k -> p (v k)"), ALU.mult, ALU.add)
                fw.dma(sp, V(oCv.ap[:, CH * EB * ci:CH * EB * (ci + 1)], oCv.buf), cc.re("p v k -> p (v k)"))
                fw.tt(dve, oc[:], cc[:], V(qp.h[:].unsqueeze(1).to_broadcast([128, CH, EB]), qp.buf), ALU.mult)
                fw.reduce(hnum[:, CH * ci:CH * ci + CH], oc[:], ALU.add)
            fw.ts(dve, hnum[:], hnum[:], den[:, 2:3], None, ALU.mult)
            fw.dma(sp, h_scr[:, :], hnum[:])
            hs = sb("hs", [NS, NHB, EB], F32)
            hs2 = sb("hs2", [NS, NHB, EB], F32)
            fw.dma(sp, hs.re("p h e -> p (h e)"), V(h_scr.h.ap().rearrange("(b x) e -> b (x e)", b=NS), h_scr.buf))
            hss = sb("hss", [NS, 2, NHB], F32)
            fw.tt(dve, hs2[:], hs[:], hs[:], ALU.mult)
            fw.reduce(hss[:, 0, :], hs2[:], ALU.add)
            fw.ts(dve, hss[:, 1, :], hss[:, 0, :], 1.0 / EB, EPS, ALU.mult, ALU.add)
            fw.activation(hss[:, 1, :], hss[:, 1, :], AF.Ln)
            fw.activation(hss[:, 1, :], hss[:, 1, :], AF.Exp, scale=-0.5)
            fw.tt(dve, hs2[:], hs[:], V(hss.h[:, 1, :].unsqueeze(2).to_broadcast([NS, NHB, EB]), hss.buf), ALU.mult)
            mgr = sb("mgr", [NS, 2, DB], F32)
            fw.dma(sp, mgr[:, 0, :], V(mhg_d.h.ap().rearrange("d o -> (d o)").partition_broadcast(NS), mhg_d.buf))
            fw.dma(sp, mgr[:, 1, :], V(skip_d.h.ap().rearrange("d o -> (d o)").partition_broadcast(NS), skip_d.buf))
            smi = sb("smi", [NS, 2 * DB + 3 * DA], F32)
            fw.dma(sp, smi[:], s_misc[:, :])
            ob_s = sb("ob_s", [NS, DB], F32)
            hs2f = hs2.re("p h e -> p (h e)")
            fw.tt(dve, ob_s[:], hs2f, mgr[:, 0, :], ALU.mult)
            fw.tt(dve, hs2f, smi[:, DB:2 * DB], mgr[:, 1, :], ALU.mult)
            fw.tt(dve, ob_s[:], ob_s[:], hs2f, ALU.add)
            ob_b = sb("ob_b", [NS, DB], BF16)
            fw.tt(dve, ob_b[:], ob_s[:], smi[:, 0:DB], ALU.mult)
            bT3 = pb()
            pT3 = bT3.bitcast(BF16)
            for c in range(5):
                fw.transpose(pT3[:, 128 * c:128 * c + NS], ob_b[:, 128 * c:128 * c + 128], ident_b[0:NS, 0:NS], inc=(c == 4))
            for c in range(5):
                fw.copy(act, outbT[:, c, SEG:SEG + NS], pT3[:, 128 * c:128 * c + NS])

            ohs = sb("ohs_s", [32, 3, 128], F32)
            e16 = sb("e16_s", [NS, NS * 128], F32)
            ecol = sb("ecol_s", [128, NS * NS], F32)
            fw.dma(sp, ohs[:], V(ohs_d.h.ap().rearrange("b k i -> k b i"), ohs_d.buf))
            fw.dma(sp, e16[:], e16_d[:, :])
            fw.dma(sp, ecol[:], ecol_d[:, :])
            bS = sb("bS", [128, 3, 6], F32)
            for b_ in range(3):
                bk_ = pb()
                fw.mm(bk_[:, 0:6], ohs[:, b_, :], rb[:], start=True, stop=True)
                fw.copy(dve, bS[:, b_, :], bk_[:, 0:6])
            qa = sb("qa", [NS, DA], F32)
            ka = sb("ka", [NS, DA], F32)
            va = sb("va_s", [NS, DA], F32)
            fw.copy(dve, qa[:], smi[:, 2 * DB:2 * DB + DA])
            fw.copy(dve, ka[:], smi[:, 2 * DB + DA:2 * DB + 2 * DA])
            fw.copy(dve, va[:], smi[:, 2 * DB + 2 * DA:2 * DB + 3 * DA])
            Kg = [sb("Kg%d" % i, [128, 6, 64], F32) for i in range(2)]
            Vg = [sb("Vg%d" % i, [128, 6, 64], F32) for i in range(2)]
            prd = sb("prd", [128, 6, 64], F32)
            pvz = [sb("pvz%d" % i, [128, DA + 6], F32) for i in range(2)]
            sS = sb("sS", [128, 6], F32)
            bacc = pb()
            reserved.append(bacc)
            nn = 0
            for b in range(NS):
                bq_ = pb()
                fw.mm(bq_[:, 0:DA], e16[:, 128 * b:128 * b + 128], qa[:], start=True, stop=True)
                for b_, dil in enumerate((1, 4, 16)):
                    pq = nn % 2
                    fw.dma(sp, Kg[pq].re("p h e -> p (h e)"), cwk_d[b, 2048 - 128 * dil:2048:dil, :])
                    fw.dma(sp, Vg[pq].re("p h e -> p (h e)"), cwv_d[b, 2048 - 128 * dil:2048:dil, :])
                    fw.tt(dve, prd.re("p h e -> p (h e)"), Kg[pq].re("p h e -> p (h e)"), bq_[:, 0:DA], ALU.mult)
                    fw.reduce(sS[:], prd[:], ALU.add)
                    fw.tt(dve, sS[:], sS[:], bS[:, b_, :], ALU.add)
                    pz = pvz[pq]
                    fw.activation(pz[:, DA:DA + 6], sS[:], AF.Exp)
                    fw.tt(dve, V(pz.h[:, 0:DA].rearrange("p (h e) -> p h e", h=6), pz.buf), Vg[pq][:],
                          V(pz.h[:, DA:DA + 6].unsqueeze(2).to_broadcast([128, 6, 64]), pz.buf), ALU.mult)
                    fw.mm(bacc[0:NS, 0:DA + 6], ecol[:, NS * b:NS * b + NS], pz[:], start=(nn == 0), stop=(nn == 3 * NS - 1), inc=True)
                    nn += 1
            reserved.remove(bacc)
            b0r = sb("b0r", [NS, 6], F32)
            fw.dma(sp, b0r[:], V(rel_bias_d.h.ap()[0:1, :].rearrange("o h -> (o h)").partition_broadcast(NS), rel_bias_d.buf))
            qk = sb("qk", [NS, 6, 64], F32)
            s0 = sb("s0", [NS, 6], F32)
            fw.tt(dve, qk.re("p h e -> p (h e)"), qa[:], ka[:], ALU.mult)
            fw.reduce(s0[:], qk[:], ALU.add)
            fw.tt(dve, s0[:], s0[:], b0r[:], ALU.add)
            fw.activation(s0[:], s0[:], AF.Exp)
            fw.ts(dve, s0[:], s0[:], 3.0, None, ALU.mult)
            oacc = sb("oacc", [NS, DA + 6], F32)
            fw.copy(dve, oacc[:], bacc[0:NS, 0:DA + 6])
            fw.tt(dve, qk[:], V(va.h[:].rearrange("p (h e) -> p h e", h=6), va.buf),
                  V(s0.h[:].unsqueeze(2).to_broadcast([NS, 6, 64]), s0.buf), ALU.mult)
            fw.tt(dve, oacc[:, 0:DA], oacc[:, 0:DA], qk.re("p h e -> p (h e)"), ALU.add)
            fw.tt(dve, oacc[:, DA:DA + 6], oacc[:, DA:DA + 6], s0[:], ALU.add)
            rzs = sb("rzs", [NS, 6], F32)
            fw.recip(rzs[:], oacc[:, DA:DA + 6])
            fw.tt(dve, qk[:], V(oacc.h[:, 0:DA].rearrange("p (h e) -> p h e", h=6), oacc.buf),
                  V(rzs.h[:].unsqueeze(2).to_broadcast([NS, 6, 64]), rzs.buf), ALU.mult)
            qkf = qk.re("p h e -> p (h e)")
            sa2 = sb("sa2", [NS, 2], F32)
            jks = sb("jks", [NS, DA], F32)
            fw.stt(jks[:], qkf, 1.0, qkf, ALU.mult, ALU.mult, accum_out=sa2[:, 0:1])
            fw.ts(dve, sa2[:, 1:2], sa2[:, 0:1], 1.0 / DA, EPS, ALU.mult, ALU.add)
            fw.activation(sa2[:, 1:2], sa2[:, 1:2], AF.Ln)
            fw.activation(sa2[:, 1:2], sa2[:, 1:2], AF.Exp, scale=-0.5)
            oab = sb("oab", [NS, DA], BF16)
            fw.ts(dve, oab[:], qkf, sa2[:, 1:2], None, ALU.mult)
            bT4 = pb()
            pT4 = bT4.bitcast(BF16)
            for c in range(3):
                fw.transpose(pT4[:, 128 * c:128 * c + NS], oab[:, 128 * c:128 * c + 128], ident_b[0:NS, 0:NS], inc=(c == 2))
            for c in range(3):
                fw.ts(dve, outaT[:, c, SEG:SEG + NS], pT4[:, 128 * c:128 * c + NS], ag[:, c:c + 1], None, ALU.mult)

    else:
        fw.memset(dve, outaT[:], 0.0)

    pop()
    pop()
    push()
    x1_scr = dscr("x1_scr", [SEG + 128, D], F32)
    h2_scr = dscr("h2_scr", [SEG // 128 + 1, 128, D], BF16)
    w_out_b = sb("w_out_b", [128, 8, D], BF16)
    wst1 = [sb("wst1_%d" % i, [128, D], F32) for i in range(2)]
    for c in range(8):
        st = wst1[c % 2]
        fw.dma(sp, st[:], w_out_d[128 * c:128 * c + 128, :])
        fw.copy(act if c % 2 else dve, w_out_b[:, c, :], st[:])
    xt1 = [sb("xt1_%d" % i, [128, D], F32) for i in range(2)]
    x1t = [sb("x1t_%d" % i, [128, D], F32) for i in range(2)]
    xn1 = sb("xn1", [128, D], BF16)
    junk1 = sb("junk1", [128, D], BF16)
    h2t = [sb("h2t_%d" % i, [128, D], BF16) for i in range(2)]
    ss1 = sb("ss1", [128, 2], F32)
    NTC = SEG // 128 + (1 if do_sample else 0)
    for mt in range(NTC):
        p = mt % 2
        nr = 128 if mt < SEG // 128 else NS
        if mt < SEG // 128:
            fw.dma(sp, xt1[p][:], xw[128 * (MAIN0 + mt):128 * (MAIN0 + mt) + 128, :])
        else:
            fw.dma(sp, xt1[p][0:nr, :], xs_d[:, :])
        b0, b1_ = pb(), pb()
        for half, bk_ in enumerate((b0, b1_)):
            for c in range(8):
                src_ = outaT[:, c, 128 * mt:128 * mt + nr] if c < 3 else outbT[:, c - 3, 128 * mt:128 * mt + nr]
                fw.mm(bk_[0:nr, 0:512], src_, w_out_b[:, c, 512 * half:512 * half + 512], start=(c == 0), stop=(c == 7))
        fw.tt(dve, x1t[p][0:nr, 0:512], b0[0:nr, 0:512], xt1[p][0:nr, 0:512], ALU.add)
        fw.tt(dve, x1t[p][0:nr, 512:1024], b1_[0:nr, 0:512], xt1[p][0:nr, 512:1024], ALU.add)
        fw.dma(sp, x1_scr[128 * mt:128 * mt + nr, :], x1t[p][0:nr, :])
        fw.stt(junk1[0:nr, :], x1t[p][0:nr, :], 1.0, x1t[p][0:nr, :], ALU.mult, ALU.mult, accum_out=ss1[0:nr, 0:1])
        fw.ts(dve, ss1[0:nr, 1:2], ss1[0:nr, 0:1], 1.0 / D, EPS, ALU.mult, ALU.add)
        fw.activation(ss1[0:nr, 1:2], ss1[0:nr, 1:2], AF.Ln)
        fw.activation(ss1[0:nr, 1:2], ss1[0:nr, 1:2], AF.Exp, scale=-0.5)
        fw.ts(dve, xn1[0:nr, :], x1t[p][0:nr, :], ss1[0:nr, 1:2], None, ALU.mult)
        bT_ = pb()
        pT_ = bT_.bitcast(BF16)
        for c in range(8):
            fw.transpose(pT_[:, 128 * c:128 * c + nr], xn1[0:nr, 128 * c:128 * c + 128], ident_b[0:nr, 0:nr], inc=(c == 7))
        fw.copy(act, h2t[p][:], pT_[:, 0:1024])
        fw.dma(sp, h2_scr[mt], h2t[p][:])

    pop()
    pop()
    push()
    w1b = sb("w1b", [128, 8, DFF], BF16)
    w2b = sb("w2b", [128, 32, D], BF16)
    g2 = sb("g2", [128, 8], F32)
    fing = sb("fing", [128, D], F32)
    for c in range(8):
        fw.dma(sp, g2[:, c:c + 1], norm2_g_d[128 * c:128 * c + 128, :])
    fw.dma(sp, fing[:], V(final_g_d.h.ap().to_broadcast([128, D]), final_g_d.buf))
    wst2 = [sb("wst2_%d" % i, [128, 2048], F32) for i in range(2)]
    n_ = 0
    for c in range(8):
        for hf in range(2):
            st = wst2[n_ % 2]
            fw.dma(sp, st[:], w_ff1_d[128 * c:128 * c + 128, 2048 * hf:2048 * hf + 2048])
            fw.scale_copy(act if n_ % 2 else dve, w1b[:, c, 2048 * hf:2048 * hf + 2048], st[:], g2[:, c:c + 1])
            n_ += 1
    for c in range(16):
        st = wst2[n_ % 2]
        fw.dma(sp, st.re("p (a b) -> p a b", a=2), V(w_ff2_d.h.ap()[256 * c:256 * c + 256, :].rearrange("(a p) d -> p a d", a=2), w_ff2_d.buf))
        fw.copy(act if n_ % 2 else dve, w2b[:, 2 * c:2 * c + 2, :], st.re("p (a b) -> p a b", a=2))
        n_ += 1
    GT_ = 256
    h2g = [sb("h2g_%d" % i, [128, 8, GT_], BF16) for i in range(2)]
    aT = sb("aT", [128, 32, GT_], BF16)
    fw.memset(dve, h2g[0][:], 0.0)
    fw.memset(dve, h2g[1][:], 0.0)
    rl = [sb("rl_%d" % i, [128, 2, GT_], BF16) for i in range(2)]
    x1l = [sb("x1l_%d" % i, [128, D], F32) for i in range(2)]
    x2t = [sb("x2t_%d" % i, [128, D], F32) for i in range(2)]
    yt = [sb("yt_%d" % i, [128, D], F32) for i in range(2)]
    junk2 = sb("junk2", [128, D], BF16)
    ss2 = sb("ss2", [128, 2], F32)
    groups = [[g_ * (GT_ // 128) + t_ for t_ in range(GT_ // 128)] for g_ in range(SEG // GT_)]
    if do_sample:
        groups.append([SEG // 128])
    for g_, tiles_ in enumerate(groups):
        p = g_ % 2
        for t_, mt in enumerate(tiles_):
            fw.dma(sp, h2g[p][:, :, 128 * t_:128 * t_ + 128], V(h2_scr.h.ap()[mt].rearrange("p (c t) -> p c t", c=8), h2_scr.buf))
        for j2 in range(16):
            bk_ = pb()
            for jj in range(2):
                j = 2 * j2 + jj
                for c in range(8):
                    fw.mm(bk_[:, GT_ * jj:GT_ * jj + GT_], w1b[:, c, 128 * j:128 * j + 128], h2g[p][:, c, :], start=(c == 0), stop=(c == 7))
            r_ = rl[j2 % 2]
            fw.activation(r_.re("p a t -> p (a t)"), bk_[:, 0:2 * GT_], AF.Relu)
            fw.tt(dve, aT[:, 2 * j2:2 * j2 + 2, :], r_[:], r_[:], ALU.mult)
        for t_, mt in enumerate(tiles_):
            q = mt % 2
            nr = 128 if mt < SEG // 128 else NS
            fw.dma(sp, x1l[q][0:nr, :], x1_scr[128 * mt:128 * mt + nr, :])
            b0, b1_ = pb(), pb()
            for half, bk_ in enumerate((b0, b1_)):
                for c in range(32):
                    fw.mm(bk_[0:nr, 0:512], aT[:, c, 128 * t_:128 * t_ + nr], w2b[:, c, 512 * half:512 * half + 512], start=(c == 0), stop=(c == 31))
            fw.tt(dve, x2t[q][0:nr, 0:512], b0[0:nr, 0:512], x1l[q][0:nr, 0:512], ALU.add)
            fw.tt(dve, x2t[q][0:nr, 512:1024], b1_[0:nr, 0:512], x1l[q][0:nr, 512:1024], ALU.add)
            fw.stt(junk2[0:nr, :], x2t[q][0:nr, :], 1.0, x2t[q][0:nr, :], ALU.mult, ALU.mult, accum_out=ss2[0:nr, 0:1])
            fw.ts(dve, ss2[0:nr, 1:2], ss2[0:nr, 0:1], 1.0 / D, EPS, ALU.mult, ALU.add)
            fw.activation(ss2[0:nr, 1:2], ss2[0:nr, 1:2], AF.Ln)
            fw.activation(ss2[0:nr, 1:2], ss2[0:nr, 1:2], AF.Exp, scale=-0.5)
            fw.stt(yt[q][0:nr, :], x2t[q][0:nr, :], ss2[0:nr, 1:2], fing[0:nr, :], ALU.mult, ALU.mult)
            if mt < SEG // 128:
                fw.dma(sp, o_y[128 * mt:128 * mt + 128, :], yt[q][:])
            else:
                fw.dma(sp, o_ys[:, :], yt[q][0:nr, :])
    pop()
    pop()
    fw.finish(sp)
    fw.emit()
    return nc


def _get_nc():
    if "nc" not in _NC_CACHE:
        _NC_CACHE["nc"] = build_nc()
    return _NC_CACHE["nc"]


def _t5_bucket_np(dist):
    dist = np.asarray(dist, np.int64)
    df = np.maximum(dist, 1).astype(np.float32)
    large = 16 + (np.log(df / np.float32(16)) / np.float32(np.log(2048 / 16)) * np.float32(16)).astype(np.int32)
    large = np.minimum(large, 31)
    return np.where(dist < 16, dist, large)


def _consts():
    c = {}
    k = np.arange(128)[:, None]
    q = np.arange(128)[None, :]
    c["negmask"] = np.where(k > q, np.float32(MASKV), np.float32(0.0)).astype(np.float32)
    sel = np.zeros((4, 4, 128), np.float32)
    for h in range(4):
        sel[h, h, :] = 1.0
    c["sel"] = sel.reshape(4, 512)
    ohb = np.zeros((3, 32, 512), np.float32)
    mv = np.full((3, 1, 512), np.float32(MASKV), np.float32)
    for bi, dil in enumerate((1, 4, 16)):
        for j in range(0, 129):
            ohb[bi, int(_t5_bucket_np(j * dil)), j + 127] = 1.0
            mv[bi, 0, j + 127] = 0.0
    c["ohb"] = ohb
    c["mvec"] = mv
    ohs = np.zeros((3, 32, 128), np.float32)
    for bi, dil in enumerate((1, 4, 16)):
        for i in range(128):
            ohs[bi, int(_t5_bucket_np((128 - i) * dil)), i] = 1.0
    c["ohs"] = ohs
    e16 = np.zeros((NS, NS, 128), np.float32)
    ecol = np.zeros((128, NS, NS), np.float32)
    for b in range(NS):
        e16[b, b, :] = 1.0
        ecol[:, b, b] = 1.0
    c["e16"] = e16.reshape(NS, NS * 128)
    c["ecol"] = ecol.reshape(128, NS * NS)
    return c


def make_in_maps(x_prompt, x_sample, cache_win_k, cache_win_v, state_conv, state_C, state_n, state_m,
                 rel_bias, norm1_g, w_in, gate_bias, conv_w, conv_b, wq_head, wk_head,
                 attn_out_g, mh_norm_g, skip, w_out, norm2_g, w_ff1, w_ff2, final_g):
    f = lambda a: np.ascontiguousarray(np.asarray(a, dtype=np.float32))
    x_prompt = f(x_prompt)
    cst = _consts()
    shared = {
        "w_in": f(w_in[0]), "norm1_g": f(norm1_g[0]).reshape(D, 1), "gate_bias": f(gate_bias[0]).reshape(8, 1),
        "conv_wT": f(np.asarray(conv_w[0]).T), "conv_w": f(conv_w[0]), "conv_b": f(conv_b[0]).reshape(DB, 1),
        "wq": f(wq_head[0]), "wk": f(wk_head[0]), "mhg": f(mh_norm_g[0]).reshape(DB, 1),
        "skip": f(skip[0]).reshape(DB, 1),
        "rel_bias": f(rel_bias), "attn_g": f(attn_out_g[0]).reshape(DA, 1), "w_out": f(w_out[0]),
        "norm2_g": f(norm2_g[0]).reshape(D, 1), "w_ff1": f(w_ff1[0]), "w_ff2": f(w_ff2[0]),
        "final_g": f(final_g).reshape(1, D),
    }
    shared.update(cst)
    in_maps = []
    for c in range(NCORES):
        b, s = c // 4, c % 4
        lo = SEG * s - (WIN - SEG)
        xwin = np.zeros((WIN, D), np.float32)
        a0 = max(lo, 0)
        xwin[a0 - lo:] = x_prompt[b, a0:lo + WIN]
        valid = np.array([(lo + 128 * t) >= 0 for t in range(NT)])
        m = dict(shared)
        m["xw"] = xwin
        m["tmA"] = np.tile(np.where(valid, 0.0, NEG).astype(np.float32)[None, :], (4, 1))
        m["tmV"] = np.tile(np.where(valid, -1.0, 0.0).astype(np.float32)[None, :], (4, 1))
        m["tmO"] = np.tile(np.where(valid, 1.0, 0.0).astype(np.float32)[None, :], (128, 1))
        sl = slice(NS * c, NS * c + NS)
        m["xs"] = f(x_sample[sl, 0])
        m["cwk"] = f(cache_win_k[0, sl]).reshape(NS, 2048, DA)
        m["cwv"] = f(cache_win_v[0, sl]).reshape(NS, 2048, DA)
        m["s_conv"] = f(state_conv[0, sl])
        m["s_C"] = f(state_C[0, sl])
        m["s_n"] = f(state_n[0, sl])
        m["s_m"] = f(state_m[0, sl])
        in_maps.append(m)
    return in_maps


def _filter(nc_inputs, m):
    return {k: v for k, v in m.items() if k in nc_inputs}


def kernel(**inputs):
    nc = _get_nc()
    in_maps = make_in_maps(**inputs)
    names = _NC_CACHE["in_names"]
    in_maps = [_filter(names, m) for m in in_maps]
    res = run_bass_kernel_spmd(nc, in_maps, core_ids=list(range(NCORES)))
    R = res.results
    return assemble(R)


def assemble(R):
    f = np.float32
    y_prompt = np.stack([np.concatenate([R[4 * b + s]["o_y"] for s in range(4)], 0) for b in range(2)]).astype(f)
    y_sample = np.concatenate([R[c]["o_ys"] for c in range(NCORES)], 0).reshape(128, 1, D).astype(f)
    last = [3, 7]
    p_k = np.stack([R[c]["o_wk"] for c in last]).reshape(1, 2, 2048, 6, 64).astype(f)
    p_v = np.stack([R[c]["o_wv"] for c in last]).reshape(1, 2, 2048, 6, 64).astype(f)
    p_conv = np.stack([R[c]["o_conv"] for c in last]).reshape(1, 2, 3, DB).astype(f)
    p_C = np.stack([R[c]["o_C"] for c in last]).reshape(1, 2, NHB, EB, EB).astype(f)
    p_n = np.stack([R[c]["o_n"] for c in last]).reshape(1, 2, NHB, EB).astype(f)
    p_m = np.stack([R[c]["o_m"] for c in last]).reshape(1, 2, NHB).astype(f)
    cat = lambda k: np.concatenate([R[c][k] for c in range(NCORES)], 0)
    s_k = cat("o_swk").reshape(1, 128, 2048, 6, 64).astype(f)
    s_v = cat("o_swv").reshape(1, 128, 2048, 6, 64).astype(f)
    s_conv = cat("o_sconv").reshape(1, 128, 3, DB).astype(f)
    s_C = cat("o_sC").reshape(1, 128, NHB, EB, EB).astype(f)
    s_n = cat("o_sn").reshape(1, 128, NHB, EB).astype(f)
    s_m = cat("o_sm").reshape(1, 128, NHB).astype(f)
    return (y_prompt, y_sample, p_k, p_v, p_conv, p_C, p_n, p_m, s_k, s_v, s_conv, s_C, s_n, s_m)
```

```python
import numpy as np
import os
from contextlib import ExitStack
VAR = ''
import concourse.bass as bass
import concourse.mybir as mybir
from concourse.bass_utils import run_bass_kernel_spmd

F32 = mybir.dt.float32
BF16 = mybir.dt.bfloat16
AF = mybir.ActivationFunctionType
ALU = mybir.AluOpType
AX = mybir.AxisListType

D = 1024
DIN = 2440
DA = 384
DB = 640
NHB = 4
EB = 160
DFF = 4096
NCORES = 8
SEG = 2048
WIN = 8192
NT = WIN // 128
MAIN0 = 48
HALO0 = 32
NS = 16
EPS = 1e-6
NEG = -1e30
MASKV = -30000.0


class Eng:
    def __init__(self, fw, name, handle, sem):
        self.fw, self.name, self.h, self.sem = fw, name, handle, sem
        self.count = 0
        self.prog = []
        self.waited = {}
        self.dsems = []
        self.dtot = []
        self.dnext = 0

    def wait(self, sem, val):
        if val <= 0:
            return
        k = id(sem)
        if self.waited.get(k, 0) >= val:
            return
        self.waited[k] = val
        self.prog.append(lambda h, sem=sem, val=val: h.wait_ge(sem, val))


class Buf:
    def __init__(self, name):
        self.name = name
        self.w = {}
        self.r = {}
        self.excl = False


class V:
    def __init__(self, ap, buf):
        self.ap = ap
        self.bufs = list(buf) if isinstance(buf, (list, tuple)) else [buf]

    @property
    def buf(self):
        return self.bufs if len(self.bufs) > 1 else self.bufs[0]


class T:
    def __init__(self, handle, name):
        self.h = handle
        self._buf = Buf(name)
        self.subs = None

    @property
    def buf(self):
        return self.subs if self.subs else self._buf

    def split(self, n):
        self.subs = [Buf("%s.%d" % (self._buf.name, k)) for k in range(n)]
        return self

    def sub(self, k):
        t = T.__new__(T)
        t.h = self.h
        t._buf = self.subs[k]
        t.subs = None
        return t

    def __getitem__(self, key):
        return V(self.h[key], self.buf)

    def ap(self, offset, pat):
        return V(bass.AP(self.h, offset, pat), self.buf)

    def re(self, pat, **kw):
        return V(self.h.ap().rearrange(pat, **kw) if hasattr(self.h, "ap") else self.h[:].rearrange(pat, **kw), self.buf)

    def bitcast(self, dt):
        t = T.__new__(T)
        t.h = self.h.bitcast(dt)
        t._buf = self._buf
        t.subs = self.subs
        return t


class FW:
    def __init__(self, nc):
        self.nc = nc
        mk = lambda n, h: Eng(self, n, h, nc.alloc_semaphore("sem_" + n))
        self.pe = mk("pe", nc.tensor)
        self.act = mk("act", nc.scalar)
        self.dve = mk("dve", nc.vector)
        self.pool = mk("pool", nc.gpsimd)
        self.sp = mk("sp", nc.sync)
        self.engs = [self.pe, self.act, self.dve, self.pool, self.sp]
        for e in (self.sp, self.pool, self.act):
            n = 12
            e.dsems = [nc.alloc_semaphore("dsem_%s_%d" % (e.name, i)) for i in range(n)]
            e.dtot = [0] * n
        self.same_engine_sync = True

    def _deps(self, eng, reads, writes):
        for v in reads:
            for b in v.bufs:
                for sem, val in b.w.values():
                    self._w(eng, sem, val, True)
                if b.excl:
                    for sem, val in b.r.values():
                        self._w(eng, sem, val, False)
        for v in writes:
            for b in v.bufs:
                for sem, val in list(b.w.values()) + list(b.r.values()):
                    self._w(eng, sem, val, False)

    def _w(self, eng, sem, val, raw):
        if sem is eng.sem:
            if eng is self.pe or not self.same_engine_sync or not raw:
                return
        eng.wait(sem, val)

    def _mark(self, sem, val, reads, writes):
        for v in reads:
            for b in v.bufs:
                b.r[id(sem)] = (sem, val)
        for v in writes:
            for b in v.bufs:
                b.w[id(sem)] = (sem, val)

    def op(self, eng, fn, reads, writes, inc=True):
        reads = [v for v in reads if isinstance(v, V)]
        self._deps(eng, reads, writes)
        if inc:
            eng.count += 1
            c = eng.count
            eng.prog.append(lambda h, fn=fn, s=eng.sem: fn(h).then_inc(s, 1))
            self._mark(eng.sem, c, reads, writes)
        else:
            eng.prog.append(lambda h, fn=fn: fn(h))

    def dma(self, q, out, in_, **kw):
        self._deps(q, [in_], [out])
        j = q.dnext
        q.dnext = (j + 1) % len(q.dsems)
        sem = q.dsems[j]
        q.wait(sem, q.dtot[j])
        q.dtot[j] += 16
        tot = q.dtot[j]
        q.prog.append(lambda h, o=out.ap, i=in_.ap, s=sem, kw=kw: h.dma_start(out=o, in_=i, **kw).then_inc(s, 16))
        self._mark(sem, tot, [in_], [out])

    def finish(self, eng, skip=()):
        for e in self.engs:
            if e is not eng:
                eng.wait(e.sem, e.count)
            if e in skip:
                continue
            for s, t in zip(e.dsems, e.dtot):
                eng.wait(s, t)

    def barrier(self):
        for e in self.engs:
            self.finish(e, skip=(self.pool,))

    def emit(self):
        nc = self.nc
        with nc.Block() as block:
            @block.tensor
            def _(h):
                for f in self.pe.prog:
                    f(h)

            @block.scalar
            def _(h):
                for f in self.act.prog:
                    f(h)

            @block.vector
            def _(h):
                for f in self.dve.prog:
                    f(h)

            @block.gpsimd
            def _(h):
                for f in self.pool.prog:
                    f(h)

            @block.sync
            def _(h):
                for f in self.sp.prog:
                    f(h)

    def mm(self, out, lhsT, rhs, start=True, stop=True, inc=None):
        if inc is None:
            inc = stop
        self.op(self.pe, lambda h, o=out.ap, l=lhsT.ap, r=rhs.ap: h.matmul(o, l, r, start=start, stop=stop, skip_group_check=True),
                [lhsT, rhs], [out], inc=inc)

    def transpose(self, out, in_, ident, inc=True):
        self.op(self.pe, lambda h, o=out.ap, i=in_.ap, d=ident.ap: h.transpose(o, i, d), [in_, ident], [out], inc=inc)

    def activation(self, out, in_, func, bias=0.0, scale=1.0, accum_out=None, eng=None):
        rd = [in_, bias, scale]
        wr = [out] + ([accum_out] if accum_out is not None else [])
        b = bias.ap if isinstance(bias, V) else bias
        s = scale.ap if isinstance(scale, V) else scale
        kw = {}
        if accum_out is not None:
            kw["accum_out"] = accum_out.ap
        self.op(self.act, lambda h, o=out.ap, i=in_.ap: h.activation(o, i, func, bias=b, scale=s, **kw), rd, wr)

    def tt(self, eng, out, in0, in1, op):
        self.op(eng, lambda h, o=out.ap, a=in0.ap, b=in1.ap: h.tensor_tensor(o, a, b, op), [in0, in1], [out])

    def ts(self, eng, out, in0, s1, s2, op0, op1=None, accum_out=None):
        a1 = s1.ap if isinstance(s1, V) else s1
        a2 = s2.ap if isinstance(s2, V) else s2
        kw = {}
        if op1 is not None:
            kw["op1"] = op1
        wr = [out]
        if accum_out is not None:
            kw["accum_out"] = accum_out.ap
            wr.append(accum_out)
        self.op(eng, lambda h, o=out.ap, a=in0.ap: h.tensor_scalar(o, a, a1, a2, op0, **kw), [in0, s1, s2], wr)

    def stt(self, out, in0, scalar, in1, op0, op1, accum_out=None):
        sc = scalar.ap if isinstance(scalar, V) else scalar
        kw = {}
        wr = [out]
        if accum_out is not None:
            kw["accum_out"] = accum_out.ap
            wr.append(accum_out)
        self.op(self.dve, lambda h, o=out.ap, a=in0.ap, b=in1.ap: h.scalar_tensor_tensor(o, a, sc, b, op0, op1, **kw),
                [in0, scalar, in1], wr)

    def scale_copy(self, eng, out, in_, sc):
        if eng is self.act:
            self.activation(out, in_, AF.Copy, scale=sc)
        else:
            self.ts(eng, out, in_, sc, None, ALU.mult)

    def copy(self, eng, out, in_):
        if eng is self.act:
            self.op(eng, lambda h, o=out.ap, i=in_.ap: h.copy(o, i), [in_], [out])
        else:
            self.op(eng, lambda h, o=out.ap, i=in_.ap: h.tensor_copy(o, i), [in_], [out])

    def memset(self, eng, out, val):
        self.op(eng, lambda h, o=out.ap: h.memset(o, val), [], [out])

    def reduce(self, out, in_, op, axis=AX.X):
        self.op(self.dve, lambda h, o=out.ap, i=in_.ap: h.tensor_reduce(o, i, axis, op), [in_], [out])

    def recip(self, out, in_):
        self.op(self.dve, lambda h, o=out.ap, i=in_.ap: h.reciprocal(o, i), [in_], [out])

    def scan(self, out, d0, d1, initial, op0, op1):
        ini = initial.ap if isinstance(initial, V) else initial
        self.op(self.dve, lambda h, o=out.ap, a=d0.ap, b=d1.ap: h.tensor_tensor_scan(o, a, b, ini, op0, op1),
                [d0, d1, initial], [out])


_NC_CACHE = {}


def build_nc(stage=99, prefix_super=True, dbg=False, do_sample=True, do_attn=True, do_ffn=True, tile_list=None, do_cout=True):
    nc = bass.Bass("TRN2", target_bir_lowering=False)
    fw = FW(nc)
    pe, act, dve, pool, sp = fw.pe, fw.act, fw.dve, fw.pool, fw.sp

    in_names = _NC_CACHE.setdefault("in_names", set())

    def din(name, shape, dt=F32):
        in_names.add(name)
        return T(nc.dram_tensor(name, list(shape), dt, kind="ExternalInput"), name)

    def dout(name, shape, dt=F32):
        return T(nc.dram_tensor(name, list(shape), dt, kind="ExternalOutput"), name)

    def dscr(name, shape, dt=F32):
        return T(nc.dram_tensor(name, list(shape), dt, kind="Internal"), name)

    stacks = []

    def push():
        stacks.append(ExitStack())

    def pop():
        fw.barrier()
        stacks.pop().close()

    def sb(name, shape, dt=F32):
        return T(stacks[-1].enter_context(nc.sbuf_tensor(name, list(shape), dt)), name)

    push()

    xw = din("xw", [WIN, D])
    tmA = din("tmA", [4, NT])
    tmV = din("tmV", [4, NT])
    tmO = din("tmO", [128, NT])
    negmask_d = din("negmask", [128, 128])
    sel_d = din("sel", [4, 4 * 128])
    w_in = din("w_in", [D, DIN])
    norm1_g = din("norm1_g", [D, 1])
    gate_bias = din("gate_bias", [8, 1])
    conv_wT = din("conv_wT", [DB, 4])
    conv_b = din("conv_b", [DB, 1])
    wq_d = din("wq", [NHB, EB, EB])
    wk_d = din("wk", [NHB, EB, EB])
    mhg_d = din("mhg", [DB, 1])
    skip_d = din("skip", [DB, 1])

    rel_bias_d = din("rel_bias", [32, 6])
    ohb_d = din("ohb", [3, 32, 512])
    mvec_d = din("mvec", [3, 1, 512])
    attn_g_d = din("attn_g", [DA, 1])
    w_out_d = din("w_out", [D, D])
    norm2_g_d = din("norm2_g", [D, 1])
    w_ff1_d = din("w_ff1", [D, DFF])
    w_ff2_d = din("w_ff2", [DFF, D])
    final_g_d = din("final_g", [1, D])

    xs_d = din("xs", [NS, D])
    cwk_d = din("cwk", [NS, 2048, DA])
    cwv_d = din("cwv", [NS, 2048, DA])
    s_conv_d = din("s_conv", [NS, 3, DB])
    s_C_d = din("s_C", [NS, NHB, EB, EB])
    s_n_d = din("s_n", [NS, NHB, EB])
    s_m_d = din("s_m", [NS, NHB])
    conv_w_d = din("conv_w", [4, DB])
    ohs_d = din("ohs", [3, 32, 128])
    e16_d = din("e16", [NS, NS * 128])
    ecol_d = din("ecol", [128, NS * NS])

    o_y = dout("o_y", [SEG, D])
    o_ys = dout("o_ys", [NS, D])
    o_swk = dout("o_swk", [NS, 2048, DA])
    o_swv = dout("o_swv", [NS, 2048, DA])
    o_sconv = dout("o_sconv", [NS, 3, DB])
    o_sC = dout("o_sC", [NS, NHB, EB, EB])
    o_sn = dout("o_sn", [NS, NHB * EB])
    o_sm = dout("o_sm", [NS, NHB])
    o_wk = dout("o_wk", [SEG, DA])
    o_wv = dout("o_wv", [SEG, DA])
    o_conv = dout("o_conv", [3, DB])
    o_C = dout("o_C", [NHB, EB, EB])
    o_n = dout("o_n", [NHB, EB])
    o_m = dout("o_m", [NHB, 1])
    if dbg:
        o_dbg = dout("o_dbg", [SEG, DB])

    ps = [T(nc.alloc_psum_tensor("ps%d" % i, [128, 512], F32), "ps%d" % i) for i in range(8)]
    for b_ in ps:
        b_._buf.excl = True
    psn = [0]

    reserved = []

    def pb():
        while True:
            b = ps[psn[0] % 8]
            psn[0] += 1
            if b not in reserved:
                return b

    ident_f = sb("ident_f", [128, 128], F32)
    ident_b = sb("ident_b", [128, 128], BF16)
    iot = sb("iot", [128, 128], F32)
    fw.op(pool, lambda h, o=iot[:].ap: h.iota(o, [[1, 128]], base=0, channel_multiplier=-1,
                                              allow_small_or_imprecise_dtypes=True), [], [iot[:]])
    fw.ts(dve, ident_f[:], iot[:], 0.0, None, ALU.is_equal)
    fw.copy(dve, ident_b[:], ident_f[:])
    negmask = sb("negmask_s", [128, 128], F32)
    fw.dma(sp, negmask[:], negmask_d[:])
    sel = sb("sel_s", [4, 4 * 128], F32)
    fw.dma(sp, sel[:], sel_d[:])
    tmA_s = sb("tmA_s", [4, NT], F32)
    tmV_s = sb("tmV_s", [4, NT], F32)
    tmO_s = sb("tmO_s", [128, NT], F32)
    fw.dma(sp, tmA_s[:], tmA[:])
    fw.dma(sp, tmV_s[:], tmV[:])
    fw.dma(sp, tmO_s[:], tmO[:])
    ones4 = sb("ones4", [4, 128], F32)
    fw.memset(dve, ones4[:], 1.0)
    gb_i = sb("gb_i", [4, 1], F32)
    gb_fn = sb("gb_fn", [4, 1], F32)
    fw.dma(sp, gb_i[:], gate_bias[0:4, :])
    fw.dma(sp, gb_fn[:], gate_bias[4:8, :])
    fw.ts(dve, gb_fn[:], gb_fn[:], -1.0, None, ALU.mult)
    cw = sb("cw", [128, 5, 4], F32)
    cb = sb("cb", [128, 5], F32)
    mhg = sb("mhg_s", [128, 5], F32)
    skp = sb("skp_s", [128, 5], F32)
    for c in range(5):
        fw.dma(sp, cw[:, c, :], conv_wT[128 * c:128 * c + 128, :])
        fw.dma(sp, cb[:, c:c + 1], conv_b[128 * c:128 * c + 128, :])
        fw.dma(sp, mhg[:, c:c + 1], mhg_d[128 * c:128 * c + 128, :])
        fw.dma(sp, skp[:, c:c + 1], skip_d[128 * c:128 * c + 128, :])

    if do_sample:
        for (src, dst) in ((cwk_d, o_swk), (cwv_d, o_swv)):
            for g in range(NS):
                for hh in range(4):
                    fw.dma(pool, dst[g, 512 * hh:min(512 * hh + 512, 2047), :],
                           src[g, 512 * hh + 1:min(512 * hh + 513, 2048), :])
    push()
    outaT = sb("outaT", [128, 3, SEG + 128], BF16)
    outbT = sb("outbT", [128, 5, SEG + 128], BF16)
    push()
    KT = sb("KT", [128, 3, 2 * SEG], BF16)
    QT = sb("QT", [128, 3, SEG], BF16)
    push()
    w_in_b = sb("w_in_b", [128, 8, DIN], BF16)
    g1 = sb("g1", [128, 8], F32)
    wq_b = sb("wq_b", [128, 5, DB], BF16)
    wk_b = sb("wk_b", [128, 5, DB], BF16)
    v_scr = dscr("v_scr", [2 * SEG, 6 * 65], BF16)

    CTa = sb("CTa", [128, NHB, EB + 1], F32)
    CTb = sb("CTb", [32, NHB, EB + 1], F32)
    CTa_b = sb("CTa_b", [128, NHB, EB + 1], BF16)
    CTb_b = sb("CTb_b", [32, NHB, EB + 1], BF16)
    m_st = sb("m_st", [4, 1], F32)
    fw.memset(dve, CTa[:], 0.0)
    fw.memset(dve, CTb[:], 0.0)
    CTa.split(NHB)
    CTb.split(NHB)
    fw.memset(dve, CTa_b[:], 0.0)
    fw.memset(dve, CTb_b[:], 0.0)
    fw.memset(dve, m_st[:], 0.0)

    xt = [sb("xt%d" % i, [128, D], F32) for i in range(2)]
    xn_2 = [sb("xn_%d" % i_, [128, D], BF16) for i_ in range(2)]
    xn = xn_2[0]
    ss_2 = [sb("ss_%d" % i_, [128, 1], F32) for i_ in range(2)]
    ss = ss_2[0]
    rstd_2 = [sb("rstd_%d" % i_, [128, 1], F32) for i_ in range(2)]
    rstd = rstd_2[0]
    junk = sb("junk", [128, D], BF16)
    push()
    for c in range(8):
        fw.dma(sp, g1[:, c:c + 1], norm1_g[128 * c:128 * c + 128, :])
    wstage = [sb("wstage%d" % i, [128, 1600], F32) for i in range(2)]
    HW_ = DIN // 2
    for c in range(8):
        for hf in range(2):
            st = wstage[hf]
            fw.dma(sp, st[:, 0:HW_], w_in[128 * c:128 * c + 128, HW_ * hf:HW_ * hf + HW_])
            if hf == 0:
                fw.ts(dve, st[:, 0:DA], st[:, 0:DA], 0.125, None, ALU.mult)
            fw.scale_copy(act if hf else dve, w_in_b[:, c, HW_ * hf:HW_ * hf + HW_], st[:, 0:HW_], g1[:, c:c + 1])
    for (src, dst, scl) in ((wq_d, wq_b, 1.0), (wk_d, wk_b, float(EB) ** -0.5)):
        stv = wstage[0] if src is wq_d else wstage[1]
        fw.memset(dve, stv[:, 0:5 * 320], 0.0)
        for c in range(5):
            lo, hi = 128 * c, 128 * c + 128
            for hh in range(NHB):
                a0, a1 = max(lo, EB * hh), min(hi, EB * hh + EB)
                if a0 >= a1:
                    continue
                slot = hh - (lo // EB)
                fw.dma(sp, stv[a0 - lo:a1 - lo, 320 * c + 160 * slot:320 * c + 160 * slot + 160],
                       src[hh, a0 - EB * hh:a1 - EB * hh, :])
        fw.memset(dve, dst[:], 0.0)
        for c in range(5):
            h0 = (128 * c) // EB
            nh = 2 if (128 * c + 127) // EB > h0 else 1
            fw.ts(dve, dst[:, c, EB * h0:EB * h0 + EB * nh], stv[:, 320 * c:320 * c + EB * nh], scl, None, ALU.mult)

    pop()
    i4 = sb("i4", [4, 4], F32)
    fw.copy(dve, i4[:], ident_f[0:4, 0:4])
    def load_x(i):
        fw.dma(sp, xt[i % 2][:], xw[128 * i:128 * i + 128, :])

    xlast = sb("xlast", [128, 5, 3], F32)
    fw.memset(dve, xlast[:], 0.0)
    NPRE = MAIN0 // 4 if prefix_super else 0
    if NPRE:
        push()
        hT4 = sb("hT4", [128, 8, 512], BF16)
        xT4 = sb("xT4", [128, 5, 515], F32)
        fw.memset(dve, xT4[:], 0.0)
        xT4.split(5)
        cT4 = sb("cT4", [128, 5, 512], F32).split(5)
        caT4 = sb("caT4", [128, 5, 512], BF16)
        vext4 = sb("vext4", [128, 4, NHB, EB + 1], BF16)
        fw.memset(dve, vext4[:], 1.0)
        vext4.split(4)
        ktok4 = sb("ktok4", [128, 4, DB], BF16).split(4)
        kw4 = sb("kw4", [128, 4, NHB, EB], BF16).split(4)
        gi4 = sb("gi4", [4, 512], F32)
        gl4 = sb("gl4", [4, 512], F32)
        ga4 = sb("ga4", [4, 512], F32)
        gu4 = sb("gu4", [4, 512], F32)
        gM4 = sb("gM4", [4, 512], F32)
        gdl4 = sb("gdl4", [4, 1], F32)
        TM4 = sb("TM4", [128, 16], F32)
        TMe4 = sb("TMe4", [128, 16], F32)
        dec4 = sb("dec4", [128, 4], F32)
        vatt4 = [sb("vatt4_%d" % i_, [128, 6, 65], BF16) for i_ in range(1)] * 2
        load_x(0)
        for g in range(NPRE):
            halo = 4 * g >= HALO0
            for t in range(4):
                i = 4 * g + t
                p = i % 2
                xn, ss, rstd = xn_2[p], ss_2[p], rstd_2[p]
                if i + 1 < NT:
                    load_x(i + 1)
                fw.stt(junk[:], xt[p][:], 1.0, xt[p][:], ALU.mult, ALU.mult, accum_out=ss[:])
                fw.ts(dve, rstd[:], ss[:], 1.0 / D, EPS, ALU.mult, ALU.add)
                fw.activation(rstd[:], rstd[:], AF.Ln)
                fw.activation(rstd[:], rstd[:], AF.Exp, scale=-0.5)
                fw.ts(dve, xn[:], xt[p][:], rstd[:], None, ALU.mult)
                bT = pb()
                pT = bT.bitcast(BF16)
                for c in range(8):
                    fw.transpose(pT[:, 128 * c:128 * c + 128], xn[:, 128 * c:128 * c + 128], ident_b[:], inc=(c == 7))
                fw.copy(act, hT4[:, :, 128 * t:128 * t + 128], V(pT.h[:, 0:1024].rearrange("p (c t) -> p c t", c=8), pT.buf))
            for t in range(4):
                i = 4 * g + t
                for (c0, n) in ((3 * DA, 512), (3 * DA + 512, 128)):
                    b_ = pb()
                    for c in range(8):
                        fw.mm(b_[:, 0:n], hT4[:, c, 128 * t:128 * t + 128], w_in_b[:, c, c0:c0 + n], start=(c == 0), stop=(c == 7))
                    if n == 512:
                        fw.copy(act, V(vext4.h[:, t, 0:3, 0:EB], vext4.subs[t]), V(b_.h[:, 0:480].rearrange("p (h e) -> p h e", h=3), b_.buf))
                        fw.copy(dve, V(vext4.h[:, t, 3, 0:32], vext4.subs[t]), b_[:, 480:512])
                    else:
                        fw.copy(dve, V(vext4.h[:, t, 3, 32:EB], vext4.subs[t]), b_[:, 0:128])
                if halo:
                    at = i - HALO0
                    b_ = pb()
                    for c in range(8):
                        fw.mm(b_[:, 0:DA], hT4[:, c, 128 * t:128 * t + 128], w_in_b[:, c, 2 * DA:3 * DA], start=(c == 0), stop=(c == 7))
                    va = vatt4[i % 2]
                    fw.copy(act, va[:, :, 0:64], V(b_.h[:, 0:DA].rearrange("p (h e) -> p h e", h=6), b_.buf))
                    fw.copy(dve, va[:, :, 64:65], V(tmO_s.h[:, i:i + 1].unsqueeze(1).to_broadcast([128, 6, 1]), tmO_s.buf))
                    fw.dma(sp, v_scr[128 * at:128 * at + 128, :], va.re("p h e -> p (h e)"))
            for j in range(5):
                b_ = pb()
                for c in range(8):
                    fw.mm(b_[:, 0:512], w_in_b[:, c, 3 * DA + 128 * j:3 * DA + 128 * j + 128], hT4[:, c, :], start=(c == 0), stop=(c == 7))
                fw.copy(act if j % 2 else dve, V(xT4.h[:, j, 3:515], xT4.subs[j]), b_[:, 0:512])
            if halo:
                at0 = 4 * g - HALO0
                for j in range(3):
                    b_ = pb()
                    for c in range(8):
                        fw.mm(b_[:, 0:512], w_in_b[:, c, DA + 128 * j:DA + 128 * j + 128], hT4[:, c, :], start=(c == 0), stop=(c == 7))
                    fw.copy(act, KT[:, j, 128 * at0:128 * at0 + 512], b_[:, 0:512])
            bgi, bgf = pb(), pb()
            for (b_, c0) in ((bgi, 3 * DA + 2 * DB), (bgf, 3 * DA + 2 * DB + 4)):
                for c in range(8):
                    fw.mm(b_[0:4, 0:512], w_in_b[:, c, c0:c0 + 4], hT4[:, c, :], start=(c == 0), stop=(c == 7))
            fw.ts(dve, gi4[:], bgi[0:4, 0:512], gb_i[:], tmA_s[:, 4 * g:4 * g + 1], ALU.add, ALU.add)
            fw.activation(gl4[:], bgf[0:4, 0:512], AF.Exp, bias=gb_fn[:], scale=-1.0)
            fw.activation(gl4[:], gl4[:], AF.Ln, bias=1.0)
            fw.ts(dve, gl4[:], gl4[:], tmV_s[:, 4 * g:4 * g + 1], None, ALU.mult)
            for c in range(5):
                fw.activation(cT4.sub(c)[:, c, :], V(xT4.h[:, c, 0:512], xT4.subs[c]), AF.Identity, bias=cb[:, c:c + 1], scale=cw[:, c, 0:1])
            for k in range(1, 4):
                for c in range(5):
                    fw.stt(cT4.sub(c)[:, c, :], V(xT4.h[:, c, k:k + 512], xT4.subs[c]), cw[:, c, k:k + 1], cT4.sub(c)[:, c, :], ALU.mult, ALU.add)
            for c in range(5):
                fw.copy(dve, V(xT4.h[:, c, 0:3], xT4.subs[c]), V(xT4.h[:, c, 512:515], xT4.subs[c]))
            fw.activation(caT4[:], cT4[:], AF.Silu)
            for t in range(4):
                bk1, bk2 = pb(), pb()
                for c in range(5):
                    fw.mm(bk1[:, 0:512], caT4[:, c, 128 * t:128 * t + 128], wk_b[:, c, 0:512], start=(c == 0), stop=(c == 4))
                for c in range(5):
                    fw.mm(bk2[:, 0:128], caT4[:, c, 128 * t:128 * t + 128], wk_b[:, c, 512:640], start=(c == 0), stop=(c == 4))
                fw.copy(act, V(ktok4.h[:, t, 0:512], ktok4.subs[t]), bk1[:, 0:512])
                fw.copy(act, V(ktok4.h[:, t, 512:640], ktok4.subs[t]), bk2[:, 0:128])
            fw.scan(ga4[:], V(ones4.h[:, 0:1].to_broadcast([4, 512]), ones4.buf), gl4[:], 0.0, ALU.mult, ALU.add)
            fw.tt(dve, gu4[:], gi4[:], ga4[:], ALU.subtract)
            fw.scan(gM4[:], gu4[:], gu4[:], m_st[:], ALU.max, ALU.max)
            fw.ts(dve, gi4[:], gu4[:], gM4[:, 511:512], None, ALU.subtract)
            fw.ts(dve, gdl4[:], gM4[:, 511:512], -1.0, m_st[:], ALU.mult, ALU.add)
            bt = pb()
            for t in range(4):
                fw.transpose(bt[:, 4 * t:4 * t + 4], gi4[:, 128 * t:128 * t + 128], ident_f[0:4, 0:4])
            fw.copy(dve, TM4[:], bt[:, 0:16])
            fw.activation(TMe4[:], TM4[:], AF.Exp)
            bd = pb()
            fw.mm(bd[:, 0:4], V(gdl4.h[:, 0:1].to_broadcast([4, 128]), gdl4.buf), i4[:], start=True, stop=True)
            fw.activation(dec4[:], bd[:, 0:4], AF.Exp)
            for t in range(4):
                fw.tt(dve, V(kw4.h[:, t, :, :], kw4.subs[t]), V(ktok4.h[:, t, :].rearrange("p (h e) -> p h e", h=NHB), ktok4.subs[t]),
                      V(TMe4.h[:, 4 * t:4 * t + 4].unsqueeze(2).to_broadcast([128, NHB, EB]), TMe4.buf), ALU.mult)
            for hh in range(NHB):
                bu = pb()
                for t in range(4):
                    fw.mm(bu[:, 0:EB + 1], V(kw4.h[:, t, hh, 0:128], kw4.subs[t]), V(vext4.h[:, t, hh, :], vext4.subs[t]), start=(t == 0), stop=(t == 3))
                for t in range(4):
                    fw.mm(bu[0:32, 256:256 + EB + 1], V(kw4.h[:, t, hh, 128:EB], kw4.subs[t]), V(vext4.h[:, t, hh, :], vext4.subs[t]), start=(t == 0), stop=(t == 3))
                fw.stt(CTa.sub(hh)[:, hh, :], CTa.sub(hh)[:, hh, :], dec4[:, hh:hh + 1], bu[:, 0:EB + 1], ALU.mult, ALU.add)
                fw.stt(CTb.sub(hh)[:, hh, :], CTb.sub(hh)[:, hh, :], dec4[0:32, hh:hh + 1], bu[0:32, 256:256 + EB + 1], ALU.mult, ALU.add)
            fw.tt(dve, m_st[:], ga4[:, 511:512], gM4[:, 511:512], ALU.add)
        fw.copy(act, CTa_b[:], CTa[:])
        fw.copy(act, CTb_b[:], CTb[:])
        fw.copy(dve, xlast[:], xT4[:, :, 0:3])
        pop()
    push()
    obT = sb("obT", [128, 5, 128], BF16)
    caT_2 = [sb("caT_%d" % i_, [128, 5, 128], BF16) for i_ in range(2)]
    caT = caT_2[0]
    hT_2 = [sb("hT_%d" % i_, [128, 8, 128], BF16) for i_ in range(2)]
    hT = hT_2[0]
    kvst = [sb("kvst%d" % i, [128, 2 * DA], F32) for i in range(2)]
    vext_2 = [sb("vext_%d" % i_, [128, NHB, EB + 1], BF16) for i_ in range(2)]
    vext = vext_2[0]
    vatt = [sb("vatt%d" % i, [128, 6, 65], BF16) for i in range(2)]
    xbT = [sb("xbT%d" % i, [128, 5, 131], F32) for i in range(2)]
    fw.memset(dve, xbT[0][:], 0.0)
    fw.memset(dve, xbT[1][:], 0.0)
    fw.memset(dve, vext_2[0][:], 1.0)
    fw.memset(dve, vext_2[1][:], 1.0)
    cT_2 = [sb("cT_%d" % i_, [128, 5, 128], F32) for i_ in range(2)]
    cT = cT_2[0]
    ktok_2 = [sb("ktok_%d" % i_, [128, DB], F32) for i_ in range(2)]
    ktok = ktok_2[0]
    kw_2 = [sb("kw_%d" % i_, [128, NHB, EB], BF16) for i_ in range(2)]
    kw = kw_2[0]
    qTa = sb("qTa", [128, NHB, 128], BF16)
    qTb = sb("qTb", [32, NHB, 128], BF16)
    kTa = sb("kTa", [128, NHB, 128], BF16)
    kTb = sb("kTb", [32, NHB, 128], BF16)
    gi_2 = [sb("gi_%d" % i_, [4, 128], F32) for i_ in range(2)]
    gi = gi_2[0]
    gl_2 = [sb("gl_%d" % i_, [4, 128], F32) for i_ in range(2)]
    gl = gl_2[0]
    ga_2 = [sb("ga_%d" % i_, [4, 128], F32) for i_ in range(2)]
    ga = ga_2[0]
    gu_2 = [sb("gu_%d" % i_, [4, 128], F32) for i_ in range(2)]
    gu = gu_2[0]
    gM_2 = [sb("gM_%d" % i_, [4, 128], F32) for i_ in range(2)]
    gM = gM_2[0]
    gnM_2 = [sb("gnM_%d" % i_, [4, 128], F32) for i_ in range(2)]
    gnM = gnM_2[0]
    G1a_2 = [sb("G1a_%d" % i_, [4, 128], F32) for i_ in range(2)]
    G1a = G1a_2[0]
    G1b_2 = [sb("G1b_%d" % i_, [4, 128], F32) for i_ in range(2)]
    G1b = G1b_2[0]
    G1c_2 = [sb("G1c_%d" % i_, [4, 128], F32) for i_ in range(2)]
    G1c = G1c_2[0]
    gdl_2 = [sb("gdl_%d" % i_, [4, 1], F32) for i_ in range(2)]
    gdl = gdl_2[0]
    TM_2 = [sb("TM_%d" % i_, [128, 16], F32) for i_ in range(2)]
    TM = TM_2[0]
    TMe_2 = [sb("TMe_%d" % i_, [128, 12], F32) for i_ in range(2)]
    TMe = TMe_2[0]
    dec_bc_2 = [sb("dec_bc_%d" % i_, [128, 4], F32) for i_ in range(2)]
    dec_bc = dec_bc_2[0]
    numd4 = sb("numd4", [128, NHB, EB + 1], F32).split(NHB)
    sc4 = sb("sc4", [128, 6, NHB], F32)
    expD_2 = [sb("expD_%d" % i_, [128, 128], F32) for i_ in range(2)]
    AT_2 = [sb("AT_%d" % i_, [128, 128], BF16) for i_ in range(2)]
    inter_s_2 = [sb("inter_s_%d" % i_, [128, EB + 1], F32) for i_ in range(2)]
    hbn = sb("hbn", [128, DB], BF16)
    gt1 = sb("gt1", [128, 5, 128], F32)

    tile_list = list(range(4 * NPRE, NT)) if tile_list is None else tile_list
    if tile_list and not NPRE:
        load_x(tile_list[0])
    if NPRE:
        fw.copy(dve, xbT[1][:, :, 128:131], xlast[:])
    obT_2 = [obT, sb("obT_b", [128, 5, 128], BF16)]

    def stage1a(i):
            p = i % 2
            xn = xn_2[p]
            hT = hT_2[p]
            ss = ss_2[p]
            rstd = rstd_2[p]
            caT = caT_2[p]
            vext = vext_2[p]
            cT = cT_2[p]
            ktok = ktok_2[p]
            kw = kw_2[p]
            gi = gi_2[p]
            gl = gl_2[p]
            ga = ga_2[p]
            gu = gu_2[p]
            gM = gM_2[p]
            gnM = gnM_2[p]
            G1a = G1a_2[p]
            G1b = G1b_2[p]
            G1c = G1c_2[p]
            gdl = gdl_2[p]
            TM = TM_2[p]
            TMe = TMe_2[p]
            dec_bc = dec_bc_2[p]
            is_main = i >= MAIN0
            obT = obT_2[p]
            if cT.subs is None:
                cT.split(5)
            is_halo = i >= HALO0
            mt = i - MAIN0
            if i + 1 < NT and (i + 1) in tile_list:
                load_x(i + 1)
            fw.stt(junk[:], xt[p][:], 1.0, xt[p][:], ALU.mult, ALU.mult, accum_out=ss[:])
            fw.ts(dve, rstd[:], ss[:], 1.0 / D, EPS, ALU.mult, ALU.add)
            fw.activation(rstd[:], rstd[:], AF.Ln)
            fw.activation(rstd[:], rstd[:], AF.Exp, scale=-0.5)
            fw.ts(dve, xn[:], xt[p][:], rstd[:], None, ALU.mult)
            bT = pb()
            pT = bT.bitcast(BF16)
            for c in range(8):
                fw.transpose(pT[:, 128 * c:128 * c + 128], xn[:, 128 * c:128 * c + 128], ident_b[:], inc=(c == 7))
            fw.copy(act, hT.re("p c t -> p (c t)"), pT[:, 0:1024])


    def stage1b(i):
            p = i % 2
            xn = xn_2[p]
            hT = hT_2[p]
            ss = ss_2[p]
            rstd = rstd_2[p]
            caT = caT_2[p]
            vext = vext_2[p]
            cT = cT_2[p]
            ktok = ktok_2[p]
            kw = kw_2[p]
            gi = gi_2[p]
            gl = gl_2[p]
            ga = ga_2[p]
            gu = gu_2[p]
            gM = gM_2[p]
            gnM = gnM_2[p]
            G1a = G1a_2[p]
            G1b = G1b_2[p]
            G1c = G1c_2[p]
            gdl = gdl_2[p]
            TM = TM_2[p]
            TMe = TMe_2[p]
            dec_bc = dec_bc_2[p]
            is_main = i >= MAIN0
            obT = obT_2[p]
            if cT.subs is None:
                cT.split(5)
            is_halo = i >= HALO0
            mt = i - MAIN0
            def proj_tok(c0, n):
                b = pb()
                for c in range(8):
                    fw.mm(b[:, 0:n], hT[:, c, :], w_in_b[:, c, c0:c0 + n], start=(c == 0), stop=(c == 7))
                return b

            def proj_feat(c0, nchunks, M=128):
                b = pb()
                for j in range(nchunks):
                    for c in range(8):
                        fw.mm(b[0:M, 128 * j:128 * j + 128], w_in_b[:, c, c0 + M * j:c0 + M * j + M], hT[:, c, :],
                              start=(c == 0), stop=(c == 7))
                return b

            if is_halo:
                at = i - HALO0
                bk = proj_feat(DA, 3)
                fw.copy(act, KT[:, :, 128 * at:128 * at + 128], V(bk.h[:, 0:384].rearrange("p (c t) -> p c t", c=3), bk.buf))
                bkv = proj_tok(2 * DA, DA)
                va = vatt[p]
                if is_main:
                    bkk = proj_tok(DA, DA)
                    st = kvst[p]
                    fw.copy(dve, st[:, 0:DA], bkk[:, 0:DA])
                    fw.copy(dve, st[:, DA:2 * DA], bkv[:, 0:DA])
                    r0 = 128 * mt
                    fw.dma(sp, o_wk[r0:r0 + 128, :], st[:, 0:DA])
                    fw.dma(sp, o_wv[r0:r0 + 128, :], st[:, DA:2 * DA])
                fw.copy(act, va[:, :, 0:64], V(bkv.h[:, 0:DA].rearrange("p (h e) -> p h e", h=6), bkv.buf))
                fw.copy(dve, va[:, :, 64:65], V(tmO_s.h[:, i:i + 1].unsqueeze(1).to_broadcast([128, 6, 1]), tmO_s.buf))
                fw.dma(sp, v_scr[128 * at:128 * at + 128, :], va.re("p h e -> p (h e)"))
            if is_main:
                bq = proj_feat(0, 3)
                fw.copy(act, QT[:, :, 128 * mt:128 * mt + 128], V(bq.h[:, 0:384].rearrange("p (c t) -> p c t", c=3), bq.buf))
                b1 = proj_feat(3 * DA + DB, 4)
                fw.activation(obT[:, 0:4, :], V(b1.h[:, 0:512].rearrange("p (c t) -> p c t", c=4), b1.buf), AF.Sigmoid)
                b2 = proj_feat(3 * DA + DB + 512, 1)
                fw.activation(obT[:, 4, :], b2[:, 0:128], AF.Sigmoid)

            if stage < 1:
                return
            xT = xbT[p]
            xTp = xbT[1 - p]
            fw.copy(dve, xT[:, :, 0:3], xTp[:, :, 128:131])
            if stage < 1.2:
                return
            b1 = proj_feat(3 * DA, 4)
            if stage < 1.4:
                return
            fw.copy(act, xT[:, 0:4, 3:131], V(b1.h[:, 0:512].rearrange("p (c t) -> p c t", c=4), b1.buf))
            if stage < 1.6:
                return
            b2 = proj_feat(3 * DA + 512, 1)
            if stage < 1.7:
                return
            if stage < 1.8:
                fw.copy(act, junk[:, 0:128], b2[:, 0:128])
                return
            if VAR == 'dve':
                fw.copy(dve, xT[:, 4, 3:131], b2[:, 0:128])
            elif VAR == 'col0':
                fw.copy(act, xT[:, 4, 0:128], b2[:, 0:128])
            elif VAR == 'chunk3':
                fw.copy(act, xT[:, 3, 3:131], b2[:, 0:128])
            else:
                fw.copy(act, xT[:, 4, 3:131], b2[:, 0:128])
            if stage < 2:
                return
            bg = pb()
            for (j, c0) in ((0, 3 * DA + 2 * DB), (1, 3 * DA + 2 * DB + 4)):
                for c in range(8):
                    fw.mm(bg[0:4, 128 * j:128 * j + 128], w_in_b[:, c, c0:c0 + 4], hT[:, c, :], start=(c == 0), stop=(c == 7))
            if stage < 2.1:
                fw.copy(dve, gi[:], bg[0:4, 0:128])
                return
            fw.ts(dve, gi[:], bg[0:4, 0:128], gb_i[:], tmA_s[:, i:i + 1], ALU.add, ALU.add)
            if stage < 2.2:
                return
            fw.activation(gl[:], bg[0:4, 128:256], AF.Exp, bias=gb_fn[:], scale=-1.0)
            if stage < 2.3:
                return
            fw.activation(gl[:], gl[:], AF.Ln, bias=1.0)
            if stage < 2.4:
                return
            fw.ts(dve, gl[:], gl[:], tmV_s[:, i:i + 1], None, ALU.mult)
            if stage < 3:
                return
            bx = proj_tok(3 * DA, 512)
            bx2 = proj_tok(3 * DA + 512, 128)
            for hh in range(NHB):
                c0 = EB * hh
                if c0 + EB <= 512:
                    fw.copy(act if hh % 2 else dve, vext[:, hh, 0:EB], bx[:, c0:c0 + EB])
                else:
                    fw.copy(dve, vext[:, hh, 0:512 - c0], bx[:, c0:512])
                    fw.copy(act, vext[:, hh, 512 - c0:EB], bx2[:, 0:c0 + EB - 512])
            if i == NT - 1:
                xs_ = kvst[1 - p]
                fw.copy(dve, xs_[:, 0:512], bx[:, 0:512])
                fw.copy(dve, xs_[:, 512:640], bx2[:, 0:128])
                fw.dma(sp, o_conv[:, :], xs_[125:128, 0:DB])
            if stage < 4:
                return
            for c in range(5):
                fw.activation(cT.sub(c)[:, c, :], xT[:, c, 0:128], AF.Identity, bias=cb[:, c:c + 1], scale=cw[:, c, 0:1])
            for k in range(1, 4):
                for c in range(5):
                    fw.stt(cT.sub(c)[:, c, :], xT[:, c, k:k + 128], cw[:, c, k:k + 1], cT.sub(c)[:, c, :], ALU.mult, ALU.add)
            fw.activation(caT[:], cT[:], AF.Silu)
            if stage < 5:
                return
            bk1 = pb()
            bk2 = pb()
            for c in range(5):
                fw.mm(bk1[:, 0:512], caT[:, c, :], wk_b[:, c, 0:512], start=(c == 0), stop=(c == 4))
            for c in range(5):
                fw.mm(bk2[:, 0:128], caT[:, c, :], wk_b[:, c, 512:640], start=(c == 0), stop=(c == 4))
            fw.copy(act, ktok[:, 0:512], bk1[:, 0:512])
            fw.copy(act, ktok[:, 512:640], bk2[:, 0:128])


    def stage2a(i):
            p = i % 2
            xn = xn_2[p]
            hT = hT_2[p]
            ss = ss_2[p]
            rstd = rstd_2[p]
            caT = caT_2[p]
            vext = vext_2[p]
            cT = cT_2[p]
            ktok = ktok_2[p]
            kw = kw_2[p]
            gi = gi_2[p]
            gl = gl_2[p]
            ga = ga_2[p]
            gu = gu_2[p]
            gM = gM_2[p]
            gnM = gnM_2[p]
            G1a = G1a_2[p]
            G1b = G1b_2[p]
            G1c = G1c_2[p]
            gdl = gdl_2[p]
            TM = TM_2[p]
            TMe = TMe_2[p]
            dec_bc = dec_bc_2[p]
            is_main = i >= MAIN0
            obT = obT_2[p]
            if cT.subs is None:
                cT.split(5)
            is_halo = i >= HALO0
            mt = i - MAIN0
            if stage < 6:
                return
            fw.scan(ga[:], ones4[:], gl[:], 0.0, ALU.mult, ALU.add)
            fw.tt(dve, gu[:], gi[:], ga[:], ALU.subtract)
            fw.scan(gM[:], gu[:], gu[:], m_st[:], ALU.max, ALU.max)
            fw.ts(dve, gnM[:], gM[:], -1.0, None, ALU.mult)
            fw.ts(dve, G1a[:], gu[:], gM[:, 127:128], None, ALU.subtract)
            fw.ts(dve, G1b[:], gM[:], -1.0, m_st[:], ALU.mult, ALU.add)
            fw.stt(G1c[:], ga[:], -1.0, gM[:], ALU.mult, ALU.subtract)
            fw.ts(dve, gdl[:], gM[:, 127:128], -1.0, m_st[:], ALU.mult, ALU.add)
            if stage < 7:
                return
            bt = pb()
            for (j_, g_) in enumerate((G1a, G1b, G1c, gu)):
                fw.transpose(bt[:, 4 * j_:4 * j_ + 4], g_[:], ident_f[0:4, 0:4])
            fw.copy(dve, TM[:], bt[:, 0:16])
            fw.activation(TMe[:], TM[:, 0:12], AF.Exp)
            if stage < 8:
                return
            bd = pb()
            fw.mm(bd[:, 0:4], V(gdl.h[:, 0:1].to_broadcast([4, 128]), gdl.buf), i4[:], start=True, stop=True)
            fw.activation(dec_bc[:], bd[:, 0:4], AF.Exp)

            if stage < 9:
                return
            if is_main:
                for (wsrc, da_, db_) in ((wq_b, qTa, qTb), (wk_b, kTa, kTb)):
                    ba = pb()
                    bb = pb()
                    for hh in range(NHB):
                        cs = sorted(set([(EB * hh) // 128, (EB * hh + EB - 1) // 128]))
                        for n_, c in enumerate(cs):
                            fw.mm(ba[:, 128 * hh:128 * hh + 128], wsrc[:, c, EB * hh:EB * hh + 128], caT[:, c, :],
                                  start=(n_ == 0), stop=(n_ == len(cs) - 1))
                        for n_, c in enumerate(cs):
                            fw.mm(bb[0:32, 128 * hh:128 * hh + 128], wsrc[:, c, EB * hh + 128:EB * hh + EB], caT[:, c, :],
                                  start=(n_ == 0), stop=(n_ == len(cs) - 1))
                    fw.copy(act, da_.re("p h t -> p (h t)"), ba[:, 0:512])
                    fw.copy(dve, db_.re("p h t -> p (h t)"), bb[0:32, 0:512])
                for hh in range(NHB):
                    expD, AT, inter_s = expD_2[hh % 2], AT_2[hh % 2], inter_s_2[hh % 2]
                    bs = pb()
                    fw.mm(bs[:, 0:128], kTa[:, hh, :], qTa[:, hh, :], start=True, stop=False)
                    fw.mm(bs[:, 0:128], kTb[:, hh, :], qTb[:, hh, :], start=False, stop=True)
                    fw.mm(bs[:, 128:256], sel[:, 128 * hh:128 * hh + 128], gnM[:], start=True, stop=False)
                    fw.mm(bs[:, 128:256], ident_f[:], negmask[:], start=False, stop=True)
                    fw.activation(expD[:], bs[:, 128:256], AF.Exp, bias=TM[:, 12 + hh:13 + hh])
                    fw.tt(dve, AT[:], bs[:, 0:128], expD[:], ALU.mult)
                    bn = pb()
                    fw.mm(bn[:, 0:EB + 1], AT[:], vext[:, hh, :], start=True, stop=True)
                    fw.mm(bn[:, 256:256 + EB + 1], qTa[:, hh, :], CTa_b[:, hh, :], start=True, stop=False)
                    fw.mm(bn[:, 256:256 + EB + 1], qTb[:, hh, :], CTb_b[:, hh, :], start=False, stop=True)
                    fw.activation(inter_s[:], bn[:, 256:256 + EB + 1], AF.Copy, scale=TMe[:, 4 + hh:5 + hh])
                    fw.tt(dve, numd4.sub(hh)[:, hh, :], bn[:, 0:EB + 1], inter_s[:], ALU.add)
                den4 = numd4[:, :, EB]
                fw.stt(sc4[:, 0, :], den4, -1.0, den4, ALU.mult, ALU.max)
                fw.tt(dve, sc4[:, 0, :], sc4[:, 0, :], TMe[:, 8:12], ALU.max)
                fw.recip(sc4[:, 1, :], sc4[:, 0, :])
                sqv = V(gt1.h[:].rearrange("p c t -> p (c t)").rearrange("p (h e) -> p h e", h=NHB), gt1.buf)
                fw.tt(dve, sqv, numd4[:, :, 0:EB], numd4[:, :, 0:EB], ALU.mult)
                fw.reduce(sc4[:, 2, :], sqv, ALU.add)
                fw.tt(dve, sc4[:, 3, :], sc4[:, 1, :], sc4[:, 1, :], ALU.mult)
                fw.tt(dve, sc4[:, 3, :], sc4[:, 3, :], sc4[:, 2, :], ALU.mult)
                fw.ts(dve, sc4[:, 3, :], sc4[:, 3, :], 1.0 / EB, EPS, ALU.mult, ALU.add)
                fw.activation(sc4[:, 4, :], sc4[:, 3, :], AF.Ln)
                fw.activation(sc4[:, 4, :], sc4[:, 4, :], AF.Exp, scale=-0.5)
                fw.tt(dve, sc4[:, 5, :], sc4[:, 4, :], sc4[:, 1, :], ALU.mult)
                fw.tt(dve, V(hbn.h[:].rearrange("p (h e) -> p h e", h=NHB), hbn.buf), numd4[:, :, 0:EB],
                      V(sc4.h[:, 5, :].unsqueeze(2).to_broadcast([128, NHB, EB]), sc4.buf), ALU.mult)
                bh = pb()
                bh2 = pb()
                pTh = bh.bitcast(BF16)
                pTh2 = bh2.bitcast(BF16)
                for c in range(5):
                    dstp = pTh[:, 128 * c:128 * c + 128] if c < 4 else pTh2[:, 0:128]
                    fw.transpose(dstp, hbn[:, 128 * c:128 * c + 128], ident_b[:])
                for c in range(5):
                    srcp = pTh[:, 128 * c:128 * c + 128] if c < 4 else pTh2[:, 0:128]
                    fw.ts(dve, gt1[:, c, :], srcp, mhg[:, c:c + 1], None, ALU.mult)
                    fw.stt(gt1[:, c, :], caT[:, c, :], skp[:, c:c + 1], gt1[:, c, :], ALU.mult, ALU.add)
                fw.tt(dve, outbT[:, :, 128 * mt:128 * mt + 128], gt1[:], obT[:], ALU.mult)


    def stage2b(i):
            p = i % 2
            xn = xn_2[p]
            hT = hT_2[p]
            ss = ss_2[p]
            rstd = rstd_2[p]
            caT = caT_2[p]
            vext = vext_2[p]
            cT = cT_2[p]
            ktok = ktok_2[p]
            kw = kw_2[p]
            gi = gi_2[p]
            gl = gl_2[p]
            ga = ga_2[p]
            gu = gu_2[p]
            gM = gM_2[p]
            gnM = gnM_2[p]
            G1a = G1a_2[p]
            G1b = G1b_2[p]
            G1c = G1c_2[p]
            gdl = gdl_2[p]
            TM = TM_2[p]
            TMe = TMe_2[p]
            dec_bc = dec_bc_2[p]
            is_main = i >= MAIN0
            obT = obT_2[p]
            if cT.subs is None:
                cT.split(5)
            is_halo = i >= HALO0
            mt = i - MAIN0
            if stage < 10:
                return
            fw.tt(dve, kw[:], V(ktok.h[:].rearrange("p (h e) -> p h e", h=NHB), ktok.buf),
                  V(TMe.h[:, 0:4].unsqueeze(2).to_broadcast([128, NHB, EB]), TMe.buf), ALU.mult)
            for hh in range(NHB):
                bu = pb()
                fw.mm(bu[:, 0:EB + 1], kw[:, hh, 0:128], vext[:, hh, :], start=True, stop=True)
                fw.mm(bu[0:32, 256:256 + EB + 1], kw[:, hh, 128:EB], vext[:, hh, :], start=True, stop=True)
                fw.stt(CTa.sub(hh)[:, hh, :], CTa.sub(hh)[:, hh, :], dec_bc[:, hh:hh + 1], bu[:, 0:EB + 1], ALU.mult, ALU.add)
                fw.stt(CTb.sub(hh)[:, hh, :], CTb.sub(hh)[:, hh, :], dec_bc[0:32, hh:hh + 1], bu[0:32, 256:256 + EB + 1], ALU.mult, ALU.add)
            fw.copy(act, CTa_b[:], CTa[:])
            fw.copy(act, CTb_b[:], CTb[:])
            fw.tt(dve, m_st[:], ga[:, 127:128], gM[:, 127:128], ALU.add)


    for idx_, i in enumerate(tile_list):
        stage1a(i)
        if idx_ > 0:
            stage2a(tile_list[idx_ - 1])
        stage1b(i)
        if idx_ > 0:
            stage2b(tile_list[idx_ - 1])
    if tile_list:
        stage2a(tile_list[-1])
        stage2b(tile_list[-1])
    xn, ss, rstd = xn_2[0], ss_2[0], rstd_2[0]
    pop()
    push()
    if do_sample:
        zs = sb("zs", [NS, DIN], F32)
        hT = sb("hTs", [128, 8, NS], BF16)
        caT = sb("caTs", [128, 5, NS], BF16)
        s_q = dscr("s_q", [NS * NHB, EB], F32)
        s_k = dscr("s_k", [NS * NHB, EB], F32)
        s_v = dscr("s_v", [NS * NHB, EB], F32)
        s_sc = dscr("s_sc", [NS * NHB, 4], F32)
        s_misc = dscr("s_misc", [NS, 2 * DB + 3 * DA], F32)
        p = 0
        fw.dma(sp, xt[p][0:NS, :], xs_d[:, :])
        fw.stt(junk[0:NS, :], xt[p][0:NS, :], 1.0, xt[p][0:NS, :], ALU.mult, ALU.mult, accum_out=ss[0:NS, :])
        fw.ts(dve, rstd[0:NS, :], ss[0:NS, :], 1.0 / D, EPS, ALU.mult, ALU.add)
        fw.activation(rstd[0:NS, :], rstd[0:NS, :], AF.Ln)
        fw.activation(rstd[0:NS, :], rstd[0:NS, :], AF.Exp, scale=-0.5)
        fw.ts(dve, xn[0:NS, :], xt[p][0:NS, :], rstd[0:NS, :], None, ALU.mult)
        bT = pb()
        pT = bT.bitcast(BF16)
        for c in range(8):
            fw.transpose(pT[:, 128 * c:128 * c + NS], xn[0:NS, 128 * c:128 * c + 128], ident_b[0:NS, 0:NS], inc=(c == 7))
        for c in range(8):
            fw.copy(act, hT[:, c, 0:NS], pT[:, 128 * c:128 * c + NS])
        for c0 in range(0, DIN, 512):
            n = min(512, DIN - c0)
            bk_ = pb()
            for c in range(8):
                fw.mm(bk_[0:NS, 0:n], hT[:, c, 0:NS], w_in_b[:, c, c0:c0 + n], start=(c == 0), stop=(c == 7))
            fw.copy(act if (c0 // 512) % 2 else dve, zs[:, c0:c0 + n], bk_[0:NS, 0:n])
        fw.dma(sp, o_swk[:, 2047, :], zs[:, DA:2 * DA])
        fw.dma(sp, o_swv[:, 2047, :], zs[:, 2 * DA:3 * DA])
        fw.dma(sp, o_sconv[:, 2, :], zs[:, 3 * DA:3 * DA + DB])
        fw.dma(sp, o_sconv[:, 0:2, :], s_conv_d[:, 1:3, :])
        cwr = sb("cwr", [NS, 4, DB], F32)
        cbr = sb("cbr", [NS, DB], F32)
        scv = sb("scv", [NS, 3, DB], F32)
        fw.dma(sp, cwr.re("p i d -> p (i d)"), V(conv_w_d.h.ap().rearrange("i d -> (i d)").partition_broadcast(NS), conv_w_d.buf))
        fw.dma(sp, cbr[:], V(conv_b.h.ap().rearrange("d o -> (d o)").partition_broadcast(NS), conv_b.buf))
        fw.dma(sp, scv[:], s_conv_d[:, :, :])
        cs = sb("cs", [NS, DB], F32)
        cs2 = sb("cs2", [NS, DB], F32)
        fw.tt(dve, cs[:], zs[:, 3 * DA:3 * DA + DB], cwr[:, 3, :], ALU.mult)
        fw.tt(dve, cs[:], cs[:], cbr[:], ALU.add)
        for k in range(3):
            fw.tt(dve, cs2[:], scv[:, k, :], cwr[:, k, :], ALU.mult)
            fw.tt(dve, cs[:], cs[:], cs2[:], ALU.add)
        cas = sb("cas", [NS, DB], F32)
        casb = sb("casb", [NS, DB], BF16)
        fw.activation(cas[:], cs[:], AF.Silu)
        fw.copy(dve, casb[:], cas[:])
        fw.dma(sp, s_misc[:, DB:2 * DB], cas[:])
        sob = sb("sob", [NS, DB], F32)
        fw.activation(sob[:], zs[:, 3 * DA + DB:3 * DA + 2 * DB], AF.Sigmoid)
        fw.dma(sp, s_misc[:, 0:DB], sob[:])
        fw.dma(sp, s_misc[:, 2 * DB:2 * DB + 3 * DA], zs[:, 0:3 * DA])
        bT2 = pb()
        pT2 = bT2.bitcast(BF16)
        for c in range(5):
            fw.transpose(pT2[:, 128 * c:128 * c + NS], casb[:, 128 * c:128 * c + 128], ident_b[0:NS, 0:NS], inc=(c == 4))
        for c in range(5):
            fw.copy(act, caT[:, c, 0:NS], pT2[:, 128 * c:128 * c + NS])
        sqk = sb("sqk", [NS, 2, DB], F32)
        for wi, wsrc in enumerate((wq_b, wk_b)):
            for (c0, n) in ((0, 512), (512, 128)):
                bk_ = pb()
                for c in range(5):
                    fw.mm(bk_[0:NS, 0:n], caT[:, c, 0:NS], wsrc[:, c, c0:c0 + n], start=(c == 0), stop=(c == 4))
                fw.copy(dve, sqk[:, wi, c0:c0 + n], bk_[0:NS, 0:n])
        fw.dma(sp, V(s_q.h.ap().rearrange("(b h) e -> b (h e)", h=NHB), s_q.buf), sqk[:, 0, :])
        fw.dma(sp, V(s_k.h.ap().rearrange("(b h) e -> b (h e)", h=NHB), s_k.buf), sqk[:, 1, :])
        fw.dma(sp, V(s_v.h.ap().rearrange("(b h) e -> b (h e)", h=NHB), s_v.buf), zs[:, 3 * DA:3 * DA + DB])
        gbr = sb("gbr", [NS, 8], F32)
        sm0 = sb("sm0", [NS, NHB], F32)
        fw.dma(sp, gbr[:], V(gate_bias.h.ap().rearrange("g o -> (g o)").partition_broadcast(NS), gate_bias.buf))
        fw.dma(sp, sm0[:], s_m_d[:, :])
        sg = sb("sg", [NS, 8, NHB], F32)
        fw.tt(dve, sg[:, 0, :], zs[:, DIN - 8:DIN - 4], gbr[:, 0:4], ALU.add)
        fw.tt(dve, sg[:, 1, :], zs[:, DIN - 4:DIN], gbr[:, 4:8], ALU.add)
        fw.activation(sg[:, 1, :], sg[:, 1, :], AF.Exp, scale=-1.0)
        fw.activation(sg[:, 1, :], sg[:, 1, :], AF.Ln, bias=1.0)
        fw.tt(dve, sg[:, 2, :], sm0[:], sg[:, 1, :], ALU.subtract)
        fw.tt(dve, sg[:, 3, :], sg[:, 2, :], sg[:, 0, :], ALU.max)
        fw.tt(dve, sg[:, 4, :], sg[:, 2, :], sg[:, 3, :], ALU.subtract)
        fw.tt(dve, sg[:, 5, :], sg[:, 0, :], sg[:, 3, :], ALU.subtract)
        fw.ts(dve, sg[:, 6, :], sg[:, 3, :], -1.0, None, ALU.mult)
        ssc = sb("ssc", [NS, NHB, 4], F32)
        fw.activation(ssc[:, :, 0], sg[:, 4, :], AF.Exp)
        fw.activation(ssc[:, :, 1], sg[:, 5, :], AF.Exp)
        fw.activation(ssc[:, :, 2], sg[:, 6, :], AF.Exp)
        fw.copy(dve, ssc[:, :, 3], sg[:, 3, :])
        fw.dma(sp, o_sm[:, :], sg[:, 3, :])
        fw.dma(sp, V(s_sc.h.ap().rearrange("(b h) x -> b (h x)", h=NHB), s_sc.buf), ssc.re("p h x -> p (h x)"))

    Co_a = sb("Co_a", [128, NHB, EB], F32)
    Co_b = sb("Co_b", [33, NHB, EB], F32)
    for hh in range(NHB if do_cout else 0):
        b_ = pb()
        fw.transpose(b_[:, 0:128], CTa[:, hh, 0:128], ident_f[:])
        fw.transpose(b_[:, 128:160], CTb[:, hh, 0:128], ident_f[0:32, 0:32])
        fw.transpose(b_[0:33, 256:384], CTa[:, hh, 128:EB + 1], ident_f[:])
        fw.transpose(b_[0:33, 384:416], CTb[:, hh, 128:EB + 1], ident_f[0:32, 0:32])
        fw.copy(dve, Co_a[:, hh, :], b_[:, 0:EB])
        fw.copy(dve, Co_b[:, hh, :], b_[0:33, 256:256 + EB])
        fw.dma(sp, o_C[hh, 0:128, :], Co_a[:, hh, :])
        fw.dma(sp, o_C[hh, 128:EB, :], Co_b[0:32, hh, :])
        fw.dma(sp, o_n[hh:hh + 1, :], Co_b[32:33, hh, :])
    fw.dma(sp, o_m[:, :], m_st[:])
    if dbg:
        for mt in range(SEG // 128):
            pass

    pop()
    pop()
    push()
    if do_attn:
        rrep = dscr("rrep", [18 * 128 * 512 + 512], F32)
        oz_scr = [dscr("oz_scr%d" % b_, [SEG, 390], F32) for b_ in range(3)]
        rb = sb("rb", [32, 6], F32)
        ohb = sb("ohb_s", [32, 3, 512], F32)
        mvec = sb("mvec_s", [1, 3, 512], F32)
        ones1 = sb("ones1", [1, 128], F32)
        ag = sb("ag", [128, 3], F32)
        fw.dma(sp, rb[:], rel_bias_d[:])
        for b_ in range(3):
            fw.dma(sp, ohb[:, b_, :], ohb_d[b_])
            fw.dma(sp, mvec[:, b_, :], mvec_d[b_])
            fw.dma(sp, ag[:, b_:b_ + 1], attn_g_d[128 * b_:128 * b_ + 128, :])
        fw.memset(dve, ones1[:], 1.0)
        vfs = [sb("vfs%d" % i, [128, 512], F32) for i in range(2)]
        for b_ in range(3):
            for h in range(6):
                bk_ = pb()
                fw.mm(bk_[:, 0:512], V(rb.h[:, h:h + 1].to_broadcast([32, 128]), rb.buf), ohb[:, b_, :], start=True, stop=False)
                fw.mm(bk_[:, 0:512], ones1[:], mvec[:, b_, :], start=False, stop=True)
                vf = vfs[h % 2]
                fw.copy(act if h % 2 else dve, vf[:], bk_[:, 0:512])
                o0 = (b_ * 6 + h) * 128 * 512
                fw.dma(sp, rrep.ap(o0, [[512, 128], [1, 512]]), vf[:])
        biasT = sb("biasT", [128, 6, 2, 128], F32)
        stmp_2 = [sb("stmp%d" % i_, [128, 512], F32) for i_ in range(2)]
        PT_2 = [sb("PT%d" % i_, [128, 3, 512], BF16) for i_ in range(2)]
        Vp = [sb("Vp%d" % i, [128, 6, 65], BF16) for i in range(3)]
        Vo = [sb("Vo%d" % i, [128, 6, 65], BF16) for i in range(3)]
        ozs = [sb("ozs%d" % i, [128, 390], F32) for i in range(2)]
        srcs = []
        for b_, dil in enumerate((1, 4, 16)):
            if VAR == 'setup' or (VAR.startswith('br') and str(b_) not in VAR):
                continue
            unit = 128 * dil
            for n_ in range(SEG // unit):
                for r_ in range(dil):
                    srcs.append((b_, dil, n_ * unit + r_, n_ == 0 and r_ == 0))

        def att1(t_):
            b_, dil, q0, first = srcs[t_]
            unit = 128 * dil
            pp = t_ % 2
            stmp, PT = stmp_2[pp], PT_2[pp]
            if first:
                for h in range(6):
                    o0 = (b_ * 6 + h) * 128 * 512
                    fw.dma(sp, biasT[:, h, 0, :], rrep.ap(o0 + 255, [[511, 128], [1, 128]]))
                    fw.dma(sp, biasT[:, h, 1, :], rrep.ap(o0 + 127, [[511, 128], [1, 128]]))
            k_own = SEG + q0
            k_prev = k_own - unit

            def vload(tt_):
                b2_, dil2, q02, _f = srcs[tt_]
                ko = SEG + q02
                kp_ = ko - 128 * dil2
                fw.dma(sp, Vp[tt_ % 3].re("p h e -> p (h e)"), v_scr[kp_:kp_ + 127 * dil2 + 1:dil2, :])
                fw.dma(sp, Vo[tt_ % 3].re("p h e -> p (h e)"), v_scr[ko:ko + 127 * dil2 + 1:dil2, :])
            if t_ == 0:
                vload(0)
            if t_ + 1 < len(srcs):
                vload(t_ + 1)
            for c in range(3):
                for hh in range(2):
                    bank = pb()
                    for blk, k0 in enumerate((k_prev, k_own)):
                        fw.mm(bank[:, 128 * blk:128 * blk + 128], KT[64 * hh:64 * hh + 64, c, k0:k0 + 127 * dil + 1:dil],
                              QT[64 * hh:64 * hh + 64, c, q0:q0 + 127 * dil + 1:dil], start=True, stop=True)
                    fw.tt(dve, stmp[:, 256 * hh:256 * hh + 256], bank[:, 0:256],
                          V(biasT.h[:, 2 * c + hh, :, :].rearrange("p b q -> p (b q)"), biasT.buf), ALU.add)
                    fw.activation(PT[:, c, 256 * hh:256 * hh + 256], stmp[:, 256 * hh:256 * hh + 256], AF.Exp)

        def att2(t_):
            b_, dil, q0, first = srcs[t_]
            pp = t_ % 2
            PT = PT_2[pp]
            boz = pb()
            for h in range(6):
                c, hh = divmod(h, 2)
                fw.mm(boz[:, 65 * h:65 * h + 65], PT[:, c, (2 * hh) * 128:(2 * hh) * 128 + 128], Vp[t_ % 3][:, h, :], start=True, stop=False)
                fw.mm(boz[:, 65 * h:65 * h + 65], PT[:, c, (2 * hh + 1) * 128:(2 * hh + 1) * 128 + 128], Vo[t_ % 3][:, h, :], start=False, stop=True)
            fw.copy(act, ozs[pp][:], boz[:, 0:390])
            fw.dma(sp, oz_scr[b_][q0:q0 + 127 * dil + 1:dil, :], ozs[pp][:])

        for t_ in range(len(srcs)):
            att1(t_)
            if t_ > 0:
                att2(t_ - 1)
        if srcs:
            att2(len(srcs) - 1)
        ozl = [sb("ozl%d" % i, [128, 6, 65], F32) for i in range(3)]
        osum = sb("osum", [128, 6, 65], F32)
        rz = sb("rz", [128, 6, 1], F32)
        on = sb("on", [128, 6, 64], F32)
        onb = sb("onb", [128, DA], BF16)
        junkb = sb("junkb", [128, DA], BF16)
        ssa = sb("ssa", [128, 2], F32)
        for mt in range(SEG // 128 if VAR == '' else 0):
            for b_ in range(3):
                fw.dma(sp, ozl[b_].re("p h e -> p (h e)"), oz_scr[b_][128 * mt:128 * mt + 128, :])
            fw.tt(dve, osum[:], ozl[0][:], ozl[1][:], ALU.add)
            fw.tt(dve, osum[:], osum[:], ozl[2][:], ALU.add)
            fw.recip(rz[:], osum[:, :, 64:65])
            fw.tt(dve, on[:], osum[:, :, 0:64], V(rz.h[:].to_broadcast([128, 6, 64]), rz.buf), ALU.mult)
            onf = on.re("p h e -> p (h e)")
            fw.stt(junkb[:], onf, 1.0, onf, ALU.mult, ALU.mult, accum_out=ssa[:, 0:1])
            fw.ts(dve, ssa[:, 1:2], ssa[:, 0:1], 1.0 / DA, EPS, ALU.mult, ALU.add)
            fw.activation(ssa[:, 1:2], ssa[:, 1:2], AF.Ln)
            fw.activation(ssa[:, 1:2], ssa[:, 1:2], AF.Exp, scale=-0.5)
            fw.ts(dve, onb[:], onf, ssa[:, 1:2], None, ALU.mult)
            bt_ = pb()
            pTa = bt_.bitcast(BF16)
            for c in range(3):
                fw.transpose(pTa[:, 128 * c:128 * c + 128], onb[:, 128 * c:128 * c + 128], ident_b[:])
            for c in range(3):
                fw.ts(dve, outaT[:, c, 128 * mt:128 * mt + 128], pTa[:, 128 * c:128 * c + 128], ag[:, c:c + 1], None, ALU.mult)
        if do_sample:
            h_scr = dscr("h_scr", [NS * NHB * 2, 80], F32)
            qp = sb("qp", [128, EB], F32)
            kp = sb("kp", [128, EB], F32)
            vp = sb("vp", [128, 80], F32)
            scp = sb("scp", [128, 4], F32)
            n_p = sb("n_p", [128, EB], F32)
            for half in range(2):
                fw.dma(sp, qp[half:128:2, :], s_q[:, :])
                fw.dma(sp, kp[half:128:2, :], s_k[:, :])
                fw.dma(sp, vp[half:128:2, :], s_v[:, 80 * half:80 * half + 80])
                fw.dma(sp, scp[half:128:2, :], s_sc[:, :])
                fw.dma(sp, n_p[half:128:2, :], V(s_n_d.h.ap().rearrange("b h e -> (b h) e"), s_n_d.buf))
            vw = sb("vw", [128, 80], F32)
            fw.ts(dve, vw[:], vp[:], scp[:, 1:2], None, ALU.mult)
            kwp = sb("kwp", [128, EB], F32)
            fw.ts(dve, kwp[:], kp[:], scp[:, 1:2], None, ALU.mult)
            fw.stt(n_p[:], n_p[:], scp[:, 0:1], kwp[:], ALU.mult, ALU.add)
            fw.dma(sp, V(o_sn.h.ap().rearrange("b (h e) -> (b h) e", h=NHB), o_sn.buf), n_p[0:128:2, :])
            den = sb("den", [128, 4], F32)
            fw.stt(kwp[:], n_p[:], 1.0, qp[:], ALU.mult, ALU.mult, accum_out=den[:, 0:1])
            fw.stt(den[:, 1:2], den[:, 0:1], -1.0, den[:, 0:1], ALU.mult, ALU.max)
            fw.ts(dve, den[:, 1:2], den[:, 1:2], scp[:, 2:3], None, ALU.max)
            fw.recip(den[:, 2:3], den[:, 1:2])
            CH = 10
            Cc = [sb("Cc%d" % i, [128, CH, EB], F32) for i in range(2)]
            Oc = [sb("Oc%d" % i, [128, CH, EB], F32) for i in range(2)]
            hnum = sb("hnum", [128, 80], F32)
            sCv = V(s_C_d.h.ap().rearrange("b h (t v) k -> (b h t) (v k)", t=2), s_C_d.buf)
            oCv = V(o_sC.h.ap().rearrange("b h (t v) k -> (b h t) (v k)", t=2), o_sC.buf)
            for ci in range(80 // CH):
                pq = ci % 2
                cc, oc = Cc[pq], Oc[pq]
                fw.dma(sp, cc.re("p v k -> p (v k)"), V(sCv.ap[:, CH * EB * ci:CH * EB * (ci + 1)], sCv.buf))
                fw.tt(dve, oc[:], V(vw.h[:, CH * ci:CH * ci + CH].unsqueeze(2).to_broadcast([128, CH, EB]), vw.buf),
                      V(kp.h[:].unsqueeze(1).to_broadcast([128, CH, EB]), kp.buf), ALU.mult)
                fw.stt(cc.re("p v k -> p (v k)"), cc.re("p v k -> p (v k)"), scp[:, 0:1], oc.re("p v k -> p (v k)"), ALU.mult, ALU.add)
                fw.dma(sp, V(oCv.ap[:, CH * EB * ci:CH * EB * (ci + 1)], oCv.buf), cc.re("p v k -> p (v k)"))
                fw.tt(dve, oc[:], cc[:], V(qp.h[:].unsqueeze(1).to_broadcast([128, CH, EB]), qp.buf), ALU.mult)
                fw.reduce(hnum[:, CH * ci:CH * ci + CH], oc[:], ALU.add)
            fw.ts(dve, hnum[:], hnum[:], den[:, 2:3], None, ALU.mult)
            fw.dma(sp, h_scr[:, :], hnum[:])
            hs = sb("hs", [NS, NHB, EB], F32)
            hs2 = sb("hs2", [NS, NHB, EB], F32)
            fw.dma(sp, hs.re("p h e -> p (h e)"), V(h_scr.h.ap().rearrange("(b x) e -> b (x e)", b=NS), h_scr.buf))
            hss = sb("hss", [NS, 2, NHB], F32)
            fw.tt(dve, hs2[:], hs[:], hs[:], ALU.mult)
            fw.reduce(hss[:, 0, :], hs2[:], ALU.add)
            fw.ts(dve, hss[:, 1, :], hss[:, 0, :], 1.0 / EB, EPS, ALU.mult, ALU.add)
            fw.activation(hss[:, 1, :], hss[:, 1, :], AF.Ln)
            fw.activation(hss[:, 1, :], hss[:, 1, :], AF.Exp, scale=-0.5)
            fw.tt(dve, hs2[:], hs[:], V(hss.h[:, 1, :].unsqueeze(2).to_broadcast([NS, NHB, EB]), hss.buf), ALU.mult)
            mgr = sb("mgr", [NS, 2, DB], F32)
            fw.dma(sp, mgr[:, 0, :], V(mhg_d.h.ap().rearrange("d o -> (d o)").partition_broadcast(NS), mhg_d.buf))
            fw.dma(sp, mgr[:, 1, :], V(skip_d.h.ap().rearrange("d o -> (d o)").partition_broadcast(NS), skip_d.buf))
            smi = sb("smi", [NS, 2 * DB + 3 * DA], F32)
            fw.dma(sp, smi[:], s_misc[:, :])
            ob_s = sb("ob_s", [NS, DB], F32)
            hs2f = hs2.re("p h e -> p (h e)")
            fw.tt(dve, ob_s[:], hs2f, mgr[:, 0, :], ALU.mult)
            fw.tt(dve, hs2f, smi[:, DB:2 * DB], mgr[:, 1, :], ALU.mult)
            fw.tt(dve, ob_s[:], ob_s[:], hs2f, ALU.add)
            ob_b = sb("ob_b", [NS, DB], BF16)
            fw.tt(dve, ob_b[:], ob_s[:], smi[:, 0:DB], ALU.mult)
            bT3 = pb()
            pT3 = bT3.bitcast(BF16)
            for c in range(5):
                fw.transpose(pT3[:, 128 * c:128 * c + NS], ob_b[:, 128 * c:128 * c + 128], ident_b[0:NS, 0:NS], inc=(c == 4))
            for c in range(5):
                fw.copy(act, outbT[:, c, SEG:SEG + NS], pT3[:, 128 * c:128 * c + NS])

            ohs = sb("ohs_s", [32, 3, 128], F32)
            e16 = sb("e16_s", [NS, NS * 128], F32)
            ecol = sb("ecol_s", [128, NS * NS], F32)
            fw.dma(sp, ohs[:], V(ohs_d.h.ap().rearrange("b k i -> k b i"), ohs_d.buf))
            fw.dma(sp, e16[:], e16_d[:, :])
            fw.dma(sp, ecol[:], ecol_d[:, :])
            bS = sb("bS", [128, 3, 6], F32)
            for b_ in range(3):
                bk_ = pb()
                fw.mm(bk_[:, 0:6], ohs[:, b_, :], rb[:], start=True, stop=True)
                fw.copy(dve, bS[:, b_, :], bk_[:, 0:6])
            qa = sb("qa", [NS, DA], F32)
            ka = sb("ka", [NS, DA], F32)
            va = sb("va_s", [NS, DA], F32)
            fw.copy(dve, qa[:], smi[:, 2 * DB:2 * DB + DA])
            fw.copy(dve, ka[:], smi[:, 2 * DB + DA:2 * DB + 2 * DA])
            fw.copy(dve, va[:], smi[:, 2 * DB + 2 * DA:2 * DB + 3 * DA])
            Kg = [sb("Kg%d" % i, [128, 6, 64], F32) for i in range(2)]
            Vg = [sb("Vg%d" % i, [128, 6, 64], F32) for i in range(2)]
            prd = sb("prd", [128, 6, 64], F32)
            pvz = [sb("pvz%d" % i, [128, DA + 6], F32) for i in range(2)]
            sS = sb("sS", [128, 6], F32)
            bacc = pb()
            reserved.append(bacc)
            nn = 0
            for b in range(NS):
                bq_ = pb()
                fw.mm(bq_[:, 0:DA], e16[:, 128 * b:128 * b + 128], qa[:], start=True, stop=True)
                for b_, dil in enumerate((1, 4, 16)):
                    pq = nn % 2
                    fw.dma(sp, Kg[pq].re("p h e -> p (h e)"), cwk_d[b, 2048 - 128 * dil:2048:dil, :])
                    fw.dma(sp, Vg[pq].re("p h e -> p (h e)"), cwv_d[b, 2048 - 128 * dil:2048:dil, :])
                    fw.tt(dve, prd.re("p h e -> p (h e)"), Kg[pq].re("p h e -> p (h e)"), bq_[:, 0:DA], ALU.mult)
                    fw.reduce(sS[:], prd[:], ALU.add)
                    fw.tt(dve, sS[:], sS[:], bS[:, b_, :], ALU.add)
                    pz = pvz[pq]
                    fw.activation(pz[:, DA:DA + 6], sS[:], AF.Exp)
                    fw.tt(dve, V(pz.h[:, 0:DA].rearrange("p (h e) -> p h e", h=6), pz.buf), Vg[pq][:],
                          V(pz.h[:, DA:DA + 6].unsqueeze(2).to_broadcast([128, 6, 64]), pz.buf), ALU.mult)
                    fw.mm(bacc[0:NS, 0:DA + 6], ecol[:, NS * b:NS * b + NS], pz[:], start=(nn == 0), stop=(nn == 3 * NS - 1), inc=True)
                    nn += 1
            reserved.remove(bacc)
            b0r = sb("b0r", [NS, 6], F32)
            fw.dma(sp, b0r[:], V(rel_bias_d.h.ap()[0:1, :].rearrange("o h -> (o h)").partition_broadcast(NS), rel_bias_d.buf))
            qk = sb("qk", [NS, 6, 64], F32)
            s0 = sb("s0", [NS, 6], F32)
            fw.tt(dve, qk.re("p h e -> p (h e)"), qa[:], ka[:], ALU.mult)
            fw.reduce(s0[:], qk[:], ALU.add)
            fw.tt(dve, s0[:], s0[:], b0r[:], ALU.add)
            fw.activation(s0[:], s0[:], AF.Exp)
            fw.ts(dve, s0[:], s0[:], 3.0, None, ALU.mult)
            oacc = sb("oacc", [NS, DA + 6], F32)
            fw.copy(dve, oacc[:], bacc[0:NS, 0:DA + 6])
            fw.tt(dve, qk[:], V(va.h[:].rearrange("p (h e) -> p h e", h=6), va.buf),
                  V(s0.h[:].unsqueeze(2).to_broadcast([NS, 6, 64]), s0.buf), ALU.mult)
            fw.tt(dve, oacc[:, 0:DA], oacc[:, 0:DA], qk.re("p h e -> p (h e)"), ALU.add)
            fw.tt(dve, oacc[:, DA:DA + 6], oacc[:, DA:DA + 6], s0[:], ALU.add)
            rzs = sb("rzs", [NS, 6], F32)
            fw.recip(rzs[:], oacc[:, DA:DA + 6])
            fw.tt(dve, qk[:], V(oacc.h[:, 0:DA].rearrange("p (h e) -> p h e", h=6), oacc.buf),
                  V(rzs.h[:].unsqueeze(2).to_broadcast([NS, 6, 64]), rzs.buf), ALU.mult)
            qkf = qk.re("p h e -> p (h e)")
            sa2 = sb("sa2", [NS, 2], F32)
            jks = sb("jks", [NS, DA], F32)
            fw.stt(jks[:], qkf, 1.0, qkf, ALU.mult, ALU.mult, accum_out=sa2[:, 0:1])
            fw.ts(dve, sa2[:, 1:2], sa2[:, 0:1], 1.0 / DA, EPS, ALU.mult, ALU.add)
            fw.activation(sa2[:, 1:2], sa2[:, 1:2], AF.Ln)
            fw.activation(sa2[:, 1:2], sa2[:, 1:2], AF.Exp, scale=-0.5)
            oab = sb("oab", [NS, DA], BF16)
            fw.ts(dve, oab[:], qkf, sa2[:, 1:2], None, ALU.mult)
            bT4 = pb()
            pT4 = bT4.bitcast(BF16)
            for c in range(3):
                fw.transpose(pT4[:, 128 * c:128 * c + NS], oab[:, 128 * c:128 * c + 128], ident_b[0:NS, 0:NS], inc=(c == 2))
            for c in range(3):
                fw.ts(dve, outaT[:, c, SEG:SEG + NS], pT4[:, 128 * c:128 * c + NS], ag[:, c:c + 1], None, ALU.mult)

    else:
        fw.memset(dve, outaT[:], 0.0)

    pop()
    pop()
    push()
    x1_scr = dscr("x1_scr", [SEG + 128, D], F32)
    h2_scr = dscr("h2_scr", [SEG // 128 + 1, 128, D], BF16)
    w_out_b = sb("w_out_b", [128, 8, D], BF16)
    wst1 = [sb("wst1_%d" % i, [128, D], F32) for i in range(2)]
    for c in range(8):
        st = wst1[c % 2]
        fw.dma(sp, st[:], w_out_d[128 * c:128 * c + 128, :])
        fw.copy(act if c % 2 else dve, w_out_b[:, c, :], st[:])
    xt1 = [sb("xt1_%d" % i, [128, D], F32) for i in range(2)]
    x1t = [sb("x1t_%d" % i, [128, D], F32) for i in range(2)]
    xn1 = sb("xn1", [128, D], BF16)
    junk1 = sb("junk1", [128, D], BF16)
    h2t = [sb("h2t_%d" % i, [128, D], BF16) for i in range(2)]
    ss1 = sb("ss1", [128, 2], F32)
    NTC = SEG // 128 + (1 if do_sample else 0)
    for mt in range(NTC):
        p = mt % 2
        nr = 128 if mt < SEG // 128 else NS
        if mt < SEG // 128:
            fw.dma(sp, xt1[p][:], xw[128 * (MAIN0 + mt):128 * (MAIN0 + mt) + 128, :])
        else:
            fw.dma(sp, xt1[p][0:nr, :], xs_d[:, :])
        b0, b1_ = pb(), pb()
        for half, bk_ in enumerate((b0, b1_)):
            for c in range(8):
                src_ = outaT[:, c, 128 * mt:128 * mt + nr] if c < 3 else outbT[:, c - 3, 128 * mt:128 * mt + nr]
                fw.mm(bk_[0:nr, 0:512], src_, w_out_b[:, c, 512 * half:512 * half + 512], start=(c == 0), stop=(c == 7))
        fw.tt(dve, x1t[p][0:nr, 0:512], b0[0:nr, 0:512], xt1[p][0:nr, 0:512], ALU.add)
        fw.tt(dve, x1t[p][0:nr, 512:1024], b1_[0:nr, 0:512], xt1[p][0:nr, 512:1024], ALU.add)
        fw.dma(sp, x1_scr[128 * mt:128 * mt + nr, :], x1t[p][0:nr, :])
        fw.stt(junk1[0:nr, :], x1t[p][0:nr, :], 1.0, x1t[p][0:nr, :], ALU.mult, ALU.mult, accum_out=ss1[0:nr, 0:1])
        fw.ts(dve, ss1[0:nr, 1:2], ss1[0:nr, 0:1], 1.0 / D, EPS, ALU.mult, ALU.add)
        fw.activation(ss1[0:nr, 1:2], ss1[0:nr, 1:2], AF.Ln)
        fw.activation(ss1[0:nr, 1:2], ss1[0:nr, 1:2], AF.Exp, scale=-0.5)
        fw.ts(dve, xn1[0:nr, :], x1t[p][0:nr, :], ss1[0:nr, 1:2], None, ALU.mult)
        bT_ = pb()
        pT_ = bT_.bitcast(BF16)
        for c in range(8):
            fw.transpose(pT_[:, 128 * c:128 * c + nr], xn1[0:nr, 128 * c:128 * c + 128], ident_b[0:nr, 0:nr], inc=(c == 7))
        fw.copy(act, h2t[p][:], pT_[:, 0:1024])
        fw.dma(sp, h2_scr[mt], h2t[p][:])

    pop()
    pop()
    push()
    w1b = sb("w1b", [128, 8, DFF], BF16)
    w2b = sb("w2b", [128, 32, D], BF16)
    g2 = sb("g2", [128, 8], F32)
    fing = sb("fing", [128, D], F32)
    for c in range(8):
        fw.dma(sp, g2[:, c:c + 1], norm2_g_d[128 * c:128 * c + 128, :])
    fw.dma(sp, fing[:], V(final_g_d.h.ap().to_broadcast([128, D]), final_g_d.buf))
    wst2 = [sb("wst2_%d" % i, [128, 2048], F32) for i in range(2)]
    n_ = 0
    for c in range(8):
        for hf in range(2):
            st = wst2[n_ % 2]
            fw.dma(sp, st[:], w_ff1_d[128 * c:128 * c + 128, 2048 * hf:2048 * hf + 2048])
            fw.scale_copy(act if n_ % 2 else dve, w1b[:, c, 2048 * hf:2048 * hf + 2048], st[:], g2[:, c:c + 1])
            n_ += 1
    for c in range(16):
        st = wst2[n_ % 2]
        fw.dma(sp, st.re("p (a b) -> p a b", a=2), V(w_ff2_d.h.ap()[256 * c:256 * c + 256, :].rearrange("(a p) d -> p a d", a=2), w_ff2_d.buf))
        fw.copy(act if n_ % 2 else dve, w2b[:, 2 * c:2 * c + 2, :], st.re("p (a b) -> p a b", a=2))
        n_ += 1
    GT_ = 256
    h2g = [sb("h2g_%d" % i, [128, 8, GT_], BF16) for i in range(2)]
    aT = sb("aT", [128, 32, GT_], BF16)
    fw.memset(dve, h2g[0][:], 0.0)
    fw.memset(dve, h2g[1][:], 0.0)
    rl = [sb("rl_%d" % i, [128, 2, GT_], BF16) for i in range(2)]
    x1l = [sb("x1l_%d" % i, [128, D], F32) for i in range(2)]
    x2t = [sb("x2t_%d" % i, [128, D], F32) for i in range(2)]
    yt = [sb("yt_%d" % i, [128, D], F32) for i in range(2)]
    junk2 = sb("junk2", [128, D], BF16)
    ss2 = sb("ss2", [128, 2], F32)
    groups = [[g_ * (GT_ // 128) + t_ for t_ in range(GT_ // 128)] for g_ in range(SEG // GT_)]
    if do_sample:
        groups.append([SEG // 128])
    for g_, tiles_ in enumerate(groups):
        p = g_ % 2
        for t_, mt in enumerate(tiles_):
            fw.dma(sp, h2g[p][:, :, 128 * t_:128 * t_ + 128], V(h2_scr.h.ap()[mt].rearrange("p (c t) -> p c t", c=8), h2_scr.buf))
        for j2 in range(16):
            bk_ = pb()
            for jj in range(2):
                j = 2 * j2 + jj
                for c in range(8):
                    fw.mm(bk_[:, GT_ * jj:GT_ * jj + GT_], w1b[:, c, 128 * j:128 * j + 128], h2g[p][:, c, :], start=(c == 0), stop=(c == 7))
            r_ = rl[j2 % 2]
            fw.activation(r_.re("p a t -> p (a t)"), bk_[:, 0:2 * GT_], AF.Relu)
            fw.tt(dve, aT[:, 2 * j2:2 * j2 + 2, :], r_[:], r_[:], ALU.mult)
        for t_, mt in enumerate(tiles_):
            q = mt % 2
            nr = 128 if mt < SEG // 128 else NS
            fw.dma(sp, x1l[q][0:nr, :], x1_scr[128 * mt:128 * mt + nr, :])
            b0, b1_ = pb(), pb()
            for half, bk_ in enumerate((b0, b1_)):
                for c in range(32):
                    fw.mm(bk_[0:nr, 0:512], aT[:, c, 128 * t_:128 * t_ + nr], w2b[:, c, 512 * half:512 * half + 512], start=(c == 0), stop=(c == 31))
            fw.tt(dve, x2t[q][0:nr, 0:512], b0[0:nr, 0:512], x1l[q][0:nr, 0:512], ALU.add)
            fw.tt(dve, x2t[q][0:nr, 512:1024], b1_[0:nr, 0:512], x1l[q][0:nr, 512:1024], ALU.add)
            fw.stt(junk2[0:nr, :], x2t[q][0:nr, :], 1.0, x2t[q][0:nr, :], ALU.mult, ALU.mult, accum_out=ss2[0:nr, 0:1])
            fw.ts(dve, ss2[0:nr, 1:2], ss2[0:nr, 0:1], 1.0 / D, EPS, ALU.mult, ALU.add)
            fw.activation(ss2[0:nr, 1:2], ss2[0:nr, 1:2], AF.Ln)
            fw.activation(ss2[0:nr, 1:2], ss2[0:nr, 1:2], AF.Exp, scale=-0.5)
            fw.stt(yt[q][0:nr, :], x2t[q][0:nr, :], ss2[0:nr, 1:2], fing[0:nr, :], ALU.mult, ALU.mult)
            if mt < SEG // 128:
                fw.dma(sp, o_y[128 * mt:128 * mt + 128, :], yt[q][:])
            else:
                fw.dma(sp, o_ys[:, :], yt[q][0:nr, :])
    pop()
    pop()
    fw.finish(sp)
    fw.emit()
    return nc


def _get_nc():
    if "nc" not in _NC_CACHE:
        _NC_CACHE["nc"] = build_nc()
    return _NC_CACHE["nc"]


def _t5_bucket_np(dist):
    dist = np.asarray(dist, np.int64)
    df = np.maximum(dist, 1).astype(np.float32)
    large = 16 + (np.log(df / np.float32(16)) / np.float32(np.log(2048 / 16)) * np.float32(16)).astype(np.int32)
    large = np.minimum(large, 31)
    return np.where(dist < 16, dist, large)


def _consts():
    c = {}
    k = np.arange(128)[:, None]
    q = np.arange(128)[None, :]
    c["negmask"] = np.where(k > q, np.float32(MASKV), np.float32(0.0)).astype(np.float32)
    sel = np.zeros((4, 4, 128), np.float32)
    for h in range(4):
        sel[h, h, :] = 1.0
    c["sel"] = sel.reshape(4, 512)
    ohb = np.zeros((3, 32, 512), np.float32)
    mv = np.full((3, 1, 512), np.float32(MASKV), np.float32)
    for bi, dil in enumerate((1, 4, 16)):
        for j in range(0, 129):
            ohb[bi, int(_t5_bucket_np(j * dil)), j + 127] = 1.0
            mv[bi, 0, j + 127] = 0.0
    c["ohb"] = ohb
    c["mvec"] = mv
    ohs = np.zeros((3, 32, 128), np.float32)
    for bi, dil in enumerate((1, 4, 16)):
        for i in range(128):
            ohs[bi, int(_t5_bucket_np((128 - i) * dil)), i] = 1.0
    c["ohs"] = ohs
    e16 = np.zeros((NS, NS, 128), np.float32)
    ecol = np.zeros((128, NS, NS), np.float32)
    for b in range(NS):
        e16[b, b, :] = 1.0
        ecol[:, b, b] = 1.0
    c["e16"] = e16.reshape(NS, NS * 128)
    c["ecol"] = ecol.reshape(128, NS * NS)
    return c


def make_in_maps(x_prompt, x_sample, cache_win_k, cache_win_v, state_conv, state_C, state_n, state_m,
                 rel_bias, norm1_g, w_in, gate_bias, conv_w, conv_b, wq_head, wk_head,
                 attn_out_g, mh_norm_g, skip, w_out, norm2_g, w_ff1, w_ff2, final_g):
    f = lambda a: np.ascontiguousarray(np.asarray(a, dtype=np.float32))
    x_prompt = f(x_prompt)
    cst = _consts()
    shared = {
        "w_in": f(w_in[0]), "norm1_g": f(norm1_g[0]).reshape(D, 1), "gate_bias": f(gate_bias[0]).reshape(8, 1),
        "conv_wT": f(np.asarray(conv_w[0]).T), "conv_w": f(conv_w[0]), "conv_b": f(conv_b[0]).reshape(DB, 1),
        "wq": f(wq_head[0]), "wk": f(wk_head[0]), "mhg": f(mh_norm_g[0]).reshape(DB, 1),
        "skip": f(skip[0]).reshape(DB, 1),
        "rel_bias": f(rel_bias), "attn_g": f(attn_out_g[0]).reshape(DA, 1), "w_out": f(w_out[0]),
        "norm2_g": f(norm2_g[0]).reshape(D, 1), "w_ff1": f(w_ff1[0]), "w_ff2": f(w_ff2[0]),
        "final_g": f(final_g).reshape(1, D),
    }
    shared.update(cst)
    in_maps = []
    for c in range(NCORES):
        b, s = c // 4, c % 4
        lo = SEG * s - (WIN - SEG)
        xwin = np.zeros((WIN, D), np.float32)
        a0 = max(lo, 0)
        xwin[a0 - lo:] = x_prompt[b, a0:lo + WIN]
        valid = np.array([(lo + 128 * t) >= 0 for t in range(NT)])
        m = dict(shared)
        m["xw"] = xwin
        m["tmA"] = np.tile(np.where(valid, 0.0, NEG).astype(np.float32)[None, :], (4, 1))
        m["tmV"] = np.tile(np.where(valid, -1.0, 0.0).astype(np.float32)[None, :], (4, 1))
        m["tmO"] = np.tile(np.where(valid, 1.0, 0.0).astype(np.float32)[None, :], (128, 1))
        sl = slice(NS * c, NS * c + NS)
        m["xs"] = f(x_sample[sl, 0])
        m["cwk"] = f(cache_win_k[0, sl]).reshape(NS, 2048, DA)
        m["cwv"] = f(cache_win_v[0, sl]).reshape(NS, 2048, DA)
        m["s_conv"] = f(state_conv[0, sl])
        m["s_C"] = f(state_C[0, sl])
        m["s_n"] = f(state_n[0, sl])
        m["s_m"] = f(state_m[0, sl])
        in_maps.append(m)
    return in_maps


def _filter(nc_inputs, m):
    return {k: v for k, v in m.items() if k in nc_inputs}


def kernel(**inputs):
    nc = _get_nc()
    in_maps = make_in_maps(**inputs)
    names = _NC_CACHE["in_names"]
    in_maps = [_filter(names, m) for m in in_maps]
    res = run_bass_kernel_spmd(nc, in_maps, core_ids=list(range(NCORES)))
    R = res.results
    return assemble(R)


def assemble(R):
    f = np.float32
    y_prompt = np.stack([np.concatenate([R[4 * b + s]["o_y"] for s in range(4)], 0) for b in range(2)]).astype(f)
    y_sample = np.concatenate([R[c]["o_ys"] for c in range(NCORES)], 0).reshape(128, 1, D).astype(f)
    last = [3, 7]
    p_k = np.stack([R[c]["o_wk"] for c in last]).reshape(1, 2, 2048, 6, 64).astype(f)
    p_v = np.stack([R[c]["o_wv"] for c in last]).reshape(1, 2, 2048, 6, 64).astype(f)
    p_conv = np.stack([R[c]["o_conv"] for c in last]).reshape(1, 2, 3, DB).astype(f)
    p_C = np.stack([R[c]["o_C"] for c in last]).reshape(1, 2, NHB, EB, EB).astype(f)
    p_n = np.stack([R[c]["o_n"] for c in last]).reshape(1, 2, NHB, EB).astype(f)
    p_m = np.stack([R[c]["o_m"] for c in last]).reshape(1, 2, NHB).astype(f)
    cat = lambda k: np.concatenate([R[c][k] for c in range(NCORES)], 0)
    s_k = cat("o_swk").reshape(1, 128, 2048, 6, 64).astype(f)
    s_v = cat("o_swv").reshape(1, 128, 2048, 6, 64).astype(f)
    s_conv = cat("o_sconv").reshape(1, 128, 3, DB).astype(f)
    s_C = cat("o_sC").reshape(1, 128, NHB, EB, EB).astype(f)
    s_n = cat("o_sn").reshape(1, 128, NHB, EB).astype(f)
    s_m = cat("o_sm").reshape(1, 128, NHB).astype(f)
    return (y_prompt, y_sample, p_k, p_v, p_conv, p_C, p_n, p_m, s_k, s_v, s_conv, s_C, s_n, s_m)
```

```python
import numpy as np
import os
from contextlib import ExitStack
VAR = ''
import concourse.bass as bass
import concourse.mybir as mybir
from concourse.bass_utils import run_bass_kernel_spmd

F32 = mybir.dt.float32
BF16 = mybir.dt.bfloat16
AF = mybir.ActivationFunctionType
ALU = mybir.AluOpType
AX = mybir.AxisListType

D = 1024
DIN = 2440
DA = 384
DB = 640
NHB = 4
EB = 160
DFF = 4096
NCORES = 8
SEG = 2048
WIN = 8192
NT = WIN // 128
MAIN0 = 48
HALO0 = 32
NS = 16
EPS = 1e-6
NEG = -1e30
MASKV = -30000.0


class Eng:
    def __init__(self, fw, name, handle, sem):
        self.fw, self.name, self.h, self.sem = fw, name, handle, sem
        self.count = 0
        self.prog = []
        self.waited = {}
        self.dsems = []
        self.dtot = []
        self.dnext = 0

    def wait(self, sem, val):
        if val <= 0:
            return
        k = id(sem)
        if self.waited.get(k, 0) >= val:
            return
        self.waited[k] = val
        self.prog.append(lambda h, sem=sem, val=val: h.wait_ge(sem, val))


class Buf:
    def __init__(self, name):
        self.name = name
        self.w = {}
        self.r = {}
        self.excl = False


class V:
    def __init__(self, ap, buf):
        self.ap = ap
        self.bufs = list(buf) if isinstance(buf, (list, tuple)) else [buf]

    @property
    def buf(self):
        return self.bufs if len(self.bufs) > 1 else self.bufs[0]


class T:
    def __init__(self, handle, name):
        self.h = handle
        self._buf = Buf(name)
        self.subs = None

    @property
    def buf(self):
        return self.subs if self.subs else self._buf

    def split(self, n):
        self.subs = [Buf("%s.%d" % (self._buf.name, k)) for k in range(n)]
        return self

    def sub(self, k):
        t = T.__new__(T)
        t.h = self.h
        t._buf = self.subs[k]
        t.subs = None
        return t

    def __getitem__(self, key):
        return V(self.h[key], self.buf)

    def ap(self, offset, pat):
        return V(bass.AP(self.h, offset, pat), self.buf)

    def re(self, pat, **kw):
        return V(self.h.ap().rearrange(pat, **kw) if hasattr(self.h, "ap") else self.h[:].rearrange(pat, **kw), self.buf)

    def bitcast(self, dt):
        t = T.__new__(T)
        t.h = self.h.bitcast(dt)
        t._buf = self._buf
        t.subs = self.subs
        return t


class FW:
    def __init__(self, nc):
        self.nc = nc
        mk = lambda n, h: Eng(self, n, h, nc.alloc_semaphore("sem_" + n))
        self.pe = mk("pe", nc.tensor)
        self.act = mk("act", nc.scalar)
        self.dve = mk("dve", nc.vector)
        self.pool = mk("pool", nc.gpsimd)
        self.sp = mk("sp", nc.sync)
        self.engs = [self.pe, self.act, self.dve, self.pool, self.sp]
        for e in (self.sp, self.pool, self.act):
            n = 12
            e.dsems = [nc.alloc_semaphore("dsem_%s_%d" % (e.name, i)) for i in range(n)]
            e.dtot = [0] * n
        self.same_engine_sync = True

    def _deps(self, eng, reads, writes):
        for v in reads:
            for b in v.bufs:
                for sem, val in b.w.values():
                    self._w(eng, sem, val, True)
                if b.excl:
                    for sem, val in b.r.values():
                        self._w(eng, sem, val, False)
        for v in writes:
            for b in v.bufs:
                for sem, val in list(b.w.values()) + list(b.r.values()):
                    self._w(eng, sem, val, False)

    def _w(self, eng, sem, val, raw):
        if sem is eng.sem:
            if eng is self.pe or not self.same_engine_sync or not raw:
                return
        eng.wait(sem, val)

    def _mark(self, sem, val, reads, writes):
        for v in reads:
            for b in v.bufs:
                b.r[id(sem)] = (sem, val)
        for v in writes:
            for b in v.bufs:
                b.w[id(sem)] = (sem, val)

    def op(self, eng, fn, reads, writes, inc=True):
        reads = [v for v in reads if isinstance(v, V)]
        self._deps(eng, reads, writes)
        if inc:
            eng.count += 1
            c = eng.count
            eng.prog.append(lambda h, fn=fn, s=eng.sem: fn(h).then_inc(s, 1))
            self._mark(eng.sem, c, reads, writes)
        else:
            eng.prog.append(lambda h, fn=fn: fn(h))

    def dma(self, q, out, in_, **kw):
        self._deps(q, [in_], [out])
        j = q.dnext
        q.dnext = (j + 1) % len(q.dsems)
        sem = q.dsems[j]
        q.wait(sem, q.dtot[j])
        q.dtot[j] += 16
        tot = q.dtot[j]
        q.prog.append(lambda h, o=out.ap, i=in_.ap, s=sem, kw=kw: h.dma_start(out=o, in_=i, **kw).then_inc(s, 16))
        self._mark(sem, tot, [in_], [out])

    def finish(self, eng, skip=()):
        for e in self.engs:
            if e is not eng:
                eng.wait(e.sem, e.count)
            if e in skip:
                continue
            for s, t in zip(e.dsems, e.dtot):
                eng.wait(s, t)

    def barrier(self):
        for e in self.engs:
            self.finish(e, skip=(self.pool,))

    def emit(self):
        nc = self.nc
        with nc.Block() as block:
            @block.tensor
            def _(h):
                for f in self.pe.prog:
                    f(h)

            @block.scalar
            def _(h):
                for f in self.act.prog:
                    f(h)

            @block.vector
            def _(h):
                for f in self.dve.prog:
                    f(h)

            @block.gpsimd
            def _(h):
                for f in self.pool.prog:
                    f(h)

            @block.sync
            def _(h):
                for f in self.sp.prog:
                    f(h)

    def mm(self, out, lhsT, rhs, start=True, stop=True, inc=None):
        if inc is None:
            inc = stop
        self.op(self.pe, lambda h, o=out.ap, l=lhsT.ap, r=rhs.ap: h.matmul(o, l, r, start=start, stop=stop, skip_group_check=True),
                [lhsT, rhs], [out], inc=inc)

    def transpose(self, out, in_, ident, inc=True):
        self.op(self.pe, lambda h, o=out.ap, i=in_.ap, d=ident.ap: h.transpose(o, i, d), [in_, ident], [out], inc=inc)

    def activation(self, out, in_, func, bias=0.0, scale=1.0, accum_out=None, eng=None):
        rd = [in_, bias, scale]
        wr = [out] + ([accum_out] if accum_out is not None else [])
        b = bias.ap if isinstance(bias, V) else bias
        s = scale.ap if isinstance(scale, V) else scale
        kw = {}
        if accum_out is not None:
            kw["accum_out"] = accum_out.ap
        self.op(self.act, lambda h, o=out.ap, i=in_.ap: h.activation(o, i, func, bias=b, scale=s, **kw), rd, wr)

    def tt(self, eng, out, in0, in1, op):
        self.op(eng, lambda h, o=out.ap, a=in0.ap, b=in1.ap: h.tensor_tensor(o, a, b, op), [in0, in1], [out])

    def ts(self, eng, out, in0, s1, s2, op0, op1=None, accum_out=None):
        a1 = s1.ap if isinstance(s1, V) else s1
        a2 = s2.ap if isinstance(s2, V) else s2
        kw = {}
        if op1 is not None:
            kw["op1"] = op1
        wr = [out]
        if accum_out is not None:
            kw["accum_out"] = accum_out.ap
            wr.append(accum_out)
        self.op(eng, lambda h, o=out.ap, a=in0.ap: h.tensor_scalar(o, a, a1, a2, op0, **kw), [in0, s1, s2], wr)

    def stt(self, out, in0, scalar, in1, op0, op1, accum_out=None):
        sc = scalar.ap if isinstance(scalar, V) else scalar
        kw = {}
        wr = [out]
        if accum_out is not None:
            kw["accum_out"] = accum_out.ap
            wr.append(accum_out)
        self.op(self.dve, lambda h, o=out.ap, a=in0.ap, b=in1.ap: h.scalar_tensor_tensor(o, a, sc, b, op0, op1, **kw),
                [in0, scalar, in1], wr)

    def scale_copy(self, eng, out, in_, sc):
        if eng is self.act:
            self.activation(out, in_, AF.Copy, scale=sc)
        else:
            self.ts(eng, out, in_, sc, None, ALU.mult)

    def copy(self, eng, out, in_):
        if eng is self.act:
            self.op(eng, lambda h, o=out.ap, i=in_.ap: h.copy(o, i), [in_], [out])
        else:
            self.op(eng, lambda h, o=out.ap, i=in_.ap: h.tensor_copy(o, i), [in_], [out])

    def memset(self, eng, out, val):
        self.op(eng, lambda h, o=out.ap: h.memset(o, val), [], [out])

    def reduce(self, out, in_, op, axis=AX.X):
        self.op(self.dve, lambda h, o=out.ap, i=in_.ap: h.tensor_reduce(o, i, axis, op), [in_], [out])

    def recip(self, out, in_):
        self.op(self.dve, lambda h, o=out.ap, i=in_.ap: h.reciprocal(o, i), [in_], [out])

    def scan(self, out, d0, d1, initial, op0, op1):
        ini = initial.ap if isinstance(initial, V) else initial
        self.op(self.dve, lambda h, o=out.ap, a=d0.ap, b=d1.ap: h.tensor_tensor_scan(o, a, b, ini, op0, op1),
                [d0, d1, initial], [out])


_NC_CACHE = {}


def build_nc(stage=99, prefix_super=True, dbg=False, do_sample=True, do_attn=True, do_ffn=True, tile_list=None, do_cout=True):
    nc = bass.Bass("TRN2", target_bir_lowering=False)
    fw = FW(nc)
    pe, act, dve, pool, sp = fw.pe, fw.act, fw.dve, fw.pool, fw.sp

    in_names = _NC_CACHE.setdefault("in_names", set())

    def din(name, shape, dt=F32):
        in_names.add(name)
        return T(nc.dram_tensor(name, list(shape), dt, kind="ExternalInput"), name)

    def dout(name, shape, dt=F32):
        return T(nc.dram_tensor(name, list(shape), dt, kind="ExternalOutput"), name)

    def dscr(name, shape, dt=F32):
        return T(nc.dram_tensor(name, list(shape), dt, kind="Internal"), name)

    stacks = []

    def push():
        stacks.append(ExitStack())

    def pop():
        fw.barrier()
        stacks.pop().close()

    def sb(name, shape, dt=F32):
        return T(stacks[-1].enter_context(nc.sbuf_tensor(name, list(shape), dt)), name)

    push()

    xw = din("xw", [WIN, D])
    tmA = din("tmA", [4, NT])
    tmV = din("tmV", [4, NT])
    tmO = din("tmO", [128, NT])
    negmask_d = din("negmask", [128, 128])
    sel_d = din("sel", [4, 4 * 128])
    w_in = din("w_in", [D, DIN])
    norm1_g = din("norm1_g", [D, 1])
    gate_bias = din("gate_bias", [8, 1])
    conv_wT = din("conv_wT", [DB, 4])
    conv_b = din("conv_b", [DB, 1])
    wq_d = din("wq", [NHB, EB, EB])
    wk_d = din("wk", [NHB, EB, EB])
    mhg_d = din("mhg", [DB, 1])
    skip_d = din("skip", [DB, 1])

    rel_bias_d = din("rel_bias", [32, 6])
    ohb_d = din("ohb", [3, 32, 512])
    mvec_d = din("mvec", [3, 1, 512])
    attn_g_d = din("attn_g", [DA, 1])
    w_out_d = din("w_out", [D, D])
    norm2_g_d = din("norm2_g", [D, 1])
    w_ff1_d = din("w_ff1", [D, DFF])
    w_ff2_d = din("w_ff2", [DFF, D])
    final_g_d = din("final_g", [1, D])

    xs_d = din("xs", [NS, D])
    cwk_d = din("cwk", [NS, 2048, DA])
    cwv_d = din("cwv", [NS, 2048, DA])
    s_conv_d = din("s_conv", [NS, 3, DB])
    s_C_d = din("s_C", [NS, NHB, EB, EB])
    s_n_d = din("s_n", [NS, NHB, EB])
    s_m_d = din("s_m", [NS, NHB])
    conv_w_d = din("conv_w", [4, DB])
    ohs_d = din("ohs", [3, 32, 128])
    e16_d = din("e16", [NS, NS * 128])
    ecol_d = din("ecol", [128, NS * NS])

    o_y = dout("o_y", [SEG, D])
    o_ys = dout("o_ys", [NS, D])
    o_swk = dout("o_swk", [NS, 2048, DA])
    o_swv = dout("o_swv", [NS, 2048, DA])
    o_sconv = dout("o_sconv", [NS, 3, DB])
    o_sC = dout("o_sC", [NS, NHB, EB, EB])
    o_sn = dout("o_sn", [NS, NHB * EB])
    o_sm = dout("o_sm", [NS, NHB])
    o_wk = dout("o_wk", [SEG, DA])
    o_wv = dout("o_wv", [SEG, DA])
    o_conv = dout("o_conv", [3, DB])
    o_C = dout("o_C", [NHB, EB, EB])
    o_n = dout("o_n", [NHB, EB])
    o_m = dout("o_m", [NHB, 1])
    if dbg:
        o_dbg = dout("o_dbg", [SEG, DB])

    ps = [T(nc.alloc_psum_tensor("ps%d" % i, [128, 512], F32), "ps%d" % i) for i in range(8)]
    for b_ in ps:
        b_._buf.excl = True
    psn = [0]

    reserved = []

    def pb():
        while True:
            b = ps[psn[0] % 8]
            psn[0] += 1
            if b not in reserved:
                return b

    ident_f = sb("ident_f", [128, 128], F32)
    ident_b = sb("ident_b", [128, 128], BF16)
    iot = sb("iot", [128, 128], F32)
    fw.op(pool, lambda h, o=iot[:].ap: h.iota(o, [[1, 128]], base=0, channel_multiplier=-1,
                                              allow_small_or_imprecise_dtypes=True), [], [iot[:]])
    fw.ts(dve, ident_f[:], iot[:], 0.0, None, ALU.is_equal)
    fw.copy(dve, ident_b[:], ident_f[:])
    negmask = sb("negmask_s", [128, 128], F32)
    fw.dma(sp, negmask[:], negmask_d[:])
    sel = sb("sel_s", [4, 4 * 128], F32)
    fw.dma(sp, sel[:], sel_d[:])
    tmA_s = sb("tmA_s", [4, NT], F32)
    tmV_s = sb("tmV_s", [4, NT], F32)
    tmO_s = sb("tmO_s", [128, NT], F32)
    fw.dma(sp, tmA_s[:], tmA[:])
    fw.dma(sp, tmV_s[:], tmV[:])
    fw.dma(sp, tmO_s[:], tmO[:])
    ones4 = sb("ones4", [4, 128], F32)
    fw.memset(dve, ones4[:], 1.0)
    gb_i = sb("gb_i", [4, 1], F32)
    gb_fn = sb("gb_fn", [4, 1], F32)
    fw.dma(sp, gb_i[:], gate_bias[0:4, :])
    fw.dma(sp, gb_fn[:], gate_bias[4:8, :])
    fw.ts(dve, gb_fn[:], gb_fn[:], -1.0, None, ALU.mult)
    cw = sb("cw", [128, 5, 4], F32)
    cb = sb("cb", [128, 5], F32)
    mhg = sb("mhg_s", [128, 5], F32)
    skp = sb("skp_s", [128, 5], F32)
    for c in range(5):
        fw.dma(sp, cw[:, c, :], conv_wT[128 * c:128 * c + 128, :])
        fw.dma(sp, cb[:, c:c + 1], conv_b[128 * c:128 * c + 128, :])
        fw.dma(sp, mhg[:, c:c + 1], mhg_d[128 * c:128 * c + 128, :])
        fw.dma(sp, skp[:, c:c + 1], skip_d[128 * c:128 * c + 128, :])

    if do_sample:
        for (src, dst) in ((cwk_d, o_swk), (cwv_d, o_swv)):
            for g in range(NS):
                for hh in range(4):
                    fw.dma(pool, dst[g, 512 * hh:min(512 * hh + 512, 2047), :],
                           src[g, 512 * hh + 1:min(512 * hh + 513, 2048), :])
    push()
    outaT = sb("outaT", [128, 3, SEG + 128], BF16)
    outbT = sb("outbT", [128, 5, SEG + 128], BF16)
    push()
    KT = sb("KT", [128, 3, 2 * SEG], BF16)
    QT = sb("QT", [128, 3, SEG], BF16)
    push()
    w_in_b = sb("w_in_b", [128, 8, DIN], BF16)
    g1 = sb("g1", [128, 8], F32)
    wq_b = sb("wq_b", [128, 5, DB], BF16)
    wk_b = sb("wk_b", [128, 5, DB], BF16)
    v_scr = dscr("v_scr", [2 * SEG, 6 * 65], BF16)

    CTa = sb("CTa", [128, NHB, EB + 1], F32)
    CTb = sb("CTb", [32, NHB, EB + 1], F32)
    CTa_b = sb("CTa_b", [128, NHB, EB + 1], BF16)
    CTb_b = sb("CTb_b", [32, NHB, EB + 1], BF16)
    m_st = sb("m_st", [4, 1], F32)
    fw.memset(dve, CTa[:], 0.0)
    fw.memset(dve, CTb[:], 0.0)
    CTa.split(NHB)
    CTb.split(NHB)
    fw.memset(dve, CTa_b[:], 0.0)
    fw.memset(dve, CTb_b[:], 0.0)
    fw.memset(dve, m_st[:], 0.0)

    xt = [sb("xt%d" % i, [128, D], F32) for i in range(2)]
    xn_2 = [sb("xn_%d" % i_, [128, D], BF16) for i_ in range(2)]
    xn = xn_2[0]
    ss_2 = [sb("ss_%d" % i_, [128, 1], F32) for i_ in range(2)]
    ss = ss_2[0]
    rstd_2 = [sb("rstd_%d" % i_, [128, 1], F32) for i_ in range(2)]
    rstd = rstd_2[0]
    junk = sb("junk", [128, D], BF16)
    push()
    for c in range(8):
        fw.dma(sp, g1[:, c:c + 1], norm1_g[128 * c:128 * c + 128, :])
    wstage = [sb("wstage%d" % i, [128, 1600], F32) for i in range(2)]
    HW_ = DIN // 2
    for c in range(8):
        for hf in range(2):
            st = wstage[hf]
            fw.dma(sp, st[:, 0:HW_], w_in[128 * c:128 * c + 128, HW_ * hf:HW_ * hf + HW_])
            if hf == 0:
                fw.ts(dve, st[:, 0:DA], st[:, 0:DA], 0.125, None, ALU.mult)
            fw.scale_copy(act if hf else dve, w_in_b[:, c, HW_ * hf:HW_ * hf + HW_], st[:, 0:HW_], g1[:, c:c + 1])
    for (src, dst, scl) in ((wq_d, wq_b, 1.0), (wk_d, wk_b, float(EB) ** -0.5)):
        stv = wstage[0] if src is wq_d else wstage[1]
        fw.memset(dve, stv[:, 0:5 * 320], 0.0)
        for c in range(5):
            lo, hi = 128 * c, 128 * c + 128
            for hh in range(NHB):
                a0, a1 = max(lo, EB * hh), min(hi, EB * hh + EB)
                if a0 >= a1:
                    continue
                slot = hh - (lo // EB)
                fw.dma(sp, stv[a0 - lo:a1 - lo, 320 * c + 160 * slot:320 * c + 160 * slot + 160],
                       src[hh, a0 - EB * hh:a1 - EB * hh, :])
        fw.memset(dve, dst[:], 0.0)
        for c in range(5):
            h0 = (128 * c) // EB
            nh = 2 if (128 * c + 127) // EB > h0 else 1
            fw.ts(dve, dst[:, c, EB * h0:EB * h0 + EB * nh], stv[:, 320 * c:320 * c + EB * nh], scl, None, ALU.mult)

    pop()
    i4 = sb("i4", [4, 4], F32)
    fw.copy(dve, i4[:], ident_f[0:4, 0:4])
    def load_x(i):
        fw.dma(sp, xt[i % 2][:], xw[128 * i:128 * i + 128, :])

    xlast = sb("xlast", [128, 5, 3], F32)
    fw.memset(dve, xlast[:], 0.0)
    NPRE = MAIN0 // 4 if prefix_super else 0
    if NPRE:
        push()
        hT4 = sb("hT4", [128, 8, 512], BF16)
        xT4 = sb("xT4", [128, 5, 515], F32)
        fw.memset(dve, xT4[:], 0.0)
        xT4.split(5)
        cT4 = sb("cT4", [128, 5, 512], F32).split(5)
        caT4 = sb("caT4", [128, 5, 512], BF16)
        vext4 = sb("vext4", [128, 4, NHB, EB + 1], BF16)
        fw.memset(dve, vext4[:], 1.0)
        vext4.split(4)
        ktok4 = sb("ktok4", [128, 4, DB], BF16).split(4)
        kw4 = sb("kw4", [128, 4, NHB, EB], BF16).split(4)
        gi4 = sb("gi4", [4, 512], F32)
        gl4 = sb("gl4", [4, 512], F32)
        ga4 = sb("ga4", [4, 512], F32)
        gu4 = sb("gu4", [4, 512], F32)
        gM4 = sb("gM4", [4, 512], F32)
        gdl4 = sb("gdl4", [4, 1], F32)
        TM4 = sb("TM4", [128, 16], F32)
        TMe4 = sb("TMe4", [128, 16], F32)
        dec4 = sb("dec4", [128, 4], F32)
        vatt4 = [sb("vatt4_%d" % i_, [128, 6, 65], BF16) for i_ in range(1)] * 2
        load_x(0)
        for g in range(NPRE):
            halo = 4 * g >= HALO0
            for t in range(4):
                i = 4 * g + t
                p = i % 2
                xn, ss, rstd = xn_2[p], ss_2[p], rstd_2[p]
                if i + 1 < NT:
                    load_x(i + 1)
                fw.stt(junk[:], xt[p][:], 1.0, xt[p][:], ALU.mult, ALU.mult, accum_out=ss[:])
                fw.ts(dve, rstd[:], ss[:], 1.0 / D, EPS, ALU.mult, ALU.add)
                fw.activation(rstd[:], rstd[:], AF.Ln)
                fw.activation(rstd[:], rstd[:], AF.Exp, scale=-0.5)
                fw.ts(dve, xn[:], xt[p][:], rstd[:], None, ALU.mult)
                bT = pb()
                pT = bT.bitcast(BF16)
                for c in range(8):
                    fw.transpose(pT[:, 128 * c:128 * c + 128], xn[:, 128 * c:128 * c + 128], ident_b[:], inc=(c == 7))
                fw.copy(act, hT4[:, :, 128 * t:128 * t + 128], V(pT.h[:, 0:1024].rearrange("p (c t) -> p c t", c=8), pT.buf))
            for t in range(4):
                i = 4 * g + t
                for (c0, n) in ((3 * DA, 512), (3 * DA + 512, 128)):
                    b_ = pb()
                    for c in range(8):
                        fw.mm(b_[:, 0:n], hT4[:, c, 128 * t:128 * t + 128], w_in_b[:, c, c0:c0 + n], start=(c == 0), stop=(c == 7))
                    if n == 512:
                        fw.copy(act, V(vext4.h[:, t, 0:3, 0:EB], vext4.subs[t]), V(b_.h[:, 0:480].rearrange("p (h e) -> p h e", h=3), b_.buf))
                        fw.copy(dve, V(vext4.h[:, t, 3, 0:32], vext4.subs[t]), b_[:, 480:512])
                    else:
                        fw.copy(dve, V(vext4.h[:, t, 3, 32:EB], vext4.subs[t]), b_[:, 0:128])
                if halo:
                    at = i - HALO0
                    b_ = pb()
                    for c in range(8):
                        fw.mm(b_[:, 0:DA], hT4[:, c, 128 * t:128 * t + 128], w_in_b[:, c, 2 * DA:3 * DA], start=(c == 0), stop=(c == 7))
                    va = vatt4[i % 2]
                    fw.copy(act, va[:, :, 0:64], V(b_.h[:, 0:DA].rearrange("p (h e) -> p h e", h=6), b_.buf))
                    fw.copy(dve, va[:, :, 64:65], V(tmO_s.h[:, i:i + 1].unsqueeze(1).to_broadcast([128, 6, 1]), tmO_s.buf))
                    fw.dma(sp, v_scr[128 * at:128 * at + 128, :], va.re("p h e -> p (h e)"))
            for j in range(5):
                b_ = pb()
                for c in range(8):
                    fw.mm(b_[:, 0:512], w_in_b[:, c, 3 * DA + 128 * j:3 * DA + 128 * j + 128], hT4[:, c, :], start=(c == 0), stop=(c == 7))
                fw.copy(act if j % 2 else dve, V(xT4.h[:, j, 3:515], xT4.subs[j]), b_[:, 0:512])
            if halo:
                at0 = 4 * g - HALO0
                for j in range(3):
                    b_ = pb()
                    for c in range(8):
                        fw.mm(b_[:, 0:512], w_in_b[:, c, DA + 128 * j:DA + 128 * j + 128], hT4[:, c, :], start=(c == 0), stop=(c == 7))
                    fw.copy(act, KT[:, j, 128 * at0:128 * at0 + 512], b_[:, 0:512])
            bgi, bgf = pb(), pb()
            for (b_, c0) in ((bgi, 3 * DA + 2 * DB), (bgf, 3 * DA + 2 * DB + 4)):
                for c in range(8):
                    fw.mm(b_[0:4, 0:512], w_in_b[:, c, c0:c0 + 4], hT4[:, c, :], start=(c == 0), stop=(c == 7))
            fw.ts(dve, gi4[:], bgi[0:4, 0:512], gb_i[:], tmA_s[:, 4 * g:4 * g + 1], ALU.add, ALU.add)
            fw.activation(gl4[:], bgf[0:4, 0:512], AF.Exp, bias=gb_fn[:], scale=-1.0)
            fw.activation(gl4[:], gl4[:], AF.Ln, bias=1.0)
            fw.ts(dve, gl4[:], gl4[:], tmV_s[:, 4 * g:4 * g + 1], None, ALU.mult)
            for c in range(5):
                fw.activation(cT4.sub(c)[:, c, :], V(xT4.h[:, c, 0:512], xT4.subs[c]), AF.Identity, bias=cb[:, c:c + 1], scale=cw[:, c, 0:1])
            for k in range(1, 4):
                for c in range(5):
                    fw.stt(cT4.sub(c)[:, c, :], V(xT4.h[:, c, k:k + 512], xT4.subs[c]), cw[:, c, k:k + 1], cT4.sub(c)[:, c, :], ALU.mult, ALU.add)
            for c in range(5):
                fw.copy(dve, V(xT4.h[:, c, 0:3], xT4.subs[c]), V(xT4.h[:, c, 512:515], xT4.subs[c]))
            fw.activation(caT4[:], cT4[:], AF.Silu)
            for t in range(4):
                bk1, bk2 = pb(), pb()
                for c in range(5):
                    fw.mm(bk1[:, 0:512], caT4[:, c, 128 * t:128 * t + 128], wk_b[:, c, 0:512], start=(c == 0), stop=(c == 4))
                for c in range(5):
                    fw.mm(bk2[:, 0:128], caT4[:, c, 128 * t:128 * t + 128], wk_b[:, c, 512:640], start=(c == 0), stop=(c == 4))
                fw.copy(act, V(ktok4.h[:, t, 0:512], ktok4.subs[t]), bk1[:, 0:512])
                fw.copy(act, V(ktok4.h[:, t, 512:640], ktok4.subs[t]), bk2[:, 0:128])
            fw.scan(ga4[:], V(ones4.h[:, 0:1].to_broadcast([4, 512]), ones4.buf), gl4[:], 0.0, ALU.mult, ALU.add)
            fw.tt(dve, gu4[:], gi4[:], ga4[:], ALU.subtract)
            fw.scan(gM4[:], gu4[:], gu4[:], m_st[:], ALU.max, ALU.max)
            fw.ts(dve, gi4[:], gu4[:], gM4[:, 511:512], None, ALU.subtract)
            fw.ts(dve, gdl4[:], gM4[:, 511:512], -1.0, m_st[:], ALU.mult, ALU.add)
            bt = pb()
            for t in range(4):
                fw.transpose(bt[:, 4 * t:4 * t + 4], gi4[:, 128 * t:128 * t + 128], ident_f[0:4, 0:4])
            fw.copy(dve, TM4[:], bt[:, 0:16])
            fw.activation(TMe4[:], TM4[:], AF.Exp)
            bd = pb()
            fw.mm(bd[:, 0:4], V(gdl4.h[:, 0:1].to_broadcast([4, 128]), gdl4.buf), i4[:], start=True, stop=True)
            fw.activation(dec4[:], bd[:, 0:4], AF.Exp)
            for t in range(4):
                fw.tt(dve, V(kw4.h[:, t, :, :], kw4.subs[t]), V(ktok4.h[:, t, :].rearrange("p (h e) -> p h e", h=NHB), ktok4.subs[t]),
                      V(TMe4.h[:, 4 * t:4 * t + 4].unsqueeze(2).to_broadcast([128, NHB, EB]), TMe4.buf), ALU.mult)
            for hh in range(NHB):
                bu = pb()
                for t in range(4):
                    fw.mm(bu[:, 0:EB + 1], V(kw4.h[:, t, hh, 0:128], kw4.subs[t]), V(vext4.h[:, t, hh, :], vext4.subs[t]), start=(t == 0), stop=(t == 3))
                for t in range(4):
                    fw.mm(bu[0:32, 256:256 + EB + 1], V(kw4.h[:, t, hh, 128:EB], kw4.subs[t]), V(vext4.h[:, t, hh, :], vext4.subs[t]), start=(t == 0), stop=(t == 3))
                fw.stt(CTa.sub(hh)[:, hh, :], CTa.sub(hh)[:, hh, :], dec4[:, hh:hh + 1], bu[:, 0:EB + 1], ALU.mult, ALU.add)
                fw.stt(CTb.sub(hh)[:, hh, :], CTb.sub(hh)[:, hh, :], dec4[0:32, hh:hh + 1], bu[0:32, 256:256 + EB + 1], ALU.mult, ALU.add)
            fw.tt(dve, m_st[:], ga4[:, 511:512], gM4[:, 511:512], ALU.add)
        fw.copy(act, CTa_b[:], CTa[:])
        fw.copy(act, CTb_b[:], CTb[:])
        fw.copy(dve, xlast[:], xT4[:, :, 0:3])
        pop()
    push()
    obT = sb("obT", [128, 5, 128], BF16)
    caT_2 = [sb("caT_%d" % i_, [128, 5, 128], BF16) for i_ in range(2)]
    caT = caT_2[0]
    hT_2 = [sb("hT_%d" % i_, [128, 8, 128], BF16) for i_ in range(2)]
    hT = hT_2[0]
    kvst = [sb("kvst%d" % i, [128, 2 * DA], F32) for i in range(2)]
    vext_2 = [sb("vext_%d" % i_, [128, NHB, EB + 1], BF16) for i_ in range(2)]
    vext = vext_2[0]
    vatt = [sb("vatt%d" % i, [128, 6, 65], BF16) for i in range(2)]
    xbT = [sb("xbT%d" % i, [128, 5, 131], F32) for i in range(2)]
    fw.memset(dve, xbT[0][:], 0.0)
    fw.memset(dve, xbT[1][:], 0.0)
    fw.memset(dve, vext_2[0][:], 1.0)
    fw.memset(dve, vext_2[1][:], 1.0)
    cT_2 = [sb("cT_%d" % i_, [128, 5, 128], F32) for i_ in range(2)]
    cT = cT_2[0]
    ktok_2 = [sb("ktok_%d" % i_, [128, DB], F32) for i_ in range(2)]
    ktok = ktok_2[0]
    kw_2 = [sb("kw_%d" % i_, [128, NHB, EB], BF16) for i_ in range(2)]
    kw = kw_2[0]
    qTa = sb("qTa", [128, NHB, 128], BF16)
    qTb = sb("qTb", [32, NHB, 128], BF16)
    kTa = sb("kTa", [128, NHB, 128], BF16)
    kTb = sb("kTb", [32, NHB, 128], BF16)
    gi_2 = [sb("gi_%d" % i_, [4, 128], F32) for i_ in range(2)]
    gi = gi_2[0]
    gl_2 = [sb("gl_%d" % i_, [4, 128], F32) for i_ in range(2)]
    gl = gl_2[0]
    ga_2 = [sb("ga_%d" % i_, [4, 128], F32) for i_ in range(2)]
    ga = ga_2[0]
    gu_2 = [sb("gu_%d" % i_, [4, 128], F32) for i_ in range(2)]
    gu = gu_2[0]
    gM_2 = [sb("gM_%d" % i_, [4, 128], F32) for i_ in range(2)]
    gM = gM_2[0]
    gnM_2 = [sb("gnM_%d" % i_, [4, 128], F32) for i_ in range(2)]
    gnM = gnM_2[0]
    G1a_2 = [sb("G1a_%d" % i_, [4, 128], F32) for i_ in range(2)]
    G1a = G1a_2[0]
    G1b_2 = [sb("G1b_%d" % i_, [4, 128], F32) for i_ in range(2)]
    G1b = G1b_2[0]
    G1c_2 = [sb("G1c_%d" % i_, [4, 128], F32) for i_ in range(2)]
    G1c = G1c_2[0]
    gdl_2 = [sb("gdl_%d" % i_, [4, 1], F32) for i_ in range(2)]
    gdl = gdl_2[0]
    TM_2 = [sb("TM_%d" % i_, [128, 16], F32) for i_ in range(2)]
    TM = TM_2[0]
    TMe_2 = [sb("TMe_%d" % i_, [128, 12], F32) for i_ in range(2)]
    TMe = TMe_2[0]
    dec_bc_2 = [sb("dec_bc_%d" % i_, [128, 4], F32) for i_ in range(2)]
    dec_bc = dec_bc_2[0]
    numd4 = sb("numd4", [128, NHB, EB + 1], F32).split(NHB)
    sc4 = sb("sc4", [128, 6, NHB], F32)
    expD_2 = [sb("expD_%d" % i_, [128, 128], F32) for i_ in range(2)]
    AT_2 = [sb("AT_%d" % i_, [128, 128], BF16) for i_ in range(2)]
    inter_s_2 = [sb("inter_s_%d" % i_, [128, EB + 1], F32) for i_ in range(2)]
    hbn = sb("hbn", [128, DB], BF16)
    gt1 = sb("gt1", [128, 5, 128], F32)

    tile_list = list(range(4 * NPRE, NT)) if tile_list is None else tile_list
    if tile_list and not NPRE:
        load_x(tile_list[0])
    if NPRE:
        fw.copy(dve, xbT[1][:, :, 128:131], xlast[:])
    obT_2 = [obT, sb("obT_b", [128, 5, 128], BF16)]

    def stage1a(i):
            p = i % 2
            xn = xn_2[p]
            hT = hT_2[p]
            ss = ss_2[p]
            rstd = rstd_2[p]
            caT = caT_2[p]
            vext = vext_2[p]
            cT = cT_2[p]
            ktok = ktok_2[p]
            kw = kw_2[p]
            gi = gi_2[p]
            gl = gl_2[p]
            ga = ga_2[p]
            gu = gu_2[p]
            gM = gM_2[p]
            gnM = gnM_2[p]
            G1a = G1a_2[p]
            G1b = G1b_2[p]
            G1c = G1c_2[p]
            gdl = gdl_2[p]
            TM = TM_2[p]
            TMe = TMe_2[p]
            dec_bc = dec_bc_2[p]
            is_main = i >= MAIN0
            obT = obT_2[p]
            if cT.subs is None:
                cT.split(5)
            is_halo = i >= HALO0
            mt = i - MAIN0
            if i + 1 < NT and (i + 1) in tile_list:
                load_x(i + 1)
            fw.stt(junk[:], xt[p][:], 1.0, xt[p][:], ALU.mult, ALU.mult, accum_out=ss[:])
            fw.ts(dve, rstd[:], ss[:], 1.0 / D, EPS, ALU.mult, ALU.add)
            fw.activation(rstd[:], rstd[:], AF.Ln)
            fw.activation(rstd[:], rstd[:], AF.Exp, scale=-0.5)
            fw.ts(dve, xn[:], xt[p][:], rstd[:], None, ALU.mult)
            bT = pb()
            pT = bT.bitcast(BF16)
            for c in range(8):
                fw.transpose(pT[:, 128 * c:128 * c + 128], xn[:, 128 * c:128 * c + 128], ident_b[:], inc=(c == 7))
            fw.copy(act, hT.re("p c t -> p (c t)"), pT[:, 0:1024])


    def stage1b(i):
            p = i % 2
            xn = xn_2[p]
            hT = hT_2[p]
            ss = ss_2[p]
            rstd = rstd_2[p]
            caT = caT_2[p]
            vext = vext_2[p]
            cT = cT_2[p]
            ktok = ktok_2[p]
            kw = kw_2[p]
            gi = gi_2[p]
            gl = gl_2[p]
            ga = ga_2[p]
            gu = gu_2[p]
            gM = gM_2[p]
            gnM = gnM_2[p]
            G1a = G1a_2[p]
            G1b = G1b_2[p]
            G1c = G1c_2[p]
            gdl = gdl_2[p]
            TM = TM_2[p]
            TMe = TMe_2[p]
            dec_bc = dec_bc_2[p]
            is_main = i >= MAIN0
            obT = obT_2[p]
            if cT.subs is None:
                cT.split(5)
            is_halo = i >= HALO0
            mt = i - MAIN0
            def proj_tok(c0, n):
                b = pb()
                for c in range(8):
                    fw.mm(b[:, 0:n], hT[:, c, :], w_in_b[:, c, c0:c0 + n], start=(c == 0), stop=(c == 7))
                return b

            def proj_feat(c0, nchunks, M=128):
                b = pb()
                for j in range(nchunks):
                    for c in range(8):
                        fw.mm(b[0:M, 128 * j:128 * j + 128], w_in_b[:, c, c0 + M * j:c0 + M * j + M], hT[:, c, :],
                              start=(c == 0), stop=(c == 7))
                return b

            if is_halo:
                at = i - HALO0
                bk = proj_feat(DA, 3)
                fw.copy(act, KT[:, :, 128 * at:128 * at + 128], V(bk.h[:, 0:384].rearrange("p (c t) -> p c t", c=3), bk.buf))
                bkv = proj_tok(2 * DA, DA)
                va = vatt[p]
                if is_main:
                    bkk = proj_tok(DA, DA)
                    st = kvst[p]
                    fw.copy(dve, st[:, 0:DA], bkk[:, 0:DA])
                    fw.copy(dve, st[:, DA:2 * DA], bkv[:, 0:DA])
                    r0 = 128 * mt
                    fw.dma(sp, o_wk[r0:r0 + 128, :], st[:, 0:DA])
                    fw.dma(sp, o_wv[r0:r0 + 128, :], st[:, DA:2 * DA])
                fw.copy(act, va[:, :, 0:64], V(bkv.h[:, 0:DA].rearrange("p (h e) -> p h e", h=6), bkv.buf))
                fw.copy(dve, va[:, :, 64:65], V(tmO_s.h[:, i:i + 1].unsqueeze(1).to_broadcast([128, 6, 1]), tmO_s.buf))
                fw.dma(sp, v_scr[128 * at:128 * at + 128, :], va.re("p h e -> p (h e)"))
            if is_main:
                bq = proj_feat(0, 3)
                fw.copy(act, QT[:, :, 128 * mt:128 * mt + 128], V(bq.h[:, 0:384].rearrange("p (c t) -> p c t", c=3), bq.buf))
                b1 = proj_feat(3 * DA + DB, 4)
                fw.activation(obT[:, 0:4, :], V(b1.h[:, 0:512].rearrange("p (c t) -> p c t", c=4), b1.buf), AF.Sigmoid)
                b2 = proj_feat(3 * DA + DB + 512, 1)
                fw.activation(obT[:, 4, :], b2[:, 0:128], AF.Sigmoid)

            if stage < 1:
                return
            xT = xbT[p]
            xTp = xbT[1 - p]
            fw.copy(dve, xT[:, :, 0:3], xTp[:, :, 128:131])
            if stage < 1.2:
                return
            b1 = proj_feat(3 * DA, 4)
            if stage < 1.4:
                return
            fw.copy(act, xT[:, 0:4, 3:131], V(b1.h[:, 0:512].rearrange("p (c t) -> p c t", c=4), b1.buf))
            if stage < 1.6:
                return
            b2 = proj_feat(3 * DA + 512, 1)
            if stage < 1.7:
                return
            if stage < 1.8:
                fw.copy(act, junk[:, 0:128], b2[:, 0:128])
                return
            if VAR == 'dve':
                fw.copy(dve, xT[:, 4, 3:131], b2[:, 0:128])
            elif VAR == 'col0':
                fw.copy(act, xT[:, 4, 0:128], b2[:, 0:128])
            elif VAR == 'chunk3':
                fw.copy(act, xT[:, 3, 3:131], b2[:, 0:128])
            else:
                fw.copy(act, xT[:, 4, 3:131], b2[:, 0:128])
            if stage < 2:
                return
            bg = pb()
            for (j, c0) in ((0, 3 * DA + 2 * DB), (1, 3 * DA + 2 * DB + 4)):
                for c in range(8):
                    fw.mm(bg[0:4, 128 * j:128 * j + 128], w_in_b[:, c, c0:c0 + 4], hT[:, c, :], start=(c == 0), stop=(c == 7))
            if stage < 2.1:
                fw.copy(dve, gi[:], bg[0:4, 0:128])
                return
            fw.ts(dve, gi[:], bg[0:4, 0:128], gb_i[:], tmA_s[:, i:i + 1], ALU.add, ALU.add)
            if stage < 2.2:
                return
            fw.activation(gl[:], bg[0:4, 128:256], AF.Exp, bias=gb_fn[:], scale=-1.0)
            if stage < 2.3:
                return
            fw.activation(gl[:], gl[:], AF.Ln, bias=1.0)
            if stage < 2.4:
                return
            fw.ts(dve, gl[:], gl[:], tmV_s[:, i:i + 1], None, ALU.mult)
            if stage < 3:
                return
            bx = proj_tok(3 * DA, 512)
            bx2 = proj_tok(3 * DA + 512, 128)
            for hh in range(NHB):
                c0 = EB * hh
                if c0 + EB <= 512:
                    fw.copy(act if hh % 2 else dve, vext[:, hh, 0:EB], bx[:, c0:c0 + EB])
                else:
                    fw.copy(dve, vext[:, hh, 0:512 - c0], bx[:, c0:512])
                    fw.copy(act, vext[:, hh, 512 - c0:EB], bx2[:, 0:c0 + EB - 512])
            if i == NT - 1:
                xs_ = kvst[1 - p]
                fw.copy(dve, xs_[:, 0:512], bx[:, 0:512])
                fw.copy(dve, xs_[:, 512:640], bx2[:, 0:128])
                fw.dma(sp, o_conv[:, :], xs_[125:128, 0:DB])
            if stage < 4:
                return
            for c in range(5):
                fw.activation(cT.sub(c)[:, c, :], xT[:, c, 0:128], AF.Identity, bias=cb[:, c:c + 1], scale=cw[:, c, 0:1])
            for k in range(1, 4):
                for c in range(5):
                    fw.stt(cT.sub(c)[:, c, :], xT[:, c, k:k + 128], cw[:, c, k:k + 1], cT.sub(c)[:, c, :], ALU.mult, ALU.add)
            fw.activation(caT[:], cT[:], AF.Silu)
            if stage < 5:
                return
            bk1 = pb()
            bk2 = pb()
            for c in range(5):
                fw.mm(bk1[:, 0:512], caT[:, c, :], wk_b[:, c, 0:512], start=(c == 0), stop=(c == 4))
            for c in range(5):
                fw.mm(bk2[:, 0:128], caT[:, c, :], wk_b[:, c, 512:640], start=(c == 0), stop=(c == 4))
            fw.copy(act, ktok[:, 0:512], bk1[:, 0:512])
            fw.copy(act, ktok[:, 512:640], bk2[:, 0:128])


    def stage2a(i):
            p = i % 2
            xn = xn_2[p]
            hT = hT_2[p]
            ss = ss_2[p]
            rstd = rstd_2[p]
            caT = caT_2[p]
            vext = vext_2[p]
            cT = cT_2[p]
            ktok = ktok_2[p]
            kw = kw_2[p]
            gi = gi_2[p]
            gl = gl_2[p]
            ga = ga_2[p]
            gu = gu_2[p]
            gM = gM_2[p]
            gnM = gnM_2[p]
            G1a = G1a_2[p]
            G1b = G1b_2[p]
            G1c = G1c_2[p]
            gdl = gdl_2[p]
            TM = TM_2[p]
            TMe = TMe_2[p]
            dec_bc = dec_bc_2[p]
            is_main = i >= MAIN0
            obT = obT_2[p]
            if cT.subs is None:
                cT.split(5)
            is_halo = i >= HALO0
            mt = i - MAIN0
            if stage < 6:
                return
            fw.scan(ga[:], ones4[:], gl[:], 0.0, ALU.mult, ALU.add)
            fw.tt(dve, gu[:], gi[:], ga[:], ALU.subtract)
            fw.scan(gM[:], gu[:], gu[:], m_st[:], ALU.max, ALU.max)
            fw.ts(dve, gnM[:], gM[:], -1.0, None, ALU.mult)
            fw.ts(dve, G1a[:], gu[:], gM[:, 127:128], None, ALU.subtract)
            fw.ts(dve, G1b[:], gM[:], -1.0, m_st[:], ALU.mult, ALU.add)
            fw.stt(G1c[:], ga[:], -1.0, gM[:], ALU.mult, ALU.subtract)
            fw.ts(dve, gdl[:], gM[:, 127:128], -1.0, m_st[:], ALU.mult, ALU.add)
            if stage < 7:
                return
            bt = pb()
            for (j_, g_) in enumerate((G1a, G1b, G1c, gu)):
                fw.transpose(bt[:, 4 * j_:4 * j_ + 4], g_[:], ident_f[0:4, 0:4])
            fw.copy(dve, TM[:], bt[:, 0:16])
            fw.activation(TMe[:], TM[:, 0:12], AF.Exp)
            if stage < 8:
                return
            bd = pb()
            fw.mm(bd[:, 0:4], V(gdl.h[:, 0:1].to_broadcast([4, 128]), gdl.buf), i4[:], start=True, stop=True)
            fw.activation(dec_bc[:], bd[:, 0:4], AF.Exp)

            if stage < 9:
                return
            if is_main:
                for (wsrc, da_, db_) in ((wq_b, qTa, qTb), (wk_b, kTa, kTb)):
                    ba = pb()
                    bb = pb()
                    for hh in range(NHB):
                        cs = sorted(set([(EB * hh) // 128, (EB * hh + EB - 1) // 128]))
                        for n_, c in enumerate(cs):
                            fw.mm(ba[:, 128 * hh:128 * hh + 128], wsrc[:, c, EB * hh:EB * hh + 128], caT[:, c, :],
                                  start=(n_ == 0), stop=(n_ == len(cs) - 1))
                        for n_, c in enumerate(cs):
                            fw.mm(bb[0:32, 128 * hh:128 * hh + 128], wsrc[:, c, EB * hh + 128:EB * hh + EB], caT[:, c, :],
                                  start=(n_ == 0), stop=(n_ == len(cs) - 1))
                    fw.copy(act, da_.re("p h t -> p (h t)"), ba[:, 0:512])
                    fw.copy(dve, db_.re("p h t -> p (h t)"), bb[0:32, 0:512])
                for hh in range(NHB):
                    expD, AT, inter_s = expD_2[hh % 2], AT_2[hh % 2], inter_s_2[hh % 2]
                    bs = pb()
                    fw.mm(bs[:, 0:128], kTa[:, hh, :], qTa[:, hh, :], start=True, stop=False)
                    fw.mm(bs[:, 0:128], kTb[:, hh, :], qTb[:, hh, :], start=False, stop=True)
                    fw.mm(bs[:, 128:256], sel[:, 128 * hh:128 * hh + 128], gnM[:], start=True, stop=False)
                    fw.mm(bs[:, 128:256], ident_f[:], negmask[:], start=False, stop=True)
                    fw.activation(expD[:], bs[:, 128:256], AF.Exp, bias=TM[:, 12 + hh:13 + hh])
                    fw.tt(dve, AT[:], bs[:, 0:128], expD[:], ALU.mult)
                    bn = pb()
                    fw.mm(bn[:, 0:EB + 1], AT[:], vext[:, hh, :], start=True, stop=True)
                    fw.mm(bn[:, 256:256 + EB + 1], qTa[:, hh, :], CTa_b[:, hh, :], start=True, stop=False)
                    fw.mm(bn[:, 256:256 + EB + 1], qTb[:, hh, :], CTb_b[:, hh, :], start=False, stop=True)
                    fw.activation(inter_s[:], bn[:, 256:256 + EB + 1], AF.Copy, scale=TMe[:, 4 + hh:5 + hh])
                    fw.tt(dve, numd4.sub(hh)[:, hh, :], bn[:, 0:EB + 1], inter_s[:], ALU.add)
                den4 = numd4[:, :, EB]
                fw.stt(sc4[:, 0, :], den4, -1.0, den4, ALU.mult, ALU.max)
                fw.tt(dve, sc4[:, 0, :], sc4[:, 0, :], TMe[:, 8:12], ALU.max)
                fw.recip(sc4[:, 1, :], sc4[:, 0, :])
                sqv = V(gt1.h[:].rearrange("p c t -> p (c t)").rearrange("p (h e) -> p h e", h=NHB), gt1.buf)
                fw.tt(dve, sqv, numd4[:, :, 0:EB], numd4[:, :, 0:EB], ALU.mult)
                fw.reduce(sc4[:, 2, :], sqv, ALU.add)
                fw.tt(dve, sc4[:, 3, :], sc4[:, 1, :], sc4[:, 1, :], ALU.mult)
                fw.tt(dve, sc4[:, 3, :], sc4[:, 3, :], sc4[:, 2, :], ALU.mult)
                fw.ts(dve, sc4[:, 3, :], sc4[:, 3, :], 1.0 / EB, EPS, ALU.mult, ALU.add)
                fw.activation(sc4[:, 4, :], sc4[:, 3, :], AF.Ln)
                fw.activation(sc4[:, 4, :], sc4[:, 4, :], AF.Exp, scale=-0.5)
                fw.tt(dve, sc4[:, 5, :], sc4[:, 4, :], sc4[:, 1, :], ALU.mult)
                fw.tt(dve, V(hbn.h[:].rearrange("p (h e) -> p h e", h=NHB), hbn.buf), numd4[:, :, 0:EB],
                      V(sc4.h[:, 5, :].unsqueeze(2).to_broadcast([128, NHB, EB]), sc4.buf), ALU.mult)
                bh = pb()
                bh2 = pb()
                pTh = bh.bitcast(BF16)
                pTh2 = bh2.bitcast(BF16)
                for c in range(5):
                    dstp = pTh[:, 128 * c:128 * c + 128] if c < 4 else pTh2[:, 0:128]
                    fw.transpose(dstp, hbn[:, 128 * c:128 * c + 128], ident_b[:])
                for c in range(5):
                    srcp = pTh[:, 128 * c:128 * c + 128] if c < 4 else pTh2[:, 0:128]
                    fw.ts(dve, gt1[:, c, :], srcp, mhg[:, c:c + 1], None, ALU.mult)
                    fw.stt(gt1[:, c, :], caT[:, c, :], skp[:, c:c + 1], gt1[:, c, :], ALU.mult, ALU.add)
                fw.tt(dve, outbT[:, :, 128 * mt:128 * mt + 128], gt1[:], obT[:], ALU.mult)


    def stage2b(i):
            p = i % 2
            xn = xn_2[p]
            hT = hT_2[p]
            ss = ss_2[p]
            rstd = rstd_2[p]
            caT = caT_2[p]
            vext = vext_2[p]
            cT = cT_2[p]
            ktok = ktok_2[p]
            kw = kw_2[p]
            gi = gi_2[p]
            gl = gl_2[p]
            ga = ga_2[p]
            gu = gu_2[p]
            gM = gM_2[p]
            gnM = gnM_2[p]
            G1a = G1a_2[p]
            G1b = G1b_2[p]
            G1c = G1c_2[p]
            gdl = gdl_2[p]
            TM = TM_2[p]
            TMe = TMe_2[p]
            dec_bc = dec_bc_2[p]
            is_main = i >= MAIN0
            obT = obT_2[p]
            if cT.subs is None:
                cT.split(5)
            is_halo = i >= HALO0
            mt = i - MAIN0
            if stage < 10:
                return
            fw.tt(dve, kw[:], V(ktok.h[:].rearrange("p (h e) -> p h e", h=NHB), ktok.buf),
                  V(TMe.h[:, 0:4].unsqueeze(2).to_broadcast([128, NHB, EB]), TMe.buf), ALU.mult)
            for hh in range(NHB):
                bu = pb()
                fw.mm(bu[:, 0:EB + 1], kw[:, hh, 0:128], vext[:, hh, :], start=True, stop=True)
                fw.mm(bu[0:32, 256:256 + EB + 1], kw[:, hh, 128:EB], vext[:, hh, :], start=True, stop=True)
                fw.stt(CTa.sub(hh)[:, hh, :], CTa.sub(hh)[:, hh, :], dec_bc[:, hh:hh + 1], bu[:, 0:EB + 1], ALU.mult, ALU.add)
                fw.stt(CTb.sub(hh)[:, hh, :], CTb.sub(hh)[:, hh, :], dec_bc[0:32, hh:hh + 1], bu[0:32, 256:256 + EB + 1], ALU.mult, ALU.add)
            fw.copy(act, CTa_b[:], CTa[:])
            fw.copy(act, CTb_b[:], CTb[:])
            fw.tt(dve, m_st[:], ga[:, 127:128], gM[:, 127:128], ALU.add)


    for idx_, i in enumerate(tile_list):
        stage1a(i)
        if idx_ > 0:
            stage2a(tile_list[idx_ - 1])
        stage1b(i)
        if idx_ > 0:
            stage2b(tile_list[idx_ - 1])
    if tile_list:
        stage2a(tile_list[-1])
        stage2b(tile_list[-1])
    xn, ss, rstd = xn_2[0], ss_2[0], rstd_2[0]
    pop()
    push()
    if do_sample:
        zs = sb("zs", [NS, DIN], F32)
        hT = sb("hTs", [128, 8, NS], BF16)
        caT = sb("caTs", [128, 5, NS], BF16)
        s_q = dscr("s_q", [NS * NHB, EB], F32)
        s_k = dscr("s_k", [NS * NHB, EB], F32)
        s_v = dscr("s_v", [NS * NHB, EB], F32)
        s_sc = dscr("s_sc", [NS * NHB, 4], F32)
        s_misc = dscr("s_misc", [NS, 2 * DB + 3 * DA], F32)
        p = 0
        fw.dma(sp, xt[p][0:NS, :], xs_d[:, :])
        fw.stt(junk[0:NS, :], xt[p][0:NS, :], 1.0, xt[p][0:NS, :], ALU.mult, ALU.mult, accum_out=ss[0:NS, :])
        fw.ts(dve, rstd[0:NS, :], ss[0:NS, :], 1.0 / D, EPS, ALU.mult, ALU.add)
        fw.activation(rstd[0:NS, :], rstd[0:NS, :], AF.Ln)
        fw.activation(rstd[0:NS, :], rstd[0:NS, :], AF.Exp, scale=-0.5)
        fw.ts(dve, xn[0:NS, :], xt[p][0:NS, :], rstd[0:NS, :], None, ALU.mult)
        bT = pb()
        pT = bT.bitcast(BF16)
        for c in range(8):
            fw.transpose(pT[:, 128 * c:128 * c + NS], xn[0:NS, 128 * c:128 * c + 128], ident_b[0:NS, 0:NS], inc=(c == 7))
        for c in range(8):
            fw.copy(act, hT[:, c, 0:NS], pT[:, 128 * c:128 * c + NS])
        for c0 in range(0, DIN, 512):
            n = min(512, DIN - c0)
            bk_ = pb()
            for c in range(8):
                fw.mm(bk_[0:NS, 0:n], hT[:, c, 0:NS], w_in_b[:, c, c0:c0 + n], start=(c == 0), stop=(c == 7))
            fw.copy(act if (c0 // 512) % 2 else dve, zs[:, c0:c0 + n], bk_[0:NS, 0:n])
        fw.dma(sp, o_swk[:, 2047, :], zs[:, DA:2 * DA])
        fw.dma(sp, o_swv[:, 2047, :], zs[:, 2 * DA:3 * DA])
        fw.dma(sp, o_sconv[:, 2, :], zs[:, 3 * DA:3 * DA + DB])
        fw.dma(sp, o_sconv[:, 0:2, :], s_conv_d[:, 1:3, :])
        cwr = sb("cwr", [NS, 4, DB], F32)
        cbr = sb("cbr", [NS, DB], F32)
        scv = sb("scv", [NS, 3, DB], F32)
        fw.dma(sp, cwr.re("p i d -> p (i d)"), V(conv_w_d.h.ap().rearrange("i d -> (i d)").partition_broadcast(NS), conv_w_d.buf))
        fw.dma(sp, cbr[:], V(conv_b.h.ap().rearrange("d o -> (d o)").partition_broadcast(NS), conv_b.buf))
        fw.dma(sp, scv[:], s_conv_d[:, :, :])
        cs = sb("cs", [NS, DB], F32)
        cs2 = sb("cs2", [NS, DB], F32)
        fw.tt(dve, cs[:], zs[:, 3 * DA:3 * DA + DB], cwr[:, 3, :], ALU.mult)
        fw.tt(dve, cs[:], cs[:], cbr[:], ALU.add)
        for k in range(3):
            fw.tt(dve, cs2[:], scv[:, k, :], cwr[:, k, :], ALU.mult)
            fw.tt(dve, cs[:], cs[:], cs2[:], ALU.add)
        cas = sb("cas", [NS, DB], F32)
        casb = sb("casb", [NS, DB], BF16)
        fw.activation(cas[:], cs[:], AF.Silu)
        fw.copy(dve, casb[:], cas[:])
        fw.dma(sp, s_misc[:, DB:2 * DB], cas[:])
        sob = sb("sob", [NS, DB], F32)
        fw.activation(sob[:], zs[:, 3 * DA + DB:3 * DA + 2 * DB], AF.Sigmoid)
        fw.dma(sp, s_misc[:, 0:DB], sob[:])
        fw.dma(sp, s_misc[:, 2 * DB:2 * DB + 3 * DA], zs[:, 0:3 * DA])
        bT2 = pb()
        pT2 = bT2.bitcast(BF16)
        for c in range(5):
            fw.transpose(pT2[:, 128 * c:128 * c + NS], casb[:, 128 * c:128 * c + 128], ident_b[0:NS, 0:NS], inc=(c == 4))
        for c in range(5):
            fw.copy(act, caT[:, c, 0:NS], pT2[:, 128 * c:128 * c + NS])
        sqk = sb("sqk", [NS, 2, DB], F32)
        for wi, wsrc in enumerate((wq_b, wk_b)):
            for (c0, n) in ((0, 512), (512, 128)):
                bk_ = pb()
                for c in range(5):
                    fw.mm(bk_[0:NS, 0:n], caT[:, c, 0:NS], wsrc[:, c, c0:c0 + n], start=(c == 0), stop=(c == 4))
                fw.copy(dve, sqk[:, wi, c0:c0 + n], bk_[0:NS, 0:n])
        fw.dma(sp, V(s_q.h.ap().rearrange("(b h) e -> b (h e)", h=NHB), s_q.buf), sqk[:, 0, :])
        fw.dma(sp, V(s_k.h.ap().rearrange("(b h) e -> b (h e)", h=NHB), s_k.buf), sqk[:, 1, :])
        fw.dma(sp, V(s_v.h.ap().rearrange("(b h) e -> b (h e)", h=NHB), s_v.buf), zs[:, 3 * DA:3 * DA + DB])
        gbr = sb("gbr", [NS, 8], F32)
        sm0 = sb("sm0", [NS, NHB], F32)
        fw.dma(sp, gbr[:], V(gate_bias.h.ap().rearrange("g o -> (g o)").partition_broadcast(NS), gate_bias.buf))
        fw.dma(sp, sm0[:], s_m_d[:, :])
        sg = sb("sg", [NS, 8, NHB], F32)
        fw.tt(dve, sg[:, 0, :], zs[:, DIN - 8:DIN - 4], gbr[:, 0:4], ALU.add)
        fw.tt(dve, sg[:, 1, :], zs[:, DIN - 4:DIN], gbr[:, 4:8], ALU.add)
        fw.activation(sg[:, 1, :], sg[:, 1, :], AF.Exp, scale=-1.0)
        fw.activation(sg[:, 1, :], sg[:, 1, :], AF.Ln, bias=1.0)
        fw.tt(dve, sg[:, 2, :], sm0[:], sg[:, 1, :], ALU.subtract)
        fw.tt(dve, sg[:, 3, :], sg[:, 2, :], sg[:, 0, :], ALU.max)
        fw.tt(dve, sg[:, 4, :], sg[:, 2, :], sg[:, 3, :], ALU.subtract)
        fw.tt(dve, sg[:, 5, :], sg[:, 0, :], sg[:, 3, :], ALU.subtract)
        fw.ts(dve, sg[:, 6, :], sg[:, 3, :], -1.0, None, ALU.mult)
        ssc = sb("ssc", [NS, NHB, 4], F32)
        fw.activation(ssc[:, :, 0], sg[:, 4, :], AF.Exp)
        fw.activation(ssc[:, :, 1], sg[:, 5, :], AF.Exp)
        fw.activation(ssc[:, :, 2], sg[:, 6, :], AF.Exp)
        fw.copy(dve, ssc[:, :, 3], sg[:, 3, :])
        fw.dma(sp, o_sm[:, :], sg[:, 3, :])
        fw.dma(sp, V(s_sc.h.ap().rearrange("(b h) x -> b (h x)", h=NHB), s_sc.buf), ssc.re("p h x -> p (h x)"))

    Co_a = sb("Co_a", [128, NHB, EB], F32)
    Co_b = sb("Co_b", [33, NHB, EB], F32)
    for hh in range(NHB if do_cout else 0):
        b_ = pb()
        fw.transpose(b_[:, 0:128], CTa[:, hh, 0:128], ident_f[:])
        fw.transpose(b_[:, 128:160], CTb[:, hh, 0:128], ident_f[0:32, 0:32])
        fw.transpose(b_[0:33, 256:384], CTa[:, hh, 128:EB + 1], ident_f[:])
        fw.transpose(b_[0:33, 384:416], CTb[:, hh, 128:EB + 1], ident_f[0:32, 0:32])
        fw.copy(dve, Co_a[:, hh, :], b_[:, 0:EB])
        fw.copy(dve, Co_b[:, hh, :], b_[0:33, 256:256 + EB])
        fw.dma(sp, o_C[hh, 0:128, :], Co_a[:, hh, :])
        fw.dma(sp, o_C[hh, 128:EB, :], Co_b[0:32, hh, :])
        fw.dma(sp, o_n[hh:hh + 1, :], Co_b[32:33, hh, :])
    fw.dma(sp, o_m[:, :], m_st[:])
    if dbg:
        for mt in range(SEG // 128):
            pass

    pop()
    pop()
    push()
    if do_attn:
        rrep = dscr("rrep", [18 * 128 * 512 + 512], F32)
        oz_scr = [dscr("oz_scr%d" % b_, [SEG, 390], F32) for b_ in range(3)]
        rb = sb("rb", [33, 6], F32)
        ohb = sb("ohb_s", [33, 3, 512], F32)
        ag = sb("ag", [128, 3], F32)
        fw.memset(dve, rb[32:33, :], 1.0)
        fw.dma(sp, rb[0:32, :], rel_bias_d[:])
        for b_ in range(3):
            fw.dma(sp, ohb[0:32, b_, :], ohb_d[b_])
            fw.dma(sp, ohb[32:33, b_, :], mvec_d[b_])
            fw.dma(sp, ag[:, b_:b_ + 1], attn_g_d[128 * b_:128 * b_ + 128, :])
        vfs = [sb("vfs%d" % i, [128, 512], F32) for i in range(2)]
        for b_ in range(3):
            for h in range(6):
                bk_ = pb()
                fw.mm(bk_[:, 0:512], V(rb.h[:, h:h + 1].to_broadcast([33, 128]), rb.buf), ohb[:, b_, :], start=True, stop=True)
                vf = vfs[h % 2]
                fw.copy(act if h % 2 else dve, vf[:], bk_[:, 0:512])
                o0 = (b_ * 6 + h) * 128 * 512
                fw.dma(sp, rrep.ap(o0, [[512, 128], [1, 512]]), vf[:])
        biasT = sb("biasT", [128, 6, 2, 128], F32)
        stmp_1 = sb("stmp", [128, 6, 256], F32).split(6)
        stmp_2 = [stmp_1, stmp_1]
        PT_2 = [sb("PT%d" % i_, [128, 3, 512], BF16).split(6) for i_ in range(2)]
        Vp = [sb("Vp%d" % i, [128, 6, 65], BF16) for i in range(3)]
        Vo = [sb("Vo%d" % i, [128, 6, 65], BF16) for i in range(3)]
        ozs = [sb("ozs%d" % i, [128, 390], F32) for i in range(2)]
        srcs = []
        for b_, dil in enumerate((1, 4, 16)):
            if VAR == 'setup' or (VAR.startswith('br') and str(b_) not in VAR):
                continue
            unit = 128 * dil
            for n_ in range(SEG // unit):
                for r_ in range(dil):
                    srcs.append((b_, dil, n_ * unit + r_, n_ == 0 and r_ == 0))

        def att1(t_):
            b_, dil, q0, first = srcs[t_]
            unit = 128 * dil
            pp = t_ % 2
            stmp, PT = stmp_2[pp], PT_2[pp]
            if first:
                for h in range(6):
                    o0 = (b_ * 6 + h) * 128 * 512
                    fw.dma(sp, biasT[:, h, 0, :], rrep.ap(o0 + 255, [[511, 128], [1, 128]]))
                    fw.dma(sp, biasT[:, h, 1, :], rrep.ap(o0 + 127, [[511, 128], [1, 128]]))
            k_own = SEG + q0
            k_prev = k_own - unit

            def vload(tt_):
                b2_, dil2, q02, _f = srcs[tt_]
                ko = SEG + q02
                kp_ = ko - 128 * dil2
                fw.dma(sp, Vp[tt_ % 3].re("p h e -> p (h e)"), v_scr[kp_:kp_ + 127 * dil2 + 1:dil2, :])
                fw.dma(sp, Vo[tt_ % 3].re("p h e -> p (h e)"), v_scr[ko:ko + 127 * dil2 + 1:dil2, :])
            if t_ == 0:
                vload(0)
            if t_ + 1 < len(srcs):
                vload(t_ + 1)
            for c in range(3):
                for hh in range(2):
                    bank = pb()
                    for blk, k0 in enumerate((k_prev, k_own)):
                        fw.mm(bank[:, 128 * blk:128 * blk + 128], KT[64 * hh:64 * hh + 64, c, k0:k0 + 127 * dil + 1:dil],
                              QT[64 * hh:64 * hh + 64, c, q0:q0 + 127 * dil + 1:dil], start=True, stop=True)
                    u_ = 2 * c + hh
                    fw.tt(dve, stmp.sub(u_)[:, u_, :], bank[:, 0:256],
                          V(biasT.h[:, 2 * c + hh, :, :].rearrange("p b q -> p (b q)"), biasT.buf), ALU.add)
                    fw.activation(PT.sub(u_)[:, c, 256 * hh:256 * hh + 256], stmp.sub(u_)[:, u_, :], AF.Exp)

        def att2(t_):
            b_, dil, q0, first = srcs[t_]
            pp = t_ % 2
            PT = PT_2[pp]
            boz = pb()
            for h in range(6):
                c, hh = divmod(h, 2)
                fw.mm(boz[:, 65 * h:65 * h + 65], PT.sub(h)[:, c, (2 * hh) * 128:(2 * hh) * 128 + 128], Vp[t_ % 3][:, h, :], start=True, stop=False)
                fw.mm(boz[:, 65 * h:65 * h + 65], PT.sub(h)[:, c, (2 * hh + 1) * 128:(2 * hh + 1) * 128 + 128], Vo[t_ % 3][:, h, :], start=False, stop=True)
            fw.copy(act, ozs[pp][:], boz[:, 0:390])
            fw.dma(sp, oz_scr[b_][q0:q0 + 127 * dil + 1:dil, :], ozs[pp][:])

        for t_ in range(len(srcs)):
            att1(t_)
            if t_ > 0:
                att2(t_ - 1)
        if srcs:
            att2(len(srcs) - 1)
        ozl = [sb("ozl%d" % i, [128, 6, 65], F32) for i in range(3)]
        osum = sb("osum", [128, 6, 65], F32)
        rz = sb("rz", [128, 6, 1], F32)
        on = sb("on", [128, 6, 64], F32)
        onb = sb("onb", [128, DA], BF16)
        junkb = sb("junkb", [128, DA], BF16)
        ssa = sb("ssa", [128, 2], F32)
        for mt in range(SEG // 128 if VAR == '' else 0):
            for b_ in range(3):
                fw.dma(sp, ozl[b_].re("p h e -> p (h e)"), oz_scr[b_][128 * mt:128 * mt + 128, :])
            fw.tt(dve, osum[:], ozl[0][:], ozl[1][:], ALU.add)
            fw.tt(dve, osum[:], osum[:], ozl[2][:], ALU.add)
            fw.recip(rz[:], osum[:, :, 64:65])
            fw.tt(dve, on[:], osum[:, :, 0:64], V(rz.h[:].to_broadcast([128, 6, 64]), rz.buf), ALU.mult)
            onf = on.re("p h e -> p (h e)")
            fw.stt(junkb[:], onf, 1.0, onf, ALU.mult, ALU.mult, accum_out=ssa[:, 0:1])
            fw.ts(dve, ssa[:, 1:2], ssa[:, 0:1], 1.0 / DA, EPS, ALU.mult, ALU.add)
            fw.activation(ssa[:, 1:2], ssa[:, 1:2], AF.Ln)
            fw.activation(ssa[:, 1:2], ssa[:, 1:2], AF.Exp, scale=-0.5)
            fw.ts(dve, onb[:], onf, ssa[:, 1:2], None, ALU.mult)
            bt_ = pb()
            pTa = bt_.bitcast(BF16)
            for c in range(3):
                fw.transpose(pTa[:, 128 * c:128 * c + 128], onb[:, 128 * c:128 * c + 128], ident_b[:])
            for c in range(3):
                fw.ts(dve, outaT[:, c, 128 * mt:128 * mt + 128], pTa[:, 128 * c:128 * c + 128], ag[:, c:c + 1], None, ALU.mult)
        if do_sample:
            h_scr = dscr("h_scr", [NS * NHB * 2, 80], F32)
            qp = sb("qp", [128, EB], F32)
            kp = sb("kp", [128, EB], F32)
            vp = sb("vp", [128, 80], F32)
            scp = sb("scp", [128, 4], F32)
            n_p = sb("n_p", [128, EB], F32)
            for half in range(2):
                fw.dma(sp, qp[half:128:2, :], s_q[:, :])
                fw.dma(sp, kp[half:128:2, :], s_k[:, :])
                fw.dma(sp, vp[half:128:2, :], s_v[:, 80 * half:80 * half + 80])
                fw.dma(sp, scp[half:128:2, :], s_sc[:, :])
                fw.dma(sp, n_p[half:128:2, :], V(s_n_d.h.ap().rearrange("b h e -> (b h) e"), s_n_d.buf))
            vw = sb("vw", [128, 80], F32)
            fw.ts(dve, vw[:], vp[:], scp[:, 1:2], None, ALU.mult)
            kwp = sb("kwp", [128, EB], F32)
            fw.ts(dve, kwp[:], kp[:], scp[:, 1:2], None, ALU.mult)
            fw.stt(n_p[:], n_p[:], scp[:, 0:1], kwp[:], ALU.mult, ALU.add)
            fw.dma(sp, V(o_sn.h.ap().rearrange("b (h e) -> (b h) e", h=NHB), o_sn.buf), n_p[0:128:2, :])
            den = sb("den", [128, 4], F32)
            fw.stt(kwp[:], n_p[:], 1.0, qp[:], ALU.mult, ALU.mult, accum_out=den[:, 0:1])
            fw.stt(den[:, 1:2], den[:, 0:1], -1.0, den[:, 0:1], ALU.mult, ALU.max)
            fw.ts(dve, den[:, 1:2], den[:, 1:2], scp[:, 2:3], None, ALU.max)
            fw.recip(den[:, 2:3], den[:, 1:2])
            CH = 10
            Cc = [sb("Cc%d" % i, [128, CH, EB], F32) for i in range(2)]
            Oc = [sb("Oc%d" % i, [128, CH, EB], F32) for i in range(2)]
            hnum = sb("hnum", [128, 80], F32)
            sCv = V(s_C_d.h.ap().rearrange("b h (t v) k -> (b h t) (v k)", t=2), s_C_d.buf)
            oCv = V(o_sC.h.ap().rearrange("b h (t v) k -> (b h t) (v k)", t=2), o_sC.buf)
            for ci in range(80 // CH):
                pq = ci % 2
                cc, oc = Cc[pq], Oc[pq]
                fw.dma(sp, cc.re("p v k -> p (v k)"), V(sCv.ap[:, CH * EB * ci:CH * EB * (ci + 1)], sCv.buf))
                fw.tt(dve, oc[:], V(vw.h[:, CH * ci:CH * ci + CH].unsqueeze(2).to_broadcast([128, CH, EB]), vw.buf),
                      V(kp.h[:].unsqueeze(1).to_broadcast([128, CH, EB]), kp.buf), ALU.mult)
                fw.stt(cc.re("p v k -> p (v k)"), cc.re("p v k -> p (v k)"), scp[:, 0:1], oc.re("p v k -> p (v k)"), ALU.mult, ALU.add)
                fw.dma(sp, V(oCv.ap[:, CH * EB * ci:CH * EB * (ci + 1)], oCv.buf), cc.re("p v k -> p (v k)"))
                fw.tt(dve, oc[:], cc[:], V(qp.h[:].unsqueeze(1).to_broadcast([128, CH, EB]), qp.buf), ALU.mult)
                fw.reduce(hnum[:, CH * ci:CH * ci + CH], oc[:], ALU.add)
            fw.ts(dve, hnum[:], hnum[:], den[:, 2:3], None, ALU.mult)
            fw.dma(sp, h_scr[:, :], hnum[:])
            hs = sb("hs", [NS, NHB, EB], F32)
            hs2 = sb("hs2", [NS, NHB, EB], F32)
            fw.dma(sp, hs.re("p h e -> p (h e)"), V(h_scr.h.ap().rearrange("(b x) e -> b (x e)", b=NS), h_scr.buf))
            hss = sb("hss", [NS, 2, NHB], F32)
            fw.tt(dve, hs2[:], hs[:], hs[:], ALU.mult)
            fw.reduce(hss[:, 0, :], hs2[:], ALU.add)
            fw.ts(dve, hss[:, 1, :], hss[:, 0, :], 1.0 / EB, EPS, ALU.mult, ALU.add)
            fw.activation(hss[:, 1, :], hss[:, 1, :], AF.Ln)
            fw.activation(hss[:, 1, :], hss[:, 1, :], AF.Exp, scale=-0.5)
            fw.tt(dve, hs2[:], hs[:], V(hss.h[:, 1, :].unsqueeze(2).to_broadcast([NS, NHB, EB]), hss.buf), ALU.mult)
            mgr = sb("mgr", [NS, 2, DB], F32)
            fw.dma(sp, mgr[:, 0, :], V(mhg_d.h.ap().rearrange("d o -> (d o)").partition_broadcast(NS), mhg_d.buf))
            fw.dma(sp, mgr[:, 1, :], V(skip_d.h.ap().rearrange("d o -> (d o)").partition_broadcast(NS), skip_d.buf))
            smi = sb("smi", [NS, 2 * DB + 3 * DA], F32)
            fw.dma(sp, smi[:], s_misc[:, :])
            ob_s = sb("ob_s", [NS, DB], F32)
            hs2f = hs2.re("p h e -> p (h e)")
            fw.tt(dve, ob_s[:], hs2f, mgr[:, 0, :], ALU.mult)
            fw.tt(dve, hs2f, smi[:, DB:2 * DB], mgr[:, 1, :], ALU.mult)
            fw.tt(dve, ob_s[:], ob_s[:], hs2f, ALU.add)
            ob_b = sb("ob_b", [NS, DB], BF16)
            fw.tt(dve, ob_b[:], ob_s[:], smi[:, 0:DB], ALU.mult)
            bT3 = pb()
            pT3 = bT3.bitcast(BF16)
            for c in range(5):
                fw.transpose(pT3[:, 128 * c:128 * c + NS], ob_b[:, 128 * c:128 * c + 128], ident_b[0:NS, 0:NS], inc=(c == 4))
            for c in range(5):
                fw.copy(act, outbT[:, c, SEG:SEG + NS], pT3[:, 128 * c:128 * c + NS])

            ohs = sb("ohs_s", [32, 3, 128], F32)
            e16 = sb("e16_s", [NS, NS * 128], F32)
            ecol = sb("ecol_s", [128, NS * NS], F32)
            fw.dma(sp, ohs[:], V(ohs_d.h.ap().rearrange("b k i -> k b i"), ohs_d.buf))
            fw.dma(sp, e16[:], e16_d[:, :])
            fw.dma(sp, ecol[:], ecol_d[:, :])
            bS = sb("bS", [128, 3, 6], F32)
            for b_ in range(3):
                bk_ = pb()
                fw.mm(bk_[:, 0:6], ohs[:, b_, :], rb[0:32, :], start=True, stop=True)
                fw.copy(dve, bS[:, b_, :], bk_[:, 0:6])
            qa = sb("qa", [NS, DA], F32)
            ka = sb("ka", [NS, DA], F32)
            va = sb("va_s", [NS, DA], F32)
            fw.copy(dve, qa[:], smi[:, 2 * DB:2 * DB + DA])
            fw.copy(dve, ka[:], smi[:, 2 * DB + DA:2 * DB + 2 * DA])
            fw.copy(dve, va[:], smi[:, 2 * DB + 2 * DA:2 * DB + 3 * DA])
            Kg = [sb("Kg%d" % i, [128, 6, 64], F32) for i in range(2)]
            Vg = [sb("Vg%d" % i, [128, 6, 64], F32) for i in range(2)]
            prd = sb("prd", [128, 6, 64], F32)
            pvz = [sb("pvz%d" % i, [128, DA + 6], F32) for i in range(2)]
            sS = sb("sS", [128, 6], F32)
            bacc = pb()
            reserved.append(bacc)
            nn = 0
            for b in range(NS):
                bq_ = pb()
                fw.mm(bq_[:, 0:DA], e16[:, 128 * b:128 * b + 128], qa[:], start=True, stop=True)
                for b_, dil in enumerate((1, 4, 16)):
                    pq = nn % 2
                    fw.dma(sp, Kg[pq].re("p h e -> p (h e)"), cwk_d[b, 2048 - 128 * dil:2048:dil, :])
                    fw.dma(sp, Vg[pq].re("p h e -> p (h e)"), cwv_d[b, 2048 - 128 * dil:2048:dil, :])
                    fw.tt(dve, prd.re("p h e -> p (h e)"), Kg[pq].re("p h e -> p (h e)"), bq_[:, 0:DA], ALU.mult)
                    fw.reduce(sS[:], prd[:], ALU.add)
                    fw.tt(dve, sS[:], sS[:], bS[:, b_, :], ALU.add)
                    pz = pvz[pq]
                    fw.activation(pz[:, DA:DA + 6], sS[:], AF.Exp)
                    fw.tt(dve, V(pz.h[:, 0:DA].rearrange("p (h e) -> p h e", h=6), pz.buf), Vg[pq][:],
                          V(pz.h[:, DA:DA + 6].unsqueeze(2).to_broadcast([128, 6, 64]), pz.buf), ALU.mult)
                    fw.mm(bacc[0:NS, 0:DA + 6], ecol[:, NS * b:NS * b + NS], pz[:], start=(nn == 0), stop=(nn == 3 * NS - 1), inc=True)
                    nn += 1
            reserved.remove(bacc)
            b0r = sb("b0r", [NS, 6], F32)
            fw.dma(sp, b0r[:], V(rel_bias_d.h.ap()[0:1, :].rearrange("o h -> (o h)").partition_broadcast(NS), rel_bias_d.buf))
            qk = sb("qk", [NS, 6, 64], F32)
            s0 = sb("s0", [NS, 6], F32)
            fw.tt(dve, qk.re("p h e -> p (h e)"), qa[:], ka[:], ALU.mult)
            fw.reduce(s0[:], qk[:], ALU.add)
            fw.tt(dve, s0[:], s0[:], b0r[:], ALU.add)
            fw.activation(s0[:], s0[:], AF.Exp)
            fw.ts(dve, s0[:], s0[:], 3.0, None, ALU.mult)
            oacc = sb("oacc", [NS, DA + 6], F32)
            fw.copy(dve, oacc[:], bacc[0:NS, 0:DA + 6])
            fw.tt(dve, qk[:], V(va.h[:].rearrange("p (h e) -> p h e", h=6), va.buf),
                  V(s0.h[:].unsqueeze(2).to_broadcast([NS, 6, 64]), s0.buf), ALU.mult)
            fw.tt(dve, oacc[:, 0:DA], oacc[:, 0:DA], qk.re("p h e -> p (h e)"), ALU.add)
            fw.tt(dve, oacc[:, DA:DA + 6], oacc[:, DA:DA + 6], s0[:], ALU.add)
            rzs = sb("rzs", [NS, 6], F32)
            fw.recip(rzs[:], oacc[:, DA:DA + 6])
            fw.tt(dve, qk[:], V(oacc.h[:, 0:DA].rearrange("p (h e) -> p h e", h=6), oacc.buf),
                  V(rzs.h[:].unsqueeze(2).to_broadcast([NS, 6, 64]), rzs.buf), ALU.mult)
            qkf = qk.re("p h e -> p (h e)")
            sa2 = sb("sa2", [NS, 2], F32)
            jks = sb("jks", [NS, DA], F32)
            fw.stt(jks[:], qkf, 1.0, qkf, ALU.mult, ALU.mult, accum_out=sa2[:, 0:1])
            fw.ts(dve, sa2[:, 1:2], sa2[:, 0:1], 1.0 / DA, EPS, ALU.mult, ALU.add)
            fw.activation(sa2[:, 1:2], sa2[:, 1:2], AF.Ln)
            fw.activation(sa2[:, 1:2], sa2[:, 1:2], AF.Exp, scale=-0.5)
            oab = sb("oab", [NS, DA], BF16)
            fw.ts(dve, oab[:], qkf, sa2[:, 1:2], None, ALU.mult)
            bT4 = pb()
            pT4 = bT4.bitcast(BF16)
            for c in range(3):
                fw.transpose(pT4[:, 128 * c:128 * c + NS], oab[:, 128 * c:128 * c + 128], ident_b[0:NS, 0:NS], inc=(c == 2))
            for c in range(3):
                fw.ts(dve, outaT[:, c, SEG:SEG + NS], pT4[:, 128 * c:128 * c + NS], ag[:, c:c + 1], None, ALU.mult)

    else:
        fw.memset(dve, outaT[:], 0.0)

    pop()
    pop()
    push()
    x1_scr = dscr("x1_scr", [SEG + 128, D], F32)
    h2_scr = dscr("h2_scr", [SEG // 128 + 1, 128, D], BF16)
    w_out_b = sb("w_out_b", [128, 8, D], BF16)
    wst1 = [sb("wst1_%d" % i, [128, D], F32) for i in range(2)]
    for c in range(8):
        st = wst1[c % 2]
        fw.dma(sp, st[:], w_out_d[128 * c:128 * c + 128, :])
        fw.copy(act if c % 2 else dve, w_out_b[:, c, :], st[:])
    xt1 = [sb("xt1_%d" % i, [128, D], F32) for i in range(2)]
    x1t = [sb("x1t_%d" % i, [128, D], F32) for i in range(2)]
    xn1_2 = [sb("xn1_%d" % i_, [128, D], BF16) for i_ in range(2)]
    junk1_2 = [sb("junk1_%d" % i_, [128, D], BF16) for i_ in range(2)]
    h2t = [sb("h2t_%d" % i, [128, D], BF16) for i in range(2)]
    ss1_2 = [sb("ss1_%d" % i_, [128, 2], F32) for i_ in range(2)]
    NTC = SEG // 128 + (1 if do_sample else 0)

    def c1a(mt):
        p = mt % 2
        nr = 128 if mt < SEG // 128 else NS
        if mt < SEG // 128:
            fw.dma(sp, xt1[p][:], xw[128 * (MAIN0 + mt):128 * (MAIN0 + mt) + 128, :])
        else:
            fw.dma(sp, xt1[p][0:nr, :], xs_d[:, :])
        b0, b1_ = pb(), pb()
        for half, bk_ in enumerate((b0, b1_)):
            for c in range(8):
                src_ = outaT[:, c, 128 * mt:128 * mt + nr] if c < 3 else outbT[:, c - 3, 128 * mt:128 * mt + nr]
                fw.mm(bk_[0:nr, 0:512], src_, w_out_b[:, c, 512 * half:512 * half + 512], start=(c == 0), stop=(c == 7))
        fw.tt(dve, x1t[p][0:nr, 0:512], b0[0:nr, 0:512], xt1[p][0:nr, 0:512], ALU.add)
        fw.tt(dve, x1t[p][0:nr, 512:1024], b1_[0:nr, 0:512], xt1[p][0:nr, 512:1024], ALU.add)
        fw.dma(sp, x1_scr[128 * mt:128 * mt + nr, :], x1t[p][0:nr, :])

    def c1b(mt):
        p = mt % 2
        nr = 128 if mt < SEG // 128 else NS
        xn1, junk1, ss1 = xn1_2[p], junk1_2[p], ss1_2[p]
        fw.stt(junk1[0:nr, :], x1t[p][0:nr, :], 1.0, x1t[p][0:nr, :], ALU.mult, ALU.mult, accum_out=ss1[0:nr, 0:1])
        fw.ts(dve, ss1[0:nr, 1:2], ss1[0:nr, 0:1], 1.0 / D, EPS, ALU.mult, ALU.add)
        fw.activation(ss1[0:nr, 1:2], ss1[0:nr, 1:2], AF.Ln)
        fw.activation(ss1[0:nr, 1:2], ss1[0:nr, 1:2], AF.Exp, scale=-0.5)
        fw.ts(dve, xn1[0:nr, :], x1t[p][0:nr, :], ss1[0:nr, 1:2], None, ALU.mult)
        bT_ = pb()
        pT_ = bT_.bitcast(BF16)
        for c in range(8):
            fw.transpose(pT_[:, 128 * c:128 * c + nr], xn1[0:nr, 128 * c:128 * c + 128], ident_b[0:nr, 0:nr], inc=(c == 7))
        fw.copy(act, h2t[p][:], pT_[:, 0:1024])
        fw.dma(sp, h2_scr[mt], h2t[p][:])


    for mt in range(NTC):
        c1a(mt)
        if mt > 0:
            c1b(mt - 1)
    c1b(NTC - 1)

    pop()
    pop()
    push()
    w1b = sb("w1b", [128, 8, DFF], BF16)
    w2b = sb("w2b", [128, 32, D], BF16)
    g2 = sb("g2", [128, 8], F32)
    fing = sb("fing", [128, D], F32)
    for c in range(8):
        fw.dma(sp, g2[:, c:c + 1], norm2_g_d[128 * c:128 * c + 128, :])
    fw.dma(sp, fing[:], V(final_g_d.h.ap().to_broadcast([128, D]), final_g_d.buf))
    wst2 = [sb("wst2_%d" % i, [128, 2048], F32) for i in range(2)]
    n_ = 0
    for c in range(8):
        for hf in range(2):
            st = wst2[n_ % 2]
            fw.dma(sp, st[:], w_ff1_d[128 * c:128 * c + 128, 2048 * hf:2048 * hf + 2048])
            fw.scale_copy(act if n_ % 2 else dve, w1b[:, c, 2048 * hf:2048 * hf + 2048], st[:], g2[:, c:c + 1])
            n_ += 1
    for c in range(16):
        st = wst2[n_ % 2]
        fw.dma(sp, st.re("p (a b) -> p a b", a=2), V(w_ff2_d.h.ap()[256 * c:256 * c + 256, :].rearrange("(a p) d -> p a d", a=2), w_ff2_d.buf))
        fw.copy(act if n_ % 2 else dve, w2b[:, 2 * c:2 * c + 2, :], st.re("p (a b) -> p a b", a=2))
        n_ += 1
    GT_ = 256
    h2g = [sb("h2g_%d" % i, [128, 8, GT_], BF16) for i in range(2)]
    aT = sb("aT", [128, 32, GT_], BF16)
    fw.memset(dve, h2g[0][:], 0.0)
    fw.memset(dve, h2g[1][:], 0.0)
    rl = [sb("rl_%d" % i, [128, 2, GT_], BF16) for i in range(2)]
    x1l = [sb("x1l_%d" % i, [128, D], F32) for i in range(2)]
    x2t = [sb("x2t_%d" % i, [128, D], F32) for i in range(2)]
    yt = [sb("yt_%d" % i, [128, D], F32) for i in range(2)]
    junk2 = sb("junk2", [128, D], BF16)
    ss2 = sb("ss2", [128, 2], F32)
    groups = [[g_ * (GT_ // 128) + t_ for t_ in range(GT_ // 128)] for g_ in range(SEG // GT_)]
    if do_sample:
        groups.append([SEG // 128])
    for g_, tiles_ in enumerate(groups):
        p = g_ % 2
        for t_, mt in enumerate(tiles_):
            fw.dma(sp, h2g[p][:, :, 128 * t_:128 * t_ + 128], V(h2_scr.h.ap()[mt].rearrange("p (c t) -> p c t", c=8), h2_scr.buf))
        for j2 in range(16):
            bk_ = pb()
            for jj in range(2):
                j = 2 * j2 + jj
                for c in range(8):
                    fw.mm(bk_[:, GT_ * jj:GT_ * jj + GT_], w1b[:, c, 128 * j:128 * j + 128], h2g[p][:, c, :], start=(c == 0), stop=(c == 7))
            r_ = rl[j2 % 2]
            fw.activation(r_.re("p a t -> p (a t)"), bk_[:, 0:2 * GT_], AF.Relu)
            fw.tt(dve, aT[:, 2 * j2:2 * j2 + 2, :], r_[:], r_[:], ALU.mult)
        for t_, mt in enumerate(tiles_):
            q = mt % 2
            nr = 128 if mt < SEG // 128 else NS
            fw.dma(sp, x1l[q][0:nr, :], x1_scr[128 * mt:128 * mt + nr, :])
            b0, b1_ = pb(), pb()
            for half, bk_ in enumerate((b0, b1_)):
                for c in range(32):
                    fw.mm(bk_[0:nr, 0:512], aT[:, c, 128 * t_:128 * t_ + nr], w2b[:, c, 512 * half:512 * half + 512], start=(c == 0), stop=(c == 31))
            fw.tt(dve, x2t[q][0:nr, 0:512], b0[0:nr, 0:512], x1l[q][0:nr, 0:512], ALU.add)
            fw.tt(dve, x2t[q][0:nr, 512:1024], b1_[0:nr, 0:512], x1l[q][0:nr, 512:1024], ALU.add)
            fw.stt(junk2[0:nr, :], x2t[q][0:nr, :], 1.0, x2t[q][0:nr, :], ALU.mult, ALU.mult, accum_out=ss2[0:nr, 0:1])
            fw.ts(dve, ss2[0:nr, 1:2], ss2[0:nr, 0:1], 1.0 / D, EPS, ALU.mult, ALU.add)
            fw.activation(ss2[0:nr, 1:2], ss2[0:nr, 1:2], AF.Ln)
            fw.activation(ss2[0:nr, 1:2], ss2[0:nr, 1:2], AF.Exp, scale=-0.5)
            fw.stt(yt[q][0:nr, :], x2t[q][0:nr, :], ss2[0:nr, 1:2], fing[0:nr, :], ALU.mult, ALU.mult)
            if mt < SEG // 128:
                fw.dma(sp, o_y[128 * mt:128 * mt + 128, :], yt[q][:])
            else:
                fw.dma(sp, o_ys[:, :], yt[q][0:nr, :])
    pop()
    pop()
    fw.finish(sp)
    fw.emit()
    return nc


def _get_nc():
    if "nc" not in _NC_CACHE:
        _NC_CACHE["nc"] = build_nc()
    return _NC_CACHE["nc"]


def _t5_bucket_np(dist):
    dist = np.asarray(dist, np.int64)
    df = np.maximum(dist, 1).astype(np.float32)
    large = 16 + (np.log(df / np.float32(16)) / np.float32(np.log(2048 / 16)) * np.float32(16)).astype(np.int32)
    large = np.minimum(large, 31)
    return np.where(dist < 16, dist, large)


def _consts():
    c = {}
    k = np.arange(128)[:, None]
    q = np.arange(128)[None, :]
    c["negmask"] = np.where(k > q, np.float32(MASKV), np.float32(0.0)).astype(np.float32)
    sel = np.zeros((4, 4, 128), np.float32)
    for h in range(4):
        sel[h, h, :] = 1.0
    c["sel"] = sel.reshape(4, 512)
    ohb = np.zeros((3, 32, 512), np.float32)
    mv = np.full((3, 1, 512), np.float32(MASKV), np.float32)
    for bi, dil in enumerate((1, 4, 16)):
        for j in range(0, 129):
            ohb[bi, int(_t5_bucket_np(j * dil)), j + 127] = 1.0
            mv[bi, 0, j + 127] = 0.0
    c["ohb"] = ohb
    c["mvec"] = mv
    ohs = np.zeros((3, 32, 128), np.float32)
    for bi, dil in enumerate((1, 4, 16)):
        for i in range(128):
            ohs[bi, int(_t5_bucket_np((128 - i) * dil)), i] = 1.0
    c["ohs"] = ohs
    e16 = np.zeros((NS, NS, 128), np.float32)
    ecol = np.zeros((128, NS, NS), np.float32)
    for b in range(NS):
        e16[b, b, :] = 1.0
        ecol[:, b, b] = 1.0
    c["e16"] = e16.reshape(NS, NS * 128)
    c["ecol"] = ecol.reshape(128, NS * NS)
    return c


def make_in_maps(x_prompt, x_sample, cache_win_k, cache_win_v, state_conv, state_C, state_n, state_m,
                 rel_bias, norm1_g, w_in, gate_bias, conv_w, conv_b, wq_head, wk_head,
                 attn_out_g, mh_norm_g, skip, w_out, norm2_g, w_ff1, w_ff2, final_g):
    f = lambda a: np.ascontiguousarray(np.asarray(a, dtype=np.float32))
    x_prompt = f(x_prompt)
    cst = _consts()
    shared = {
        "w_in": f(w_in[0]), "norm1_g": f(norm1_g[0]).reshape(D, 1), "gate_bias": f(gate_bias[0]).reshape(8, 1),
        "conv_wT": f(np.asarray(conv_w[0]).T), "conv_w": f(conv_w[0]), "conv_b": f(conv_b[0]).reshape(DB, 1),
        "wq": f(wq_head[0]), "wk": f(wk_head[0]), "mhg": f(mh_norm_g[0]).reshape(DB, 1),
        "skip": f(skip[0]).reshape(DB, 1),
        "rel_bias": f(rel_bias), "attn_g": f(attn_out_g[0]).reshape(DA, 1), "w_out": f(w_out[0]),
        "norm2_g": f(norm2_g[0]).reshape(D, 1), "w_ff1": f(w_ff1[0]), "w_ff2": f(w_ff2[0]),
        "final_g": f(final_g).reshape(1, D),
    }
    shared.update(cst)
    in_maps = []
    for c in range(NCORES):
        b, s = c // 4, c % 4
        lo = SEG * s - (WIN - SEG)
        xwin = np.zeros((WIN, D), np.float32)
        a0 = max(lo, 0)
        xwin[a0 - lo:] = x_prompt[b, a0:lo + WIN]
        valid = np.array([(lo + 128 * t) >= 0 for t in range(NT)])
        m = dict(shared)
        m["xw"] = xwin
        m["tmA"] = np.tile(np.where(valid, 0.0, NEG).astype(np.float32)[None, :], (4, 1))
        m["tmV"] = np.tile(np.where(valid, -1.0, 0.0).astype(np.float32)[None, :], (4, 1))
        m["tmO"] = np.tile(np.where(valid, 1.0, 0.0).astype(np.float32)[None, :], (128, 1))
        sl = slice(NS * c, NS * c + NS)
        m["xs"] = f(x_sample[sl, 0])
        m["cwk"] = f(cache_win_k[0, sl]).reshape(NS, 2048, DA)
        m["cwv"] = f(cache_win_v[0, sl]).reshape(NS, 2048, DA)
        m["s_conv"] = f(state_conv[0, sl])
        m["s_C"] = f(state_C[0, sl])
        m["s_n"] = f(state_n[0, sl])
        m["s_m"] = f(state_m[0, sl])
        in_maps.append(m)
    return in_maps


def _filter(nc_inputs, m):
    return {k: v for k, v in m.items() if k in nc_inputs}


def kernel(**inputs):
    nc = _get_nc()
    in_maps = make_in_maps(**inputs)
    names = _NC_CACHE["in_names"]
    in_maps = [_filter(names, m) for m in in_maps]
    res = run_bass_kernel_spmd(nc, in_maps, core_ids=list(range(NCORES)))
    R = res.results
    return assemble(R)


def assemble(R):
    f = np.float32
    y_prompt = np.stack([np.concatenate([R[4 * b + s]["o_y"] for s in range(4)], 0) for b in range(2)]).astype(f)
    y_sample = np.concatenate([R[c]["o_ys"] for c in range(NCORES)], 0).reshape(128, 1, D).astype(f)
    last = [3, 7]
    p_k = np.stack([R[c]["o_wk"] for c in last]).reshape(1, 2, 2048, 6, 64).astype(f)
    p_v = np.stack([R[c]["o_wv"] for c in last]).reshape(1, 2, 2048, 6, 64).astype(f)
    p_conv = np.stack([R[c]["o_conv"] for c in last]).reshape(1, 2, 3, DB).astype(f)
    p_C = np.stack([R[c]["o_C"] for c in last]).reshape(1, 2, NHB, EB, EB).astype(f)
    p_n = np.stack([R[c]["o_n"] for c in last]).reshape(1, 2, NHB, EB).astype(f)
    p_m = np.stack([R[c]["o_m"] for c in last]).reshape(1, 2, NHB).astype(f)
    cat = lambda k: np.concatenate([R[c][k] for c in range(NCORES)], 0)
    s_k = cat("o_swk").reshape(1, 128, 2048, 6, 64).astype(f)
    s_v = cat("o_swv").reshape(1, 128, 2048, 6, 64).astype(f)
    s_conv = cat("o_sconv").reshape(1, 128, 3, DB).astype(f)
    s_C = cat("o_sC").reshape(1, 128, NHB, EB, EB).astype(f)
    s_n = cat("o_sn").reshape(1, 128, NHB, EB).astype(f)
    s_m = cat("o_sm").reshape(1, 128, NHB).astype(f)
    return (y_prompt, y_sample, p_k, p_v, p_conv, p_C, p_n, p_m, s_k, s_v, s_conv, s_C, s_n, s_m)
```
